# Optimizing a Trainium2 kernel written in Bass

```python
import math
import jax, jax.numpy as jnp
from jax import lax
import numpy as np

D_MODEL = 1024
BATCH = 8
SEQ = 8192
DEPTH = 2

CHUNK = 64
HEAD_DIM = 64
D_A = 384
HEADS_A = D_A // HEAD_DIM
D_B = 256
HEADS_B = D_B // HEAD_DIM
D_C = 384
HEADS_C = D_C // HEAD_DIM
D_MIX = D_A + D_B + D_C
IN_COLS = 2 * D_A + 3 * D_B + 2 * D_C
CONV_A_WIDTH = 31
CONV_B_WIDTH = 3
SG_BLOCK = 128
PEER_HEADS = 8
D_KEY = 256
N_KEYS = 128
N_EXPERTS = N_KEYS * N_KEYS
TOPK = 16
PEER_BLOCK = 128
PLE_DIM = 256
EPS = 1e-6

kernel_name = "hybrid_conv_sgmlp_peer_trunk"


def rmsnorm(x, g):
    xf = x.astype(jnp.float32)
    y = xf * lax.rsqrt(jnp.mean(xf * xf, axis=-1, keepdims=True) + EPS)
    return (y * g.astype(jnp.float32)).astype(x.dtype)


def group_layernorm(x, g, b, heads):
    shp = x.shape
    xf = x.astype(jnp.float32).reshape(shp[:-1] + (heads, shp[-1] // heads))
    m = jnp.mean(xf, axis=-1, keepdims=True)
    var = jnp.mean(jnp.square(xf - m), axis=-1, keepdims=True)
    y = ((xf - m) * lax.rsqrt(var + EPS)).reshape(shp)
    return (y * g.astype(jnp.float32) + b.astype(jnp.float32)).astype(x.dtype)


def causal_dwconv(x, w):
    k = w.shape[0]
    return lax.conv_general_dilated(
        x, w.astype(x.dtype)[:, None, :], window_strides=(1,), padding=[(k - 1, 0)],
        dimension_numbers=("NWC", "WIO", "NWC"), feature_group_count=x.shape[-1])


def conformer_conv(za, conv_w, conv_b, ln_g, ln_b):
    val, gate = jnp.split(za, 2, axis=-1)
    y = val * jax.nn.sigmoid(gate)
    y = causal_dwconv(y, conv_w) + conv_b.astype(y.dtype)
    y = group_layernorm(y, ln_g, ln_b, HEADS_A)
    return jax.nn.silu(y)


def short_gated_conv(zb, conv_w):
    bg, cg, xb = jnp.split(zb, 3, axis=-1)
    return bg * causal_dwconv(cg * xb, conv_w)


def spatial_gating(zc, ln_g, ln_b, w_s, b_s):
    u, v = jnp.split(zc, 2, axis=-1)
    bsz, s, _ = v.shape
    v = group_layernorm(v, ln_g, ln_b, HEADS_C)
    vb = v.reshape(bsz, s // SG_BLOCK, SG_BLOCK, HEADS_C, HEAD_DIM)
    pos = jnp.arange(SG_BLOCK)
    mask = (pos[None, :] // CHUNK) <= (pos[:, None] // CHUNK)
    ws = jnp.where(mask[None], w_s, 0).astype(v.dtype)
    mixed = jnp.einsum("hij,bnjhc->bnihc", ws, vb) + b_s.T.astype(v.dtype)[None, None, :, :, None]
    return u * mixed.reshape(bsz, s, D_C)


def peer(n, w_q, sub_keys, expert_u, expert_v):
    bsz, s, d = n.shape
    t = bsz * s
    nf = n.reshape(t, d)
    q = (nf @ w_q).reshape(t, PEER_HEADS, D_KEY)
    q1, q2 = q[..., : D_KEY // 2], q[..., D_KEY // 2:]
    s1 = jnp.einsum("thd,hkd->thk", q1, sub_keys[:, 0])
    s2 = jnp.einsum("thd,hkd->thk", q2, sub_keys[:, 1])
    v1, i1 = lax.top_k(s1, TOPK)
    v2, i2 = lax.top_k(s2, TOPK)
    cand = (v1[..., :, None] + v2[..., None, :]).reshape(t, PEER_HEADS, TOPK * TOPK)
    vals, ci = lax.top_k(cand, TOPK)
    e1 = jnp.take_along_axis(i1, ci // TOPK, axis=-1)
    e2 = jnp.take_along_axis(i2, ci % TOPK, axis=-1)
    idx = (e1 * N_KEYS + e2).reshape(t, PEER_HEADS * TOPK)
    gates = jax.nn.softmax(vals.astype(jnp.float32), axis=-1).astype(n.dtype)
    gates = gates.reshape(t, PEER_HEADS * TOPK)
    nblk = t // PEER_BLOCK

    def block(args):
        xb, ib, gb = args
        ug = expert_u[ib]
        act = jax.nn.gelu(jnp.einsum("td,tkd->tk", xb, ug)) * gb
        vg = expert_v[ib]
        return jnp.einsum("tk,tkd->td", act, vg)

    out = lax.map(block, (nf.reshape(nblk, PEER_BLOCK, d),
                          idx.reshape(nblk, PEER_BLOCK, PEER_HEADS * TOPK),
                          gates.reshape(nblk, PEER_BLOCK, PEER_HEADS * TOPK)))
    return out.reshape(bsz, s, d)


def setup_inputs(seed: int = 0) -> dict:
    key = jax.random.key(seed)
    ks = jax.random.split(key, 24)
    L, D = DEPTH, D_MODEL
    nrm = lambda k, shp, sc: jax.random.normal(k, shp, jnp.float32) * sc
    gain = lambda k, shp: 1.0 + 0.05 * jax.random.normal(k, shp, jnp.float32)
    return {
        "x": nrm(ks[0], (BATCH, SEQ, D), 1.0),
        "p": nrm(ks[1], (DEPTH, BATCH, SEQ, PLE_DIM), 1.0),
        "g_mix": gain(ks[2], (L, D)),
        "w_in": nrm(ks[3], (L, D, IN_COLS), D ** -0.5),
        "conv_a_w": nrm(ks[4], (L, CONV_A_WIDTH, D_A), CONV_A_WIDTH ** -0.5),
        "conv_a_b": nrm(ks[5], (L, D_A), 0.02),
        "ln_a_g": gain(ks[6], (L, D_A)),
        "ln_a_b": nrm(ks[7], (L, D_A), 0.02),
        "conv_b_w": nrm(ks[8], (L, CONV_B_WIDTH, D_B), CONV_B_WIDTH ** -0.5),
        "ln_c_g": gain(ks[9], (L, D_C)),
        "ln_c_b": nrm(ks[10], (L, D_C), 0.02),
        "w_s": nrm(ks[11], (L, HEADS_C, SG_BLOCK, SG_BLOCK), 0.5 * SG_BLOCK ** -0.5),
        "b_s": gain(ks[12], (L, HEADS_C, SG_BLOCK)),
        "w_out": nrm(ks[13], (L, D_MIX, D), D_MIX ** -0.5),
        "g_ffn": gain(ks[14], (L, D)),
        "w_q": nrm(ks[15], (L, D, PEER_HEADS * D_KEY), D ** -0.5),
        "sub_keys": nrm(ks[16], (L, PEER_HEADS, 2, N_KEYS, D_KEY // 2), (D_KEY // 2) ** -0.5),
        "expert_u": nrm(ks[17], (L, N_EXPERTS, D), D ** -0.5),
        "expert_v": nrm(ks[18], (L, N_EXPERTS, D), 0.1),
        "g_ple": gain(ks[19], (L, D)),
        "w_pe": nrm(ks[20], (L, PLE_DIM, D), PLE_DIM ** -0.5),
        "w_pg": nrm(ks[21], (L, D, D), D ** -0.5),
        "g_final": gain(ks[22], (D,)),
    }


def reference(x, p, g_mix, w_in, conv_a_w, conv_a_b, ln_a_g, ln_a_b, conv_b_w, ln_c_g, ln_c_b,
              w_s, b_s, w_out, g_ffn, w_q, sub_keys, expert_u, expert_v, g_ple, w_pe, w_pg,
              g_final):
    h = x
    for i in range(DEPTH):
        a = rmsnorm(h, g_mix[i])
        z = a @ w_in[i]
        za = z[..., : 2 * D_A]
        zb = z[..., 2 * D_A: 2 * D_A + 3 * D_B]
        zc = z[..., 2 * D_A + 3 * D_B:]
        ya = conformer_conv(za, conv_a_w[i], conv_a_b[i], ln_a_g[i], ln_a_b[i])
        yb = short_gated_conv(zb, conv_b_w[i])
        yc = spatial_gating(zc, ln_c_g[i], ln_c_b[i], w_s[i], b_s[i])
        h = h + jnp.concatenate([ya, yb, yc], axis=-1) @ w_out[i]
        h = h + peer(rmsnorm(h, g_ffn[i]), w_q[i], sub_keys[i], expert_u[i], expert_v[i])
        gate = jax.nn.sigmoid(rmsnorm(h, g_ple[i]) @ w_pg[i])
        h = h + (p[i] @ w_pe[i]) * gate
    return rmsnorm(h, g_final)
```

```python
import contextlib
import numpy as np
import concourse.bass as bass
import concourse.mybir as mybir
from concourse.bass_utils import run_bass_kernel_spmd

F32 = mybir.dt.float32
BF16 = mybir.dt.bfloat16
AF = mybir.ActivationFunctionType
ALU = mybir.AluOpType
AX = mybir.AxisListType

D = 1024
NCORES = 8
SEQ = 8192
DEPTH = 2
EPS = 1e-6
NEG = -1.0e30
ENGS = ("pe", "act", "dve", "pool", "sp")


class Buf:
    __slots__ = ("name", "last_w", "readers", "dma_readers")

    def __init__(self, name):
        self.name = name
        self.last_w = None
        self.readers = {}
        self.dma_readers = []


class Op:
    __slots__ = ("eng", "fn", "deps", "signal", "dma", "semkey", "semval", "idx")

    def __init__(self, eng, fn, dma):
        self.eng = eng
        self.fn = fn
        self.deps = []
        self.signal = False
        self.dma = dma
        self.semkey = None
        self.semval = None


class Rec:
    NDMA = 24

    def __init__(self, nc):
        self.nc = nc
        self.ops = {e: [] for e in ENGS}
        self.last_real = {e: None for e in ENGS}
        self.ndma = 0
        self.dma_ops = []

    def op(self, eng, fn, reads=(), writes=(), dma=False):
        o = Op(eng, fn, dma)
        o.idx = len(self.ops[eng])
        deps = []
        for b in reads:
            if b.last_w is not None:
                deps.append(b.last_w)
        for b in writes:
            if b.last_w is not None:
                deps.append(b.last_w)
            deps.extend(b.readers.values())
            deps.extend(b.dma_readers)
        latest = {}
        dmas = []
        seen = set()
        for d in deps:
            if d is o or id(d) in seen:
                continue
            seen.add(id(d))
            if d.dma:
                dmas.append(d)
            else:
                if d.eng == "pe" and eng == "pe" and not dma:
                    continue
                if d.eng not in latest or latest[d.eng].idx < d.idx:
                    latest[d.eng] = d
        for d in list(latest.values()) + dmas:
            o.deps.append(d)
            d.signal = True
        if dma:
            o.signal = True
            k = self.ndma
            self.ndma += 1
            o.semkey = ("dma", k % self.NDMA)
            o.semval = 16 * (k // self.NDMA + 1)
            if k >= self.NDMA:
                o.deps.append(self.dma_ops[k - self.NDMA])
            self.dma_ops.append(o)
        for b in reads:
            if dma:
                b.dma_readers.append(o)
            else:
                b.readers[eng] = o
        for b in writes:
            b.last_w = o
            b.readers = {}
            b.dma_readers = []
        self.ops[eng].append(o)
        self.last_real[eng] = o
        return o

    def barrier(self):
        lasts = [o for o in self.last_real.values() if o is not None]
        lasts += self.dma_ops[-self.NDMA:]
        for d in lasts:
            d.signal = True
        for e in ENGS:
            o = Op(e, None, False)
            seen = set()
            for d in lasts:
                if id(d) in seen:
                    continue
                seen.add(id(d))
                o.deps.append(d)
            self.ops[e].append(o)

    EPOCH = 6000

    def emit(self):
        nc = self.nc
        nep = {}
        for e in ENGS:
            cnt = 0
            for o in self.ops[e]:
                if o.dma or o.fn is None:
                    continue
                if o.signal:
                    o.semkey = ("eng", e, cnt // self.EPOCH)
                    o.semval = cnt % self.EPOCH + 1
                    cnt += 1
            nep[e] = cnt // self.EPOCH + 1
        with contextlib.ExitStack() as st:
            sems = {}
            for e in ENGS:
                for ep in range(nep[e]):
                    sems[("eng", e, ep)] = st.enter_context(nc.semaphore("s_%s%d" % (e, ep)))
            for i in range(self.NDMA):
                sems[("dma", i)] = st.enter_context(nc.semaphore("s_dma%d" % i))
            block = st.enter_context(nc.Block())

            def run(e, eng):
                waited = {}
                for o in self.ops[e]:
                    for d in o.deps:
                        if waited.get(d.semkey, 0) >= d.semval:
                            continue
                        waited[d.semkey] = d.semval
                        eng.wait_ge(sems[d.semkey], d.semval)
                    if o.fn is None:
                        continue
                    ins = o.fn(eng)
                    if o.signal:
                        ins.then_inc(sems[o.semkey], 16 if o.dma else 1)

            block.tensor(lambda eng: run("pe", eng))
            block.scalar(lambda eng: run("act", eng))
            block.vector(lambda eng: run("dve", eng))
            block.gpsimd(lambda eng: run("pool", eng))
            block.sync(lambda eng: run("sp", eng))


def build(S, L, dbg=False):
    assert S % 512 == 0
    NBLK = S // 512
    nc = bass.Bass("TRN2", target_bir_lowering=False)

    def din(name, shape, dt=F32):
        return nc.dram_tensor(name, list(shape), dt, kind="ExternalInput").ap()

    def dscr(name, shape, dt=F32, out=False):
        return nc.dram_tensor(name, list(shape), dt, kind=("ExternalOutput" if out else "Internal")).ap()

    x_d = din("x", [S, D])
    p_d = din("p", [L, S, 256])
    gmix_d = din("g_mix", [L, 128, 8]); gffn_d = din("g_ffn", [L, 128, 8]); gple_d = din("g_ple", [L, 128, 8])
    gfin_d = din("g_final", [128, 8])
    win_d = din("w_in", [L, D, 2304]); wout_d = din("w_out", [L, D, D]); wq_d = din("w_q", [L, D, 2048])
    wpg_d = din("w_pg", [L, D, D]); wpe_d = din("w_pe", [L, 256, D])
    caw_d = din("conv_a_w", [L, 128, 3, 31]); cab_d = din("conv_a_b", [L, 128, 3])
    lag_d = din("ln_a_g", [L, 128, 3]); lab_d = din("ln_a_b", [L, 128, 3])
    cbw_d = din("conv_b_w", [L, 128, 2, 3])
    lcg_d = din("ln_c_g", [L, 128, 3]); lcb_d = din("ln_c_b", [L, 128, 3])
    wst_d = din("w_sT", [L, 128, 3, 256]); bsb_d = din("b_sb", [L, 128, 3, 128])
    keys_d = din("keysT", [L, 128, 16, 128])
    eu_d = din("expert_uT", [L, D, 16384]); ev_d = din("expert_v", [L, 16384, D])
    out_d = nc.dram_tensor("out", [S, D], F32, kind="ExternalOutput").ap()

    hcur_d = dscr("hcur", [8, 128, S], out=dbg)
    h1_d = dscr("h1", [8, 128, S], out=dbg)
    h2_d = dscr("h2", [8, 128, S], out=dbg)
    pT_d = dscr("pT", [L, 2, 128, S])
    nT_d = dscr("nT", [8, 128, S], BF16)
    sS_d = dscr("sS", [S, 2048], out=dbg)
    sTK_d = dscr("sTK", [S, 16], out=dbg)
    ubf_d = dscr("ubf", [L, 8, 128, 16384], BF16)
    vbf_d = dscr("vbf", [L, 128, 128, D], BF16)

    st = contextlib.ExitStack()
    with st:
        CAP = 204 * 1024
        arena_t = st.enter_context(nc.sbuf_tensor("arena", [128, CAP // 4], F32))
        psb = [st.enter_context(nc.psum_tensor("pb%d" % i, [128, 512], F32)) for i in range(8)]
        PB = [Buf("pb%d" % i) for i in range(8)]
        R = Rec(nc)
        off = [0]

        def alloc(shape, dt=F32):
            n = int(np.prod(shape)) * (2 if dt == BF16 else 4)
            n = (n + 31) // 32 * 32
            a = arena_t[:, off[0] // 4:(off[0] + n) // 4]
            off[0] += n
            assert off[0] <= CAP, ("SBUF arena overflow", off[0])
            if dt == BF16:
                a = a.bitcast(BF16)
            a = a[:, 0:int(np.prod(shape))]
            if len(shape) == 2:
                a = a.rearrange("p (a b) -> p a b", a=shape[0])
            elif len(shape) == 3:
                a = a.rearrange("p (a b c) -> p a b c", a=shape[0], b=shape[1])
            return a

        def mm(out, lhsT, rhs, start, stop, rd, wr):
            return R.op("pe", lambda e: e.matmul(out, lhsT=lhsT, rhs=rhs, start=start, stop=stop), rd, wr)

        def tr(out, in_, ident, rd, wr):
            return R.op("pe", lambda e: e.transpose(out=out, in_=in_, identity=ident), rd, wr)

        def act(out, in_, func, rd, wr, scale=None, bias=None):
            kw = {}
            if scale is not None:
                kw["scale"] = scale
            if bias is not None:
                kw["bias"] = bias
            return R.op("act", lambda e: e.activation(out=out, in_=in_, func=func, **kw), rd, wr)

        def tt(eng, out, in0, in1, op, rd, wr):
            return R.op(eng, lambda e: e.tensor_tensor(out=out, in0=in0, in1=in1, op=op), rd, wr)

        def ts(eng, out, in0, s1, s2, op0, op1, rd, wr):
            if s2 is None:
                return R.op(eng, lambda e: e.tensor_scalar(out=out, in0=in0, scalar1=s1, scalar2=None, op0=op0), rd, wr)
            return R.op(eng, lambda e: e.tensor_scalar(out=out, in0=in0, scalar1=s1, scalar2=s2, op0=op0, op1=op1), rd, wr)

        def stt(eng, out, in0, scalar, in1, op0, op1, rd, wr):
            return R.op(eng, lambda e: e.scalar_tensor_tensor(out=out, in0=in0, scalar=scalar, in1=in1, op0=op0, op1=op1), rd, wr)

        def cp(eng, out, in_, rd, wr):
            if eng == "act":
                return R.op("act", lambda e: e.copy(out=out, in_=in_), rd, wr)
            return R.op(eng, lambda e: e.tensor_copy(out=out, in_=in_), rd, wr)

        def mset(eng, ap, val, wr):
            return R.op(eng, lambda e: e.memset(ap, val), (), wr)

        def dma(out, in_, rd, wr):
            return R.op("sp", lambda e: e.dma_start(out=out, in_=in_), rd, wr, dma=True)

        def recip(out, in_, rd, wr):
            return R.op("dve", lambda e: e.reciprocal(out=out, in_=in_), rd, wr)

        def vmax(out, in_, rd, wr):
            return R.op("dve", lambda e: e.max(out=out, in_=in_), rd, wr)

        def mrep(out, rep, vals, rd, wr):
            return R.op("dve", lambda e: e.match_replace(out=out, in_to_replace=rep, in_values=vals, imm_value=NEG), rd, wr)

        def pbf(i):
            return psb[i][:].bitcast(BF16)

        ident = alloc([128]); identb = alloc([128], BF16); onesb = alloc([128], BF16); blk = alloc([128])
        epst = alloc([1])
        Bc = Buf("consts")
        mset("pool", ident, 0.0, [Bc])
        R.op("pool", lambda e: e.affine_select(out=ident, in_=ident, pattern=[[-1, 128]], compare_op=ALU.not_equal,
                                               fill=1.0, base=0, channel_multiplier=1), [Bc], [Bc])
        cp("dve", identb, ident, [Bc], [Bc])
        mset("pool", onesb, 1.0 / 1024.0, [Bc])
        mset("pool", blk, 0.0, [Bc])
        mset("pool", blk[0:64, 0:64], 1.0 / 64.0, [Bc])
        mset("pool", blk[64:128, 64:128], 1.0 / 64.0, [Bc])
        mset("pool", epst, EPS, [Bc])
        base_off = off[0]

        xs = alloc([1024]); xst = alloc([8, 128]); cst = alloc([2048]); cstb = alloc([2048], BF16)
        pst = alloc([2, 128])
        Bxs, Bxst, Bcst, Bcstb, Bpst = Buf("xs"), Buf("xst"), Buf("cst"), Buf("cstb"), Buf("pst")
        hcur_v = hcur_d.rearrange("c p s -> p c s")
        h1_v = h1_d.rearrange("c p s -> p c s")
        h2_v = h2_d.rearrange("c p s -> p c s")
        nT_v = nT_d.rearrange("c p s -> p c s")
        for tl in range(S // 128):
            ts_ = slice(tl * 128, (tl + 1) * 128)
            dma(xs, x_d[ts_, :], [], [Bxs])
            for c in range(8):
                bk = c // 4
                tr(psb[bk][:, (c % 4) * 128:(c % 4 + 1) * 128], xs[:, c * 128:(c + 1) * 128], ident, [Bxs, Bc], [PB[bk]])
            cp("act", xst[:, 0:4, :], psb[0][:].rearrange("p (a b) -> p a b", a=4), [PB[0]], [Bxst])
            cp("dve", xst[:, 4:8, :], psb[1][:].rearrange("p (a b) -> p a b", a=4), [PB[1]], [Bxst])
            dma(hcur_v[:, :, ts_], xst, [Bxst], [])
            for l in range(L):
                dma(xs[:, 0:256], p_d[l, ts_, :], [], [Bxs])
                for c in range(2):
                    tr(psb[2][:, c * 128:(c + 1) * 128], xs[:, c * 128:(c + 1) * 128], ident, [Bxs, Bc], [PB[2]])
                cp("act", pst, psb[2][:, 0:256].rearrange("p (a b) -> p a b", a=2), [PB[2]], [Bpst])
                dma(pT_d[l].rearrange("c p s -> p c s")[:, :, ts_], pst, [Bpst], [])
        it = 0
        for l in range(L):
            for c in range(8):
                for cb in range(8):
                    dma(cst, eu_d[l, c * 128:(c + 1) * 128, cb * 2048:(cb + 1) * 2048], [], [Bcst])
                    cp("dve" if it % 2 == 0 else "act", cstb, cst, [Bcst], [Bcstb])
                    dma(ubf_d[l, c, :, cb * 2048:(cb + 1) * 2048], cstb, [Bcstb], [])
                    it += 1
            for et in range(0, 128, 2):
                dma(cst.rearrange("p (a b) -> p a b", a=2),
                    ev_d[l, et * 128:(et + 2) * 128, :].rearrange("(a p) d -> p a d", a=2), [], [Bcst])
                cp("dve" if it % 2 == 0 else "act", cstb, cst, [Bcst], [Bcstb])
                dma(vbf_d[l, et:et + 2].rearrange("a p d -> p a d"), cstb.rearrange("p (a b) -> p a b", a=2), [Bcstb], [])
                it += 1
        R.barrier()

        def rmsnorm_to(hT, g, outT, BhT, BoutT, sqb, Bsq, rstd, Brstd, t0buf, Bt0, bank, out_f32=False):
            for c in range(8):
                if c % 2 == 0:
                    act(sqb[:, c, :], hT[:, c, :], AF.Square, [BhT], [Bsq])
                else:
                    tt("pool", sqb[:, c, :], hT[:, c, :], hT[:, c, :], ALU.mult, [BhT], [Bsq])
            for c in range(8):
                mm(psb[bank][:], onesb, sqb[:, c, :], c == 0, c == 7, [Bsq, Bc], [PB[bank]])
            act(t0buf, psb[bank][:], AF.Sqrt, [PB[bank], Bc], [Bt0], bias=epst[:, 0:1])
            recip(rstd, t0buf, [Bt0], [Brstd])
            for c in range(8):
                stt("dve", outT[:, c, :], hT[:, c, :], g[:, c:c + 1], rstd, ALU.mult, ALU.mult,
                    [BhT, Brstd, Bc], [BoutT])

        def load_w_bf(dst, src_d, nk, ncol, stage, Bst, Bdst):
            for k in range(nk):
                dma(stage[:, 0:ncol], src_d[k * 128:(k + 1) * 128, :], [], [Bst])
                cp("dve" if k % 2 == 0 else "act", dst[:, k, :], stage[:, 0:ncol], [Bst], [Bdst])

        for l in range(L):
            off[0] = base_off
            Win = alloc([8, 2304], BF16); Wout = alloc([8, 1024], BF16); Wq = alloc([8, 2048], BF16)
            keysb = alloc([16, 128], BF16); wsT = alloc([3, 256], BF16)
            gmix = alloc([8]); gffn = alloc([8]); caw = alloc([3, 31]); cab = alloc([3]); lag = alloc([3]); lab = alloc([3])
            cbw = alloc([2, 3]); lcg = alloc([3]); lcb = alloc([3]); bsb = alloc([3, 128])
            hT = alloc([8, 512]); aT = alloc([8, 512], BF16); ymix = alloc([8, 512], BF16)
            rstd = alloc([512]); T = [alloc([512]) for _ in range(6)]
            ybuf = [alloc([544]) for _ in range(3)]; ubuf = [alloc([516]) for _ in range(2)]
            uC = alloc([3, 512]); vln = alloc([3, 512], BF16); vlnT = alloc([384], BF16)
            qT = alloc([16, 512], BF16)
            s_sb = alloc([2048]); vtop = alloc([16, 16]); tmpk = alloc([256]); cand = alloc([8, 256]); vals = alloc([8, 16])
            evx = alloc([8, 16]); zz = alloc([8]); tk = alloc([16])
            stage = qT.bitcast(F32) if False else None
            BW = Buf("W"); BhT = Buf("hT"); BaT = Buf("aT"); Bym = Buf("ymix"); Brs = Buf("rstd")
            BT = [Buf("T%d" % i) for i in range(6)]
            Byb = [Buf("yb%d" % i) for i in range(3)]; Bub = [Buf("ub%d" % i) for i in range(2)]
            BuC = Buf("uC"); Bvln = Buf("vln"); BvlnT = Buf("vlnT"); BqT = Buf("qT")
            Bs = Buf("s_sb"); Bv = Buf("vtop"); Btk_ = Buf("tmpk"); Bcand = Buf("cand"); Bvals = Buf("vals")
            Bevx = Buf("evx"); Bzz = Buf("zz"); Btkk = Buf("tk")
            stg = s_sb
            Bstg = Bs
            for k in range(8):
                for hf in range(2):
                    ncol = 1152
                    dma(stg[:, 0:ncol], win_d[l, k * 128:(k + 1) * 128, hf * ncol:(hf + 1) * ncol], [], [Bstg])
                    cp("dve" if hf == 0 else "act", Win[:, k, hf * ncol:(hf + 1) * ncol], stg[:, 0:ncol], [Bstg], [BW])
            load_w_bf(Wout, wout_d[l], 8, 1024, stg, Bstg, BW)
            load_w_bf(Wq, wq_d[l], 8, 2048, stg, Bstg, BW)
            dma(stg.rearrange("p (a b) -> p a b", a=16), keys_d[l], [], [Bstg])
            cp("dve", keysb, stg.rearrange("p (a b) -> p a b", a=16), [Bstg], [BW])
            dma(stg[:, 0:768].rearrange("p (a b) -> p a b", a=3), wst_d[l], [], [Bstg])
            cp("dve", wsT, stg[:, 0:768].rearrange("p (a b) -> p a b", a=3), [Bstg], [BW])
            for c in range(3):
                for hh in range(2):
                    mset("pool", wsT[64:128, c, hh * 128:hh * 128 + 64], 0.0, [BW])
            for (dst, src) in [(gmix, gmix_d[l]), (gffn, gffn_d[l]), (caw, caw_d[l]), (cab, cab_d[l]), (lag, lag_d[l]),
                               (lab, lab_d[l]), (cbw, cbw_d[l]), (lcg, lcg_d[l]), (lcb, lcb_d[l]), (bsb, bsb_d[l])]:
                dma(dst, src, [], [Bc])
            for j in range(3):
                mset("pool", ybuf[j][:, 0:32], 0.0, [Byb[j]])
            for j in range(2):
                mset("pool", ubuf[j][:, 0:4], 0.0, [Bub[j]])

            def zchunk(j, bank):
                for k in range(8):
                    mm(psb[bank][:], Win[:, k, j * 128:(j + 1) * 128], aT[:, k, :], k == 0, k == 7, [BW, BaT], [PB[bank]])

            def gln(src, Bsrc, g, b, j, func, dst, Bdst):
                mm(psb[2][:], blk, src, True, True, [Bsrc, Bc], [PB[2]])
                tt("dve", T[2], src, psb[2][:], ALU.subtract, [Bsrc, PB[2]], [BT[2]])
                act(T[3], T[2], AF.Square, [BT[2]], [BT[3]])
                mm(psb[3][:], blk, T[3], True, True, [BT[3], Bc], [PB[3]])
                act(T[4], psb[3][:], AF.Sqrt, [PB[3], Bc], [BT[4]], bias=epst[:, 0:1])
                recip(T[4], T[4], [BT[4]], [BT[4]])
                tt("dve", T[2], T[2], T[4], ALU.mult, [BT[2], BT[4]], [BT[2]])
                act(dst, T[2], func, [BT[2], Bc], [Bdst], scale=g[:, j:j + 1], bias=b[:, j:j + 1])

            for b in range(NBLK):
                tsl = slice(b * 512, (b + 1) * 512)
                dma(hT, hcur_v[:, :, tsl], [], [BhT])
                rmsnorm_to(hT, gmix, aT, BhT, BaT, ymix, Bym, rstd, Brs, T[0], BT[0], 7)
                for j in range(3):
                    zchunk(j, 0)
                    zchunk(3 + j, 1)
                    act(T[1], psb[1][:], AF.Sigmoid, [PB[1]], [BT[1]])
                    tt("dve", ybuf[j][:, 32:544], psb[0][:], T[1], ALU.mult, [PB[0], BT[1]], [Byb[j]])
                    ts("dve", T[5], ybuf[j][:, 2:514], caw[:, j, 0:1], cab[:, j:j + 1], ALU.mult, ALU.add, [Byb[j], Bc], [BT[5]])
                    for k in range(1, 31):
                        stt("dve", T[5], ybuf[j][:, 2 + k:514 + k], caw[:, j, k:k + 1], T[5], ALU.mult, ALU.add, [Byb[j], Bc, BT[5]], [BT[5]])
                    cp("pool", ybuf[j][:, 0:32], ybuf[j][:, 512:544], [Byb[j]], [Byb[j]])
                    gln(T[5], BT[5], lag, lab, j, AF.Silu, ymix[:, j, :], Bym)
                for j in range(2):
                    zchunk(8 + j, 0)
                    zchunk(10 + j, 1)
                    cp("act", T[1], psb[0][:], [PB[0]], [BT[1]])
                    tt("dve", ubuf[j][:, 4:516], psb[1][:], T[1], ALU.mult, [PB[1], BT[1]], [Bub[j]])
                    ts("dve", T[5], ubuf[j][:, 2:514], cbw[:, j, 0:1], None, ALU.mult, None, [Bub[j], Bc], [BT[5]])
                    for k in range(1, 3):
                        stt("dve", T[5], ubuf[j][:, 2 + k:514 + k], cbw[:, j, k:k + 1], T[5], ALU.mult, ALU.add, [Bub[j], Bc, BT[5]], [BT[5]])
                    cp("pool", ubuf[j][:, 0:4], ubuf[j][:, 512:516], [Bub[j]], [Bub[j]])
                    zchunk(6 + j, 0)
                    tt("dve", ymix[:, 3 + j, :], psb[0][:], T[5], ALU.mult, [PB[0], BT[5]], [Bym])
                for j in range(3):
                    zchunk(15 + j, 0)
                    cp("act", T[5], psb[0][:], [PB[0]], [BT[5]])
                    gln(T[5], BT[5], lcg, lcb, j, AF.Identity, vln[:, j, :], Bvln)
                    zchunk(12 + j, 1)
                    cp("act", uC[:, j, :], psb[1][:], [PB[1]], [BuC])
                for sbk in range(4):
                    csl = slice(sbk * 128, (sbk + 1) * 128)
                    for j in range(3):
                        tr(pbf(4)[:, j * 128:(j + 1) * 128], vln[:, j, csl], identb, [Bvln, Bc], [PB[4]])
                    cp("act", vlnT, pbf(4)[:, 0:384], [PB[4]], [BvlnT])
                    for j in range(3):
                        mm(psb[5][:, 0:256], vlnT[:, j * 128:(j + 1) * 128], wsT[:, j, :], True, True, [BvlnT, BW], [PB[5]])
                        for hh in range(2):
                            rs = slice(hh * 64, (hh + 1) * 64)
                            tt("dve", T[1][rs, 0:128], psb[5][rs, hh * 128:(hh + 1) * 128], bsb[rs, j, :], ALU.add, [PB[5], Bc], [BT[1]])
                            tt("dve", ymix[rs, 5 + j, csl], T[1][rs, 0:128], uC[rs, j, csl], ALU.mult, [BT[1], BuC], [Bym])
                for dc in range(8):
                    bk = 6 + dc % 2
                    for k in range(8):
                        mm(psb[bk][:], Wout[:, k, dc * 128:(dc + 1) * 128], ymix[:, k, :], k == 0, k == 7, [BW, Bym], [PB[bk]])
                    tt("dve", hT[:, dc, :], hT[:, dc, :], psb[bk][:], ALU.add, [BhT, PB[bk]], [BhT])
                dma(h1_v[:, :, tsl], hT, [BhT], [])
                rmsnorm_to(hT, gffn, aT, BhT, BaT, ymix, Bym, rstd, Brs, T[0], BT[0], 7)
                dma(nT_v[:, :, tsl], aT, [BaT], [])
                for qc in range(16):
                    bk = 6 + qc % 2
                    for k in range(8):
                        mm(psb[bk][:], Wq[:, k, qc * 128:(qc + 1) * 128], aT[:, k, :], k == 0, k == 7, [BW, BaT], [PB[bk]])
                    cp("act" if qc % 2 == 0 else "dve", qT[:, qc, :], psb[bk][:], [PB[bk]], [BqT])
                for tl in range(4):
                    csl = slice(tl * 128, (tl + 1) * 128)
                    for g in range(16):
                        mm(psb[g // 4][:, (g % 4) * 128:(g % 4 + 1) * 128], qT[:, g, csl], keysb[:, g, :], True, True,
                           [BqT, BW], [PB[g // 4]])
                    for q4 in range(4):
                        cp("act", s_sb[:, q4 * 512:(q4 + 1) * 512], psb[q4][:], [PB[q4]], [Bs])
                    for g in range(16):
                        sg = s_sb[:, g * 128:(g + 1) * 128]
                        vmax(vtop[:, g, 0:8], sg, [Bs], [Bv])
                        mrep(tmpk[:, 0:128], vtop[:, g, 0:8], sg, [Bs, Bv], [Btk_])
                        vmax(vtop[:, g, 8:16], tmpk[:, 0:128], [Btk_], [Bv])
                    v4 = vtop.rearrange("p (h two) k -> p h two k", two=2)
                    c4 = cand.rearrange("p h (i j) -> p h i j", i=16)
                    tt("dve", c4, v4[:, :, 0, :].unsqueeze(3).to_broadcast([128, 8, 16, 16]),
                       v4[:, :, 1, :].unsqueeze(2).to_broadcast([128, 8, 16, 16]), ALU.add, [Bv], [Bcand])
                    for h in range(8):
                        vmax(vals[:, h, 0:8], cand[:, h, :], [Bcand], [Bvals])
                        mrep(tmpk, vals[:, h, 0:8], cand[:, h, :], [Bcand, Bvals], [Btk_])
                        vmax(vals[:, h, 8:16], tmpk, [Btk_], [Bvals])
                    tt("dve", evx, vals, vals[:, :, 0:1].to_broadcast([128, 8, 16]), ALU.subtract, [Bvals], [Bevx])
                    act(evx, evx, AF.Exp, [Bevx], [Bevx])
                    R.op("dve", lambda e: e.reduce_sum(out=zz, in_=evx, axis=AX.X), [Bevx], [Bzz])
                    act(zz, zz, AF.Ln, [Bzz], [Bzz])
                    cp("dve", tk[:, 0:8], vals[:, :, 15], [Bvals], [Btkk])
                    tt("dve", zz, zz, vals[:, :, 0], ALU.add, [Bzz, Bvals], [Bzz])
                    ts("dve", tk[:, 8:16], zz, -1.0, None, ALU.mult, None, [Bzz], [Btkk])
                    rows = slice(b * 512 + tl * 128, b * 512 + (tl + 1) * 128)
                    dma(sS_d[rows, :], s_sb, [Bs], [])
                    dma(sTK_d[rows, :], tk, [Btkk], [])
            R.barrier()

            off[0] = base_off
            nTb = alloc([8, 512], BF16)
            s4 = [alloc([2048]) for _ in range(4)]; tk4 = [alloc([16]) for _ in range(4)]
            accO = [alloc([1024]) for _ in range(4)]
            h1T = alloc([8, 512])
            Ub = [alloc([8, 1024], BF16) for _ in range(2)]; Vb = [alloc([8, 1024], BF16) for _ in range(2)]
            Xb = [alloc([8, 128]) for _ in range(2)]; Eh = [alloc([8, 128], BF16) for _ in range(2)]
            Mb = [alloc([8, 128]) for _ in range(2)]
            Gacc = alloc([8, 128]); gel = alloc([1024], BF16); Hb = alloc([1024], BF16); HT = alloc([8, 128], BF16)
            BnT = Buf("nTb"); Bs4 = [Buf("s4%d" % i) for i in range(4)]; BaccO = [Buf("accO%d" % i) for i in range(4)]
            Bh1 = Buf("h1T"); BUb = [Buf("Ub%d" % i) for i in range(2)]; BVb = [Buf("Vb%d" % i) for i in range(2)]
            BX = [Buf("X%d" % i) for i in range(2)]; BEh = [Buf("Eh%d" % i) for i in range(2)]; BM = [Buf("M%d" % i) for i in range(2)]
            BG = Buf("Gacc"); Bgel = Buf("gel"); BHb = Buf("Hb"); BHT = Buf("HT")
            ubf_v = ubf_d[l].rearrange("c p e -> p c e")
            vbf_v = vbf_d[l].rearrange("a p d -> p a d")
            gelv = gel.rearrange("p (a b) -> p a b", a=8)
            Gf = Gacc.rearrange("p a b -> p (a b)")
            cnt = 0
            for tb in range(NBLK):
                tsl = slice(tb * 512, (tb + 1) * 512)
                dma(nTb, nT_v[:, :, tsl], [], [BnT])
                for tl in range(4):
                    rows = slice(tb * 512 + tl * 128, tb * 512 + (tl + 1) * 128)
                    dma(s4[tl], sS_d[rows, :], [], [Bs4[tl]])
                    dma(tk4[tl], sTK_d[rows, :], [], [Bs4[tl]])
                    mset("pool", accO[tl], 0.0, [BaccO[tl]])

                def ldw(wb):
                    dma(Ub[wb % 2], ubf_v[:, :, wb * 1024:(wb + 1) * 1024], [], [BUb[wb % 2]])
                    dma(Vb[wb % 2], vbf_v[:, wb * 8:(wb + 1) * 8, :], [], [BVb[wb % 2]])

                ldw(0)
                for wb in range(16):
                    if wb + 1 < 16:
                        ldw(wb + 1)
                    U_, V_ = Ub[wb % 2], Vb[wb % 2]
                    for tl in range(4):
                        ab = (cnt % 2) * 2
                        cnt += 1
                        csl = slice(tl * 128, (tl + 1) * 128)
                        for k in range(8):
                            for hf in range(2):
                                mm(psb[ab + hf][:], nTb[:, k, csl], U_[:, k, hf * 512:(hf + 1) * 512], k == 0, k == 7,
                                   [BnT, BUb[wb % 2]], [PB[ab + hf]])
                        for hf in range(2):
                            act(gel[:, hf * 512:(hf + 1) * 512], psb[ab + hf][:], AF.Gelu_apprx_tanh, [PB[ab + hf]], [Bgel])
                        for h in range(8):
                            xi = h % 2
                            s1 = s4[tl][:, (2 * h) * 128 + wb * 8:(2 * h) * 128 + wb * 8 + 8]
                            s2 = s4[tl][:, (2 * h + 1) * 128:(2 * h + 2) * 128]
                            tt("pool", Xb[xi], s1.unsqueeze(2).to_broadcast([128, 8, 128]),
                               s2.unsqueeze(1).to_broadcast([128, 8, 128]), ALU.add, [Bs4[tl]], [BX[xi]])
                            act(Eh[xi], Xb[xi], AF.Exp, [BX[xi], Bs4[tl]], [BEh[xi]], bias=tk4[tl][:, 8 + h:9 + h])
                            if h == 0:
                                stt("dve", Gacc, Xb[xi], tk4[tl][:, h:h + 1], Eh[xi], ALU.is_ge, ALU.mult,
                                    [BX[xi], BEh[xi], Bs4[tl]], [BG])
                            else:
                                stt("dve", Mb[xi], Xb[xi], tk4[tl][:, h:h + 1], Eh[xi], ALU.is_ge, ALU.mult,
                                    [BX[xi], BEh[xi], Bs4[tl]], [BM[xi]])
                                tt("dve", Gacc, Gacc, Mb[xi], ALU.add, [BG, BM[xi]], [BG])
                        tt("dve", Hb, gel, Gf, ALU.mult, [Bgel, BG], [BHb])
                        for et in range(8):
                            tr(pbf(4)[:, et * 128:(et + 1) * 128], Hb[:, et * 128:(et + 1) * 128], identb, [BHb, Bc], [PB[4]])
                        cp("act", HT, pbf(4)[:].rearrange("p (a b) -> p a b", a=8), [PB[4]], [BHT])
                        for et in range(8):
                            for hf in range(2):
                                mm(psb[5 + hf][:], HT[:, et, :], V_[:, et, hf * 512:(hf + 1) * 512], et == 0, et == 7,
                                   [BHT, BVb[wb % 2]], [PB[5 + hf]])
                        for hf in range(2):
                            tt("dve", accO[tl][:, hf * 512:(hf + 1) * 512], accO[tl][:, hf * 512:(hf + 1) * 512], psb[5 + hf][:],
                               ALU.add, [BaccO[tl], PB[5 + hf]], [BaccO[tl]])
                dma(h1T, h1_v[:, :, tsl], [], [Bh1])
                for tl in range(4):
                    csl = slice(tl * 128, (tl + 1) * 128)
                    for dc in range(8):
                        bk = 6 + (dc // 4) % 2
                        tr(psb[bk][:, (dc % 4) * 128:(dc % 4 + 1) * 128], accO[tl][:, dc * 128:(dc + 1) * 128], ident,
                           [BaccO[tl], Bc], [PB[bk]])
                        if dc % 4 == 3:
                            d0 = dc - 3
                            tt("dve", h1T[:, d0:d0 + 4, csl], h1T[:, d0:d0 + 4, csl],
                               psb[bk][:].rearrange("p (a b) -> p a b", a=4), ALU.add, [Bh1, PB[bk]], [Bh1])
                dma(h2_v[:, :, tsl], h1T, [Bh1], [])
            R.barrier()

            off[0] = base_off
            Wpg = alloc([8, 1024], BF16); Wpe = alloc([2, 1024], BF16); gple = alloc([8]); gfin = alloc([8])
            stg = alloc([1024]); hT = alloc([8, 512]); aT = alloc([8, 512], BF16); sqb = alloc([8, 512], BF16)
            rstd = alloc([512]); T0 = alloc([512]); T1 = alloc([512]); pTf = alloc([2, 512]); pTb = alloc([2, 512], BF16)
            oT = alloc([8, 512]); otok = alloc([1024])
            BW = Buf("Wc"); Bstg = Buf("stgc"); BhT = Buf("hTc"); BaT = Buf("aTc"); Bsq = Buf("sqc"); Brs = Buf("rsc")
            BT0 = Buf("T0c"); BT1 = Buf("T1c"); BpTf = Buf("pTf"); BpTb = Buf("pTb"); BoT = Buf("oT"); Botok = Buf("otok")
            load_w_bf(Wpg, wpg_d[l], 8, 1024, stg, Bstg, BW)
            load_w_bf(Wpe, wpe_d[l], 2, 1024, stg, Bstg, BW)
            dma(gple, gple_d[l], [], [Bc])
            dma(gfin, gfin_d, [], [Bc])
            last = (l == L - 1)
            for b in range(NBLK):
                tsl = slice(b * 512, (b + 1) * 512)
                dma(hT, h2_v[:, :, tsl], [], [BhT])
                dma(pTf, pT_d[l].rearrange("c p s -> p c s")[:, :, tsl], [], [BpTf])
                cp("pool", pTb, pTf, [BpTf], [BpTb])
                rmsnorm_to(hT, gple, aT, BhT, BaT, sqb, Bsq, rstd, Brs, T0, BT0, 7)
                for dc in range(8):
                    b0, b1 = (dc % 2) * 2, (dc % 2) * 2 + 1
                    for k in range(8):
                        mm(psb[b0][:], Wpg[:, k, dc * 128:(dc + 1) * 128], aT[:, k, :], k == 0, k == 7, [BW, BaT], [PB[b0]])
                    for k in range(2):
                        mm(psb[b1][:], Wpe[:, k, dc * 128:(dc + 1) * 128], pTb[:, k, :], k == 0, k == 1, [BW, BpTb], [PB[b1]])
                    act(T1, psb[b0][:], AF.Sigmoid, [PB[b0]], [BT1])
                    tt("dve", T1, T1, psb[b1][:], ALU.mult, [BT1, PB[b1]], [BT1])
                    tt("dve", hT[:, dc, :], hT[:, dc, :], T1, ALU.add, [BhT, BT1], [BhT])
                if not last:
                    dma(hcur_v[:, :, tsl], hT, [BhT], [])
                else:
                    if dbg:
                        dma(hcur_v[:, :, tsl], hT, [BhT], [])
                    for c in range(8):
                        act(sqb[:, c, :], hT[:, c, :], AF.Square, [BhT], [Bsq])
                    for c in range(8):
                        mm(psb[7][:], onesb, sqb[:, c, :], c == 0, c == 7, [Bsq, Bc], [PB[7]])
                    act(T0, psb[7][:], AF.Sqrt, [PB[7], Bc], [BT0], bias=epst[:, 0:1])
                    recip(rstd, T0, [BT0], [Brs])
                    for c in range(8):
                        stt("dve", oT[:, c, :], hT[:, c, :], gfin[:, c:c + 1], rstd, ALU.mult, ALU.mult,
                            [BhT, Brs, Bc], [BoT])
                    for tl in range(4):
                        csl = slice(tl * 128, (tl + 1) * 128)
                        for dc in range(8):
                            bk = 4 + (dc // 4)
                            tr(psb[bk][:, (dc % 4) * 128:(dc % 4 + 1) * 128], oT[:, dc, csl], ident, [BoT, Bc], [PB[bk]])
                        cp("act", otok[:, 0:512], psb[4][:], [PB[4]], [Botok])
                        cp("dve", otok[:, 512:1024], psb[5][:], [PB[5]], [Botok])
                        dma(out_d[b * 512 + tl * 128:b * 512 + (tl + 1) * 128, :], otok, [Botok], [])
            R.barrier()
        R.emit()
    return nc, R


def _cols(v, n):
    L_ = v.shape[0]
    return np.ascontiguousarray(v.reshape(L_, n, 128).transpose(0, 2, 1))


def prep_shared(inp, L):
    f = lambda a: np.ascontiguousarray(np.asarray(a, dtype=np.float32))
    sh = {}
    sh["g_mix"] = _cols(f(inp["g_mix"])[:L], 8); sh["g_ffn"] = _cols(f(inp["g_ffn"])[:L], 8); sh["g_ple"] = _cols(f(inp["g_ple"])[:L], 8)
    sh["g_final"] = _cols(f(inp["g_final"])[None], 8)[0]
    for k in ("w_in", "w_out", "w_q", "w_pg", "w_pe", "expert_v"):
        sh[k] = f(inp[k])[:L]
    caw = f(inp["conv_a_w"])[:L]
    sh["conv_a_w"] = np.ascontiguousarray(caw.reshape(L, 31, 3, 128).transpose(0, 3, 2, 1))
    for k in ("conv_a_b", "ln_a_g", "ln_a_b", "ln_c_g", "ln_c_b"):
        sh[k] = _cols(f(inp[k])[:L], 3)
    cbw = f(inp["conv_b_w"])[:L]
    sh["conv_b_w"] = np.ascontiguousarray(cbw.reshape(L, 3, 2, 128).transpose(0, 3, 2, 1))
    ws = f(inp["w_s"])[:L]
    sh["w_sT"] = np.ascontiguousarray(ws.reshape(L, 3, 2, 128, 128).transpose(0, 4, 1, 2, 3).reshape(L, 128, 3, 256))
    bs = f(inp["b_s"])[:L]
    bsr = bs.reshape(L, 3, 2, 128)
    sh["b_sb"] = np.ascontiguousarray(np.repeat(bsr.transpose(0, 2, 1, 3), 64, axis=1))
    sk = f(inp["sub_keys"])[:L]
    sh["keysT"] = np.ascontiguousarray(sk.reshape(L, 16, 128, 128).transpose(0, 3, 1, 2))
    sh["expert_uT"] = np.ascontiguousarray(f(inp["expert_u"])[:L].transpose(0, 2, 1))
    return sh


_CACHE = {}


def kernel(**inputs):
    S, L = SEQ, DEPTH
    key = (S, L)
    if key not in _CACHE:
        _CACHE[key] = build(S, L)[0]
    nc = _CACHE[key]
    sh = prep_shared(inputs, L)
    x = np.asarray(inputs["x"], dtype=np.float32)
    p = np.asarray(inputs["p"], dtype=np.float32)
    in_maps = []
    for c in range(NCORES):
        m = dict(sh)
        m["x"] = np.ascontiguousarray(x[c])
        m["p"] = np.ascontiguousarray(p[:, c])
        in_maps.append(m)
    res = run_bass_kernel_spmd(nc, in_maps, core_ids=list(range(NCORES)))
    return np.stack([np.asarray(r["out"], dtype=np.float32) for r in res.results], axis=0)
```

```python
import contextlib
import numpy as np
import concourse.bass as bass
import concourse.mybir as mybir
from concourse.bass_utils import run_bass_kernel_spmd

F32 = mybir.dt.float32
BF16 = mybir.dt.bfloat16
AF = mybir.ActivationFunctionType
ALU = mybir.AluOpType
AX = mybir.AxisListType

D = 1024
NCORES = 8
SEQ = 8192
DEPTH = 2
EPS = 1e-6
NEG = -1.0e30
ENGS = ("pe", "act", "dve", "pool", "sp")


class Buf:
    __slots__ = ("name", "last_w", "readers", "dma_readers")

    def __init__(self, name):
        self.name = name
        self.last_w = None
        self.readers = {}
        self.dma_readers = []


class Op:
    __slots__ = ("eng", "fn", "deps", "signal", "dma", "semkey", "semval", "idx")

    def __init__(self, eng, fn, dma):
        self.eng = eng
        self.fn = fn
        self.deps = []
        self.signal = False
        self.dma = dma
        self.semkey = None
        self.semval = None


class Rec:
    NDMA = 24

    def __init__(self, nc):
        self.nc = nc
        self.ops = {e: [] for e in ENGS}
        self.last_real = {e: None for e in ENGS}
        self.ndma = 0
        self.dma_ops = []

    def op(self, eng, fn, reads=(), writes=(), dma=False):
        o = Op(eng, fn, dma)
        o.idx = len(self.ops[eng])
        deps = []
        for b in reads:
            if b.last_w is not None:
                deps.append(b.last_w)
        for b in writes:
            if b.last_w is not None:
                deps.append(b.last_w)
            deps.extend(b.readers.values())
            deps.extend(b.dma_readers)
        latest = {}
        dmas = []
        seen = set()
        for d in deps:
            if d is o or id(d) in seen:
                continue
            seen.add(id(d))
            if d.dma:
                dmas.append(d)
            else:
                if d.eng == "pe" and eng == "pe" and not dma:
                    continue
                if d.eng not in latest or latest[d.eng].idx < d.idx:
                    latest[d.eng] = d
        for d in list(latest.values()) + dmas:
            o.deps.append(d)
            d.signal = True
        if dma:
            o.signal = True
            k = self.ndma
            self.ndma += 1
            o.semkey = ("dma", k % self.NDMA)
            o.semval = 16 * (k // self.NDMA + 1)
            if k >= self.NDMA:
                o.deps.append(self.dma_ops[k - self.NDMA])
            self.dma_ops.append(o)
        for b in reads:
            if dma:
                b.dma_readers.append(o)
            else:
                b.readers[eng] = o
        for b in writes:
            b.last_w = o
            b.readers = {}
            b.dma_readers = []
        self.ops[eng].append(o)
        self.last_real[eng] = o
        return o

    def barrier(self):
        lasts = [o for o in self.last_real.values() if o is not None]
        lasts += self.dma_ops[-self.NDMA:]
        for d in lasts:
            d.signal = True
        for e in ENGS:
            o = Op(e, None, False)
            seen = set()
            for d in lasts:
                if id(d) in seen:
                    continue
                seen.add(id(d))
                o.deps.append(d)
            self.ops[e].append(o)

    EPOCH = 6000

    def emit(self):
        nc = self.nc
        nep = {}
        for e in ENGS:
            cnt = 0
            for o in self.ops[e]:
                if o.dma or o.fn is None:
                    continue
                if o.signal:
                    o.semkey = ("eng", e, cnt // self.EPOCH)
                    o.semval = cnt % self.EPOCH + 1
                    cnt += 1
            nep[e] = cnt // self.EPOCH + 1
        with contextlib.ExitStack() as st:
            sems = {}
            for e in ENGS:
                for ep in range(nep[e]):
                    sems[("eng", e, ep)] = st.enter_context(nc.semaphore("s_%s%d" % (e, ep)))
            for i in range(self.NDMA):
                sems[("dma", i)] = st.enter_context(nc.semaphore("s_dma%d" % i))
            block = st.enter_context(nc.Block())

            def run(e, eng):
                waited = {}
                for o in self.ops[e]:
                    for d in o.deps:
                        if waited.get(d.semkey, 0) >= d.semval:
                            continue
                        waited[d.semkey] = d.semval
                        eng.wait_ge(sems[d.semkey], d.semval)
                    if o.fn is None:
                        continue
                    ins = o.fn(eng)
                    if o.signal:
                        ins.then_inc(sems[o.semkey], 16 if o.dma else 1)

            block.tensor(lambda eng: run("pe", eng))
            block.scalar(lambda eng: run("act", eng))
            block.vector(lambda eng: run("dve", eng))
            block.gpsimd(lambda eng: run("pool", eng))
            block.sync(lambda eng: run("sp", eng))


def build(S, L, dbg=False):
    assert S % 512 == 0
    NBLK = S // 512
    nc = bass.Bass("TRN2", target_bir_lowering=False)

    def din(name, shape, dt=F32):
        return nc.dram_tensor(name, list(shape), dt, kind="ExternalInput").ap()

    def dscr(name, shape, dt=F32, out=False):
        return nc.dram_tensor(name, list(shape), dt, kind=("ExternalOutput" if out else "Internal")).ap()

    x_d = din("x", [S, D])
    p_d = din("p", [L, S, 256])
    gmix_d = din("g_mix", [L, 128, 8]); gffn_d = din("g_ffn", [L, 128, 8]); gple_d = din("g_ple", [L, 128, 8])
    gfin_d = din("g_final", [128, 8])
    win_d = din("w_in", [L, D, 2304]); wout_d = din("w_out", [L, D, D]); wq_d = din("w_q", [L, D, 2048])
    wpg_d = din("w_pg", [L, D, D]); wpe_d = din("w_pe", [L, 256, D])
    caw_d = din("conv_a_w", [L, 128, 3, 31]); cab_d = din("conv_a_b", [L, 128, 3])
    lag_d = din("ln_a_g", [L, 128, 3]); lab_d = din("ln_a_b", [L, 128, 3])
    cbw_d = din("conv_b_w", [L, 128, 2, 3])
    lcg_d = din("ln_c_g", [L, 128, 3]); lcb_d = din("ln_c_b", [L, 128, 3])
    wst_d = din("w_sT", [L, 128, 3, 256]); bsb_d = din("b_sb", [L, 128, 3, 128])
    keys_d = din("keysT", [L, 128, 16, 128])
    eu_d = din("expert_uT", [L, D, 16384]); ev_d = din("expert_v", [L, 16384, D])
    out_d = nc.dram_tensor("out", [S, D], F32, kind="ExternalOutput").ap()

    hcur_d = dscr("hcur", [8, 128, S], out=dbg)
    h1_d = dscr("h1", [8, 128, S], out=dbg)
    h2_d = dscr("h2", [8, 128, S], out=dbg)
    pT_d = dscr("pT", [L, 2, 128, S])
    nT_d = dscr("nT", [8, 128, S], BF16)
    sS_d = dscr("sS", [S, 2048], out=dbg)
    sTK_d = dscr("sTK", [S, 16], out=dbg)
    ubf_d = dscr("ubf", [L, 8, 128, 16384], BF16)
    vbf_d = dscr("vbf", [L, 128, 128, D], BF16)

    st = contextlib.ExitStack()
    with st:
        CAP = 206 * 1024 + 512
        arena_t = st.enter_context(nc.sbuf_tensor("arena", [128, CAP // 4], F32))
        psb = [st.enter_context(nc.psum_tensor("pb%d" % i, [128, 512], F32)) for i in range(8)]
        PB = [Buf("pb%d" % i) for i in range(8)]
        R = Rec(nc)
        off = [0]

        def alloc(shape, dt=F32):
            n = int(np.prod(shape)) * (2 if dt == BF16 else 4)
            n = (n + 31) // 32 * 32
            a = arena_t[:, off[0] // 4:(off[0] + n) // 4]
            off[0] += n
            assert off[0] <= CAP, ("SBUF arena overflow", off[0])
            if dt == BF16:
                a = a.bitcast(BF16)
            a = a[:, 0:int(np.prod(shape))]
            if len(shape) == 2:
                a = a.rearrange("p (a b) -> p a b", a=shape[0])
            elif len(shape) == 3:
                a = a.rearrange("p (a b c) -> p a b c", a=shape[0], b=shape[1])
            return a

        def mm(out, lhsT, rhs, start, stop, rd, wr):
            return R.op("pe", lambda e: e.matmul(out, lhsT=lhsT, rhs=rhs, start=start, stop=stop), rd, wr)

        def tr(out, in_, ident, rd, wr):
            return R.op("pe", lambda e: e.transpose(out=out, in_=in_, identity=ident), rd, wr)

        def act(out, in_, func, rd, wr, scale=None, bias=None):
            kw = {}
            if scale is not None:
                kw["scale"] = scale
            if bias is not None:
                kw["bias"] = bias
            return R.op("act", lambda e: e.activation(out=out, in_=in_, func=func, **kw), rd, wr)

        def tt(eng, out, in0, in1, op, rd, wr):
            return R.op(eng, lambda e: e.tensor_tensor(out=out, in0=in0, in1=in1, op=op), rd, wr)

        def ts(eng, out, in0, s1, s2, op0, op1, rd, wr):
            if s2 is None:
                return R.op(eng, lambda e: e.tensor_scalar(out=out, in0=in0, scalar1=s1, scalar2=None, op0=op0), rd, wr)
            return R.op(eng, lambda e: e.tensor_scalar(out=out, in0=in0, scalar1=s1, scalar2=s2, op0=op0, op1=op1), rd, wr)

        def stt(eng, out, in0, scalar, in1, op0, op1, rd, wr):
            return R.op(eng, lambda e: e.scalar_tensor_tensor(out=out, in0=in0, scalar=scalar, in1=in1, op0=op0, op1=op1), rd, wr)

        def cp(eng, out, in_, rd, wr):
            if eng == "act":
                return R.op("act", lambda e: e.copy(out=out, in_=in_), rd, wr)
            return R.op(eng, lambda e: e.tensor_copy(out=out, in_=in_), rd, wr)

        def mset(eng, ap, val, wr):
            return R.op(eng, lambda e: e.memset(ap, val), (), wr)

        def dma(out, in_, rd, wr):
            return R.op("sp", lambda e: e.dma_start(out=out, in_=in_), rd, wr, dma=True)

        def recip(out, in_, rd, wr):
            return R.op("dve", lambda e: e.reciprocal(out=out, in_=in_), rd, wr)

        def vmax(out, in_, rd, wr):
            return R.op("dve", lambda e: e.max(out=out, in_=in_), rd, wr)

        def mrep(out, rep, vals, rd, wr):
            return R.op("dve", lambda e: e.match_replace(out=out, in_to_replace=rep, in_values=vals, imm_value=NEG), rd, wr)

        def pbf(i):
            return psb[i][:].bitcast(BF16)

        ident = alloc([128]); identb = alloc([128], BF16); onesb = alloc([128], BF16); blk = alloc([128])
        epst = alloc([1])
        Bc = Buf("consts")
        mset("pool", ident, 0.0, [Bc])
        R.op("pool", lambda e: e.affine_select(out=ident, in_=ident, pattern=[[-1, 128]], compare_op=ALU.not_equal,
                                               fill=1.0, base=0, channel_multiplier=1), [Bc], [Bc])
        cp("dve", identb, ident, [Bc], [Bc])
        mset("pool", onesb, 1.0 / 1024.0, [Bc])
        mset("pool", blk, 0.0, [Bc])
        mset("pool", blk[0:64, 0:64], 1.0 / 64.0, [Bc])
        mset("pool", blk[64:128, 64:128], 1.0 / 64.0, [Bc])
        mset("pool", epst, EPS, [Bc])
        base_off = off[0]

        xs = alloc([1024]); xst = alloc([8, 128]); cst = alloc([2048]); cstb = alloc([2048], BF16)
        pst = alloc([2, 128])
        Bxs, Bxst, Bcst, Bcstb, Bpst = Buf("xs"), Buf("xst"), Buf("cst"), Buf("cstb"), Buf("pst")
        hcur_v = hcur_d.rearrange("c p s -> p c s")
        h1_v = h1_d.rearrange("c p s -> p c s")
        h2_v = h2_d.rearrange("c p s -> p c s")
        nT_v = nT_d.rearrange("c p s -> p c s")
        for tl in range(S // 128):
            ts_ = slice(tl * 128, (tl + 1) * 128)
            dma(xs, x_d[ts_, :], [], [Bxs])
            for c in range(8):
                bk = c // 4
                tr(psb[bk][:, (c % 4) * 128:(c % 4 + 1) * 128], xs[:, c * 128:(c + 1) * 128], ident, [Bxs, Bc], [PB[bk]])
            cp("act", xst[:, 0:4, :], psb[0][:].rearrange("p (a b) -> p a b", a=4), [PB[0]], [Bxst])
            cp("dve", xst[:, 4:8, :], psb[1][:].rearrange("p (a b) -> p a b", a=4), [PB[1]], [Bxst])
            dma(hcur_v[:, :, ts_], xst, [Bxst], [])
            for l in range(L):
                dma(xs[:, 0:256], p_d[l, ts_, :], [], [Bxs])
                for c in range(2):
                    tr(psb[2][:, c * 128:(c + 1) * 128], xs[:, c * 128:(c + 1) * 128], ident, [Bxs, Bc], [PB[2]])
                cp("act", pst, psb[2][:, 0:256].rearrange("p (a b) -> p a b", a=2), [PB[2]], [Bpst])
                dma(pT_d[l].rearrange("c p s -> p c s")[:, :, ts_], pst, [Bpst], [])
        it = 0
        for l in range(L):
            for c in range(8):
                for cb in range(8):
                    dma(cst, eu_d[l, c * 128:(c + 1) * 128, cb * 2048:(cb + 1) * 2048], [], [Bcst])
                    cp("dve" if it % 2 == 0 else "act", cstb, cst, [Bcst], [Bcstb])
                    dma(ubf_d[l, c, :, cb * 2048:(cb + 1) * 2048], cstb, [Bcstb], [])
                    it += 1
            for et in range(0, 128, 2):
                dma(cst.rearrange("p (a b) -> p a b", a=2),
                    ev_d[l, et * 128:(et + 2) * 128, :].rearrange("(a p) d -> p a d", a=2), [], [Bcst])
                cp("dve" if it % 2 == 0 else "act", cstb, cst, [Bcst], [Bcstb])
                dma(vbf_d[l, et:et + 2].rearrange("a p d -> p a d"), cstb.rearrange("p (a b) -> p a b", a=2), [Bcstb], [])
                it += 1
        R.barrier()

        def rmsnorm_to(hT, g, outT, BhT, BoutT, sqb, Bsq, rstd, Brstd, t0buf, Bt0, bank, out_f32=False):
            for c in range(8):
                if c % 2 == 0:
                    act(sqb[:, c, :], hT[:, c, :], AF.Square, [BhT], [Bsq])
                else:
                    tt("pool", sqb[:, c, :], hT[:, c, :], hT[:, c, :], ALU.mult, [BhT], [Bsq])
            for c in range(8):
                mm(psb[bank][:], onesb, sqb[:, c, :], c == 0, c == 7, [Bsq, Bc], [PB[bank]])
            act(t0buf, psb[bank][:], AF.Sqrt, [PB[bank], Bc], [Bt0], bias=epst[:, 0:1])
            recip(rstd, t0buf, [Bt0], [Brstd])
            for c in range(8):
                stt("dve", outT[:, c, :], hT[:, c, :], g[:, c:c + 1], rstd, ALU.mult, ALU.mult,
                    [BhT, Brstd, Bc], [BoutT])

        def load_w_bf(dst, src_d, nk, ncol, stage, Bst, Bdst):
            for k in range(nk):
                dma(stage[:, 0:ncol], src_d[k * 128:(k + 1) * 128, :], [], [Bst])
                cp("dve" if k % 2 == 0 else "act", dst[:, k, :], stage[:, 0:ncol], [Bst], [Bdst])

        for l in range(L):
            off[0] = base_off
            Win = alloc([8, 2304], BF16); Wout = alloc([8, 1024], BF16); Wq = alloc([8, 2048], BF16)
            keysb = alloc([16, 128], BF16); wsT = alloc([3, 256], BF16)
            gmix = alloc([8]); gffn = alloc([8]); caw = alloc([3, 31]); cab = alloc([3]); lag = alloc([3]); lab = alloc([3])
            cbw = alloc([2, 3]); lcg = alloc([3]); lcb = alloc([3]); bsb = alloc([3, 128])
            hT = alloc([8, 512]); aT = alloc([8, 512], BF16); ymix = alloc([8, 512], BF16)
            rstd = alloc([512]); T = [alloc([512]) for _ in range(6)]
            ybuf = [alloc([544]) for _ in range(3)]; ubuf = [alloc([516]) for _ in range(2)]
            uC = alloc([3, 512]); vln = alloc([3, 512], BF16); vlnT = alloc([384], BF16)
            qT = alloc([16, 512], BF16)
            s_sb = alloc([2048]); vtop = alloc([16, 16]); tmpk = alloc([16, 128]); cand = alloc([8, 256]); vals = alloc([8, 16])
            evx = alloc([8, 16]); zz = alloc([8]); tk = alloc([16])
            stage = qT.bitcast(F32) if False else None
            BW = Buf("W"); BhT = Buf("hT"); BaT = Buf("aT"); Bym = Buf("ymix"); Brs = Buf("rstd")
            BT = [Buf("T%d" % i) for i in range(6)]
            Byb = [Buf("yb%d" % i) for i in range(3)]; Bub = [Buf("ub%d" % i) for i in range(2)]
            BuC = Buf("uC"); Bvln = Buf("vln"); BvlnT = Buf("vlnT"); BqT = Buf("qT")
            Bs = Buf("s_sb"); Bv = [Buf("vtop%d" % i) for i in range(16)]; Btk_ = [Buf("tmpk%d" % i) for i in range(16)]; Bcand = Buf("cand"); Bvals = [Buf("vals%d" % i) for i in range(8)]
            Bevx = Buf("evx"); Bzz = Buf("zz"); Btkk = Buf("tk")
            stg = s_sb
            Bstg = Bs
            for k in range(8):
                for hf in range(2):
                    ncol = 1152
                    dma(stg[:, 0:ncol], win_d[l, k * 128:(k + 1) * 128, hf * ncol:(hf + 1) * ncol], [], [Bstg])
                    cp("dve" if hf == 0 else "act", Win[:, k, hf * ncol:(hf + 1) * ncol], stg[:, 0:ncol], [Bstg], [BW])
            load_w_bf(Wout, wout_d[l], 8, 1024, stg, Bstg, BW)
            load_w_bf(Wq, wq_d[l], 8, 2048, stg, Bstg, BW)
            dma(stg.rearrange("p (a b) -> p a b", a=16), keys_d[l], [], [Bstg])
            cp("dve", keysb, stg.rearrange("p (a b) -> p a b", a=16), [Bstg], [BW])
            dma(stg[:, 0:768].rearrange("p (a b) -> p a b", a=3), wst_d[l], [], [Bstg])
            cp("dve", wsT, stg[:, 0:768].rearrange("p (a b) -> p a b", a=3), [Bstg], [BW])
            for c in range(3):
                for hh in range(2):
                    mset("pool", wsT[64:128, c, hh * 128:hh * 128 + 64], 0.0, [BW])
            for (dst, src) in [(gmix, gmix_d[l]), (gffn, gffn_d[l]), (caw, caw_d[l]), (cab, cab_d[l]), (lag, lag_d[l]),
                               (lab, lab_d[l]), (cbw, cbw_d[l]), (lcg, lcg_d[l]), (lcb, lcb_d[l]), (bsb, bsb_d[l])]:
                dma(dst, src, [], [Bc])
            for j in range(3):
                mset("pool", ybuf[j][:, 0:32], 0.0, [Byb[j]])
            for j in range(2):
                mset("pool", ubuf[j][:, 0:4], 0.0, [Bub[j]])

            def zchunk(j, bank):
                for k in range(8):
                    mm(psb[bank][:], Win[:, k, j * 128:(j + 1) * 128], aT[:, k, :], k == 0, k == 7, [BW, BaT], [PB[bank]])

            def gln(src, Bsrc, g, b, j, func, dst, Bdst):
                mm(psb[2][:], blk, src, True, True, [Bsrc, Bc], [PB[2]])
                tt("dve", T[2], src, psb[2][:], ALU.subtract, [Bsrc, PB[2]], [BT[2]])
                act(T[3], T[2], AF.Square, [BT[2]], [BT[3]])
                mm(psb[3][:], blk, T[3], True, True, [BT[3], Bc], [PB[3]])
                act(T[4], psb[3][:], AF.Sqrt, [PB[3], Bc], [BT[4]], bias=epst[:, 0:1])
                recip(T[4], T[4], [BT[4]], [BT[4]])
                tt("dve", T[2], T[2], T[4], ALU.mult, [BT[2], BT[4]], [BT[2]])
                act(dst, T[2], func, [BT[2], Bc], [Bdst], scale=g[:, j:j + 1], bias=b[:, j:j + 1])

            for b in range(NBLK):
                tsl = slice(b * 512, (b + 1) * 512)
                dma(hT, hcur_v[:, :, tsl], [], [BhT])
                rmsnorm_to(hT, gmix, aT, BhT, BaT, ymix, Bym, rstd, Brs, T[0], BT[0], 7)
                for j in range(3):
                    zchunk(j, 0)
                    zchunk(3 + j, 1)
                    act(T[1], psb[1][:], AF.Sigmoid, [PB[1]], [BT[1]])
                    tt("dve", ybuf[j][:, 32:544], psb[0][:], T[1], ALU.mult, [PB[0], BT[1]], [Byb[j]])
                    ts("dve", T[5], ybuf[j][:, 2:514], caw[:, j, 0:1], cab[:, j:j + 1], ALU.mult, ALU.add, [Byb[j], Bc], [BT[5]])
                    for k in range(1, 31):
                        stt("dve", T[5], ybuf[j][:, 2 + k:514 + k], caw[:, j, k:k + 1], T[5], ALU.mult, ALU.add, [Byb[j], Bc, BT[5]], [BT[5]])
                    cp("pool", ybuf[j][:, 0:32], ybuf[j][:, 512:544], [Byb[j]], [Byb[j]])
                    gln(T[5], BT[5], lag, lab, j, AF.Silu, ymix[:, j, :], Bym)
                for j in range(2):
                    zchunk(8 + j, 0)
                    zchunk(10 + j, 1)
                    cp("act", T[1], psb[0][:], [PB[0]], [BT[1]])
                    tt("dve", ubuf[j][:, 4:516], psb[1][:], T[1], ALU.mult, [PB[1], BT[1]], [Bub[j]])
                    ts("dve", T[5], ubuf[j][:, 2:514], cbw[:, j, 0:1], None, ALU.mult, None, [Bub[j], Bc], [BT[5]])
                    for k in range(1, 3):
                        stt("dve", T[5], ubuf[j][:, 2 + k:514 + k], cbw[:, j, k:k + 1], T[5], ALU.mult, ALU.add, [Bub[j], Bc, BT[5]], [BT[5]])
                    cp("pool", ubuf[j][:, 0:4], ubuf[j][:, 512:516], [Bub[j]], [Bub[j]])
                    zchunk(6 + j, 0)
                    tt("dve", ymix[:, 3 + j, :], psb[0][:], T[5], ALU.mult, [PB[0], BT[5]], [Bym])
                for j in range(3):
                    zchunk(15 + j, 0)
                    cp("act", T[5], psb[0][:], [PB[0]], [BT[5]])
                    gln(T[5], BT[5], lcg, lcb, j, AF.Identity, vln[:, j, :], Bvln)
                    zchunk(12 + j, 1)
                    cp("act", uC[:, j, :], psb[1][:], [PB[1]], [BuC])
                for sbk in range(4):
                    csl = slice(sbk * 128, (sbk + 1) * 128)
                    for j in range(3):
                        tr(pbf(4)[:, j * 128:(j + 1) * 128], vln[:, j, csl], identb, [Bvln, Bc], [PB[4]])
                    cp("act", vlnT, pbf(4)[:, 0:384], [PB[4]], [BvlnT])
                    for j in range(3):
                        mm(psb[5][:, 0:256], vlnT[:, j * 128:(j + 1) * 128], wsT[:, j, :], True, True, [BvlnT, BW], [PB[5]])
                        for hh in range(2):
                            rs = slice(hh * 64, (hh + 1) * 64)
                            tt("dve", T[1][rs, 0:128], psb[5][rs, hh * 128:(hh + 1) * 128], bsb[rs, j, :], ALU.add, [PB[5], Bc], [BT[1]])
                            tt("dve", ymix[rs, 5 + j, csl], T[1][rs, 0:128], uC[rs, j, csl], ALU.mult, [BT[1], BuC], [Bym])
                for dc in range(8):
                    bk = 6 + dc % 2
                    for k in range(8):
                        mm(psb[bk][:], Wout[:, k, dc * 128:(dc + 1) * 128], ymix[:, k, :], k == 0, k == 7, [BW, Bym], [PB[bk]])
                    tt("dve", hT[:, dc, :], hT[:, dc, :], psb[bk][:], ALU.add, [BhT, PB[bk]], [BhT])
                dma(h1_v[:, :, tsl], hT, [BhT], [])
                rmsnorm_to(hT, gffn, aT, BhT, BaT, ymix, Bym, rstd, Brs, T[0], BT[0], 7)
                dma(nT_v[:, :, tsl], aT, [BaT], [])
                for qc in range(16):
                    bk = 6 + qc % 2
                    for k in range(8):
                        mm(psb[bk][:], Wq[:, k, qc * 128:(qc + 1) * 128], aT[:, k, :], k == 0, k == 7, [BW, BaT], [PB[bk]])
                    cp("act" if qc % 2 == 0 else "dve", qT[:, qc, :], psb[bk][:], [PB[bk]], [BqT])
                for tl in range(4):
                    csl = slice(tl * 128, (tl + 1) * 128)
                    for g in range(16):
                        mm(psb[g // 4][:, (g % 4) * 128:(g % 4 + 1) * 128], qT[:, g, csl], keysb[:, g, :], True, True,
                           [BqT, BW], [PB[g // 4]])
                    for q4 in range(4):
                        cp("act", s_sb[:, q4 * 512:(q4 + 1) * 512], psb[q4][:], [PB[q4]], [Bs])
                    for g in range(16):
                        vmax(vtop[:, g, 0:8], s_sb[:, g * 128:(g + 1) * 128], [Bs], [Bv[g]])
                    for g in range(16):
                        mrep(tmpk[:, g, :], vtop[:, g, 0:8], s_sb[:, g * 128:(g + 1) * 128], [Bs, Bv[g]], [Btk_[g]])
                    for g in range(16):
                        vmax(vtop[:, g, 8:16], tmpk[:, g, :], [Btk_[g]], [Bv[g]])
                    v4 = vtop.rearrange("p (h two) k -> p h two k", two=2)
                    c4 = cand.rearrange("p h (i j) -> p h i j", i=16)
                    tt("dve", c4, v4[:, :, 0, :].unsqueeze(3).to_broadcast([128, 8, 16, 16]),
                       v4[:, :, 1, :].unsqueeze(2).to_broadcast([128, 8, 16, 16]), ALU.add, Bv, [Bcand])
                    tmpc = tmpk.rearrange("p (h two) k -> p h (two k)", two=2)
                    for h in range(8):
                        vmax(vals[:, h, 0:8], cand[:, h, :], [Bcand], [Bvals[h]])
                    for h in range(8):
                        mrep(tmpc[:, h, :], vals[:, h, 0:8], cand[:, h, :], [Bcand, Bvals[h]], [Btk_[2 * h], Btk_[2 * h + 1]])
                    for h in range(8):
                        vmax(vals[:, h, 8:16], tmpc[:, h, :], [Btk_[2 * h], Btk_[2 * h + 1]], [Bvals[h]])
                    tt("dve", evx, vals, vals[:, :, 0:1].to_broadcast([128, 8, 16]), ALU.subtract, Bvals, [Bevx])
                    act(evx, evx, AF.Exp, [Bevx], [Bevx])
                    R.op("dve", lambda e: e.reduce_sum(out=zz, in_=evx, axis=AX.X), [Bevx], [Bzz])
                    act(zz, zz, AF.Ln, [Bzz], [Bzz])
                    cp("dve", tk[:, 0:8], vals[:, :, 15], Bvals, [Btkk])
                    tt("dve", zz, zz, vals[:, :, 0], ALU.add, [Bzz] + Bvals, [Bzz])
                    ts("dve", tk[:, 8:16], zz, -1.0, None, ALU.mult, None, [Bzz], [Btkk])
                    rows = slice(b * 512 + tl * 128, b * 512 + (tl + 1) * 128)
                    dma(sS_d[rows, :], s_sb, [Bs], [])
                    dma(sTK_d[rows, :], tk, [Btkk], [])
            R.barrier()

            off[0] = base_off
            nTb = alloc([8, 512], BF16)
            s4 = [alloc([2048]) for _ in range(4)]; tk4 = [alloc([16]) for _ in range(4)]
            cc = [alloc([8]) for _ in range(4)]; Dg = [alloc([8, 128], BF16) for _ in range(4)]
            accO = [alloc([1024]) for _ in range(4)]
            Ub = [alloc([8, 1024], BF16) for _ in range(2)]; Vb = [alloc([8, 1024], BF16) for _ in range(2)]
            gelT = alloc([8, 512], BF16)
            Xh = [alloc([4, 8, 128]) for _ in range(2)]; Eh = [alloc([4, 8, 128], BF16) for _ in range(2)]
            Mh = [alloc([4, 1024], BF16) for _ in range(2)]
            HT = [alloc([8, 128], BF16) for _ in range(2)]
            h1T = Xh[0].rearrange("p a b c -> p (a b c)").rearrange("p (a b) -> p a b", a=8)
            BnT = Buf("nTb"); Bs4 = [Buf("s4%d" % i) for i in range(4)]; BaccO = [Buf("accO%d" % i) for i in range(4)]
            BDg = [Buf("Dg%d" % i) for i in range(4)]
            BUb = [Buf("Ub%d" % i) for i in range(2)]; BVb = [Buf("Vb%d" % i) for i in range(2)]
            BX = [Buf("X%d" % i) for i in range(2)]; BEh = [Buf("Eh%d" % i) for i in range(2)]; BM = [Buf("M%d" % i) for i in range(2)]
            Bgel = Buf("gelT"); BHT = [Buf("HT%d" % i) for i in range(2)]
            Bh1 = BX[0]
            ubf_v = ubf_d[l].rearrange("c p e -> p c e")
            vbf_v = vbf_d[l].rearrange("a p d -> p a d")
            s4v = [t_.rearrange("p (h two k) -> p h two k", two=2, k=128) for t_ in s4]
            for tb in range(NBLK):
                tsl = slice(tb * 512, (tb + 1) * 512)
                dma(nTb, nT_v[:, :, tsl], [], [BnT])
                for tl in range(4):
                    rows = slice(tb * 512 + tl * 128, tb * 512 + (tl + 1) * 128)
                    dma(s4[tl], sS_d[rows, :], [], [Bs4[tl]])
                    dma(tk4[tl], sTK_d[rows, :], [], [Bs4[tl]])
                    mset("pool", accO[tl], 0.0, [BaccO[tl]])
                    tt("dve", s4v[tl][:, :, 0, :], s4v[tl][:, :, 0, :], tk4[tl][:, 0:8].unsqueeze(2).to_broadcast([128, 8, 128]),
                       ALU.subtract, [Bs4[tl]], [Bs4[tl]])
                    tt("dve", cc[tl], tk4[tl][:, 0:8], tk4[tl][:, 8:16], ALU.add, [Bs4[tl]], [BDg[tl]])
                    act(cc[tl], cc[tl], AF.Exp, [BDg[tl]], [BDg[tl]])
                    for h in range(8):
                        ts("pool", Dg[tl][:, h, :], identb, cc[tl][:, h:h + 1], None, ALU.mult, None, [BDg[tl], Bc], [BDg[tl]])

                def ldw(wb):
                    dma(Ub[wb % 2], ubf_v[:, :, wb * 1024:(wb + 1) * 1024], [], [BUb[wb % 2]])
                    dma(Vb[wb % 2], vbf_v[:, wb * 8:(wb + 1) * 8, :], [], [BVb[wb % 2]])

                ldw(0)
                for wb in range(16):
                    if wb + 1 < 16:
                        ldw(wb + 1)
                    U_, V_ = Ub[wb % 2], Vb[wb % 2]
                    for et in range(8):
                        bk = et % 2
                        for k in range(8):
                            mm(psb[bk][:], U_[:, k, et * 128:(et + 1) * 128], nTb[:, k, :], k == 0, k == 7,
                               [BnT, BUb[wb % 2]], [PB[bk]])
                        act(gelT[:, et, :], psb[bk][:], AF.Gelu_apprx_tanh, [PB[bk]], [Bgel])
                    for tl in range(4):
                        csl = slice(tl * 128, (tl + 1) * 128)
                        gb = 2 if tl % 2 == 0 else 6
                        for hh in range(2):
                            xi = hh
                            xeng = "dve" if (tl * 2 + hh) in (2, 5, 7) else "pool"
                            tt(xeng, Xh[xi],
                               s4v[tl][:, 4 * hh:4 * hh + 4, 0, wb * 8:(wb + 1) * 8].unsqueeze(3).to_broadcast([128, 4, 8, 128]),
                               s4v[tl][:, 4 * hh:4 * hh + 4, 1, :].unsqueeze(2).to_broadcast([128, 4, 8, 128]),
                               ALU.add, [Bs4[tl]], [BX[xi]])
                            act(Eh[xi], Xh[xi], AF.Exp, [BX[xi]], [BEh[xi]])
                            stt("dve", Mh[xi], Xh[xi].rearrange("p h a b -> p h (a b)"), 0.0,
                                Eh[xi].rearrange("p h a b -> p h (a b)"), ALU.is_ge, ALU.mult, [BX[xi], BEh[xi]], [BM[xi]])
                            for et in range(8):
                                bk = gb + et // 4
                                for h4 in range(4):
                                    mm(psb[bk][:, (et % 4) * 128:(et % 4 + 1) * 128], Mh[xi][:, h4, et * 128:(et + 1) * 128],
                                       Dg[tl][:, 4 * hh + h4, :], (hh == 0 and h4 == 0 and et % 4 == 0), (hh == 1 and h4 == 3),
                                       [BM[xi], BDg[tl]], [PB[bk]])
                        ht = HT[tl % 2]
                        for q in range(2):
                            tt("dve", ht[:, 4 * q:4 * q + 4, :], gelT[:, 4 * q:4 * q + 4, csl],
                               psb[gb + q][:].rearrange("p (a b) -> p a b", a=4), ALU.mult, [Bgel, PB[gb + q]], [BHT[tl % 2]])
                        for et in range(8):
                            for hf in range(2):
                                mm(psb[4 + hf][:], ht[:, et, :], V_[:, et, hf * 512:(hf + 1) * 512], et == 0, et == 7,
                                   [BHT[tl % 2], BVb[wb % 2]], [PB[4 + hf]])
                        for hf in range(2):
                            tt("dve", accO[tl][:, hf * 512:(hf + 1) * 512], accO[tl][:, hf * 512:(hf + 1) * 512], psb[4 + hf][:],
                               ALU.add, [BaccO[tl], PB[4 + hf]], [BaccO[tl]])
                dma(h1T, h1_v[:, :, tsl], [], [Bh1])
                for tl in range(4):
                    csl = slice(tl * 128, (tl + 1) * 128)
                    for dc in range(8):
                        bk = 6 + (dc // 4) % 2
                        tr(psb[bk][:, (dc % 4) * 128:(dc % 4 + 1) * 128], accO[tl][:, dc * 128:(dc + 1) * 128], ident,
                           [BaccO[tl], Bc], [PB[bk]])
                        if dc % 4 == 3:
                            d0 = dc - 3
                            tt("dve", h1T[:, d0:d0 + 4, csl], h1T[:, d0:d0 + 4, csl],
                               psb[bk][:].rearrange("p (a b) -> p a b", a=4), ALU.add, [Bh1, PB[bk]], [Bh1])
                dma(h2_v[:, :, tsl], h1T, [Bh1], [])
            R.barrier()

            off[0] = base_off
            Wpg = alloc([8, 1024], BF16); Wpe = alloc([2, 1024], BF16); gple = alloc([8]); gfin = alloc([8])
            stg = alloc([1024]); hT = alloc([8, 512]); aT = alloc([8, 512], BF16); sqb = alloc([8, 512], BF16)
            rstd = alloc([512]); T0 = alloc([512]); T1 = alloc([512]); pTf = alloc([2, 512]); pTb = alloc([2, 512], BF16)
            oT = alloc([8, 512]); otok = alloc([1024])
            BW = Buf("Wc"); Bstg = Buf("stgc"); BhT = Buf("hTc"); BaT = Buf("aTc"); Bsq = Buf("sqc"); Brs = Buf("rsc")
            BT0 = Buf("T0c"); BT1 = Buf("T1c"); BpTf = Buf("pTf"); BpTb = Buf("pTb"); BoT = Buf("oT"); Botok = Buf("otok")
            load_w_bf(Wpg, wpg_d[l], 8, 1024, stg, Bstg, BW)
            load_w_bf(Wpe, wpe_d[l], 2, 1024, stg, Bstg, BW)
            dma(gple, gple_d[l], [], [Bc])
            dma(gfin, gfin_d, [], [Bc])
            last = (l == L - 1)
            for b in range(NBLK):
                tsl = slice(b * 512, (b + 1) * 512)
                dma(hT, h2_v[:, :, tsl], [], [BhT])
                dma(pTf, pT_d[l].rearrange("c p s -> p c s")[:, :, tsl], [], [BpTf])
                cp("pool", pTb, pTf, [BpTf], [BpTb])
                rmsnorm_to(hT, gple, aT, BhT, BaT, sqb, Bsq, rstd, Brs, T0, BT0, 7)
                for dc in range(8):
                    b0, b1 = (dc % 2) * 2, (dc % 2) * 2 + 1
                    for k in range(8):
                        mm(psb[b0][:], Wpg[:, k, dc * 128:(dc + 1) * 128], aT[:, k, :], k == 0, k == 7, [BW, BaT], [PB[b0]])
                    for k in range(2):
                        mm(psb[b1][:], Wpe[:, k, dc * 128:(dc + 1) * 128], pTb[:, k, :], k == 0, k == 1, [BW, BpTb], [PB[b1]])
                    act(T1, psb[b0][:], AF.Sigmoid, [PB[b0]], [BT1])
                    tt("dve", T1, T1, psb[b1][:], ALU.mult, [BT1, PB[b1]], [BT1])
                    tt("dve", hT[:, dc, :], hT[:, dc, :], T1, ALU.add, [BhT, BT1], [BhT])
                if not last:
                    dma(hcur_v[:, :, tsl], hT, [BhT], [])
                else:
                    if dbg:
                        dma(hcur_v[:, :, tsl], hT, [BhT], [])
                    for c in range(8):
                        act(sqb[:, c, :], hT[:, c, :], AF.Square, [BhT], [Bsq])
                    for c in range(8):
                        mm(psb[7][:], onesb, sqb[:, c, :], c == 0, c == 7, [Bsq, Bc], [PB[7]])
                    act(T0, psb[7][:], AF.Sqrt, [PB[7], Bc], [BT0], bias=epst[:, 0:1])
                    recip(rstd, T0, [BT0], [Brs])
                    for c in range(8):
                        stt("dve", oT[:, c, :], hT[:, c, :], gfin[:, c:c + 1], rstd, ALU.mult, ALU.mult,
                            [BhT, Brs, Bc], [BoT])
                    for tl in range(4):
                        csl = slice(tl * 128, (tl + 1) * 128)
                        for dc in range(8):
                            bk = 4 + (dc // 4)
                            tr(psb[bk][:, (dc % 4) * 128:(dc % 4 + 1) * 128], oT[:, dc, csl], ident, [BoT, Bc], [PB[bk]])
                        cp("act", otok[:, 0:512], psb[4][:], [PB[4]], [Botok])
                        cp("dve", otok[:, 512:1024], psb[5][:], [PB[5]], [Botok])
                        dma(out_d[b * 512 + tl * 128:b * 512 + (tl + 1) * 128, :], otok, [Botok], [])
            R.barrier()
        R.emit()
    return nc, R


def _cols(v, n):
    L_ = v.shape[0]
    return np.ascontiguousarray(v.reshape(L_, n, 128).transpose(0, 2, 1))


def prep_shared(inp, L):
    f = lambda a: np.ascontiguousarray(np.asarray(a, dtype=np.float32))
    sh = {}
    sh["g_mix"] = _cols(f(inp["g_mix"])[:L], 8); sh["g_ffn"] = _cols(f(inp["g_ffn"])[:L], 8); sh["g_ple"] = _cols(f(inp["g_ple"])[:L], 8)
    sh["g_final"] = _cols(f(inp["g_final"])[None], 8)[0]
    for k in ("w_in", "w_out", "w_q", "w_pg", "w_pe", "expert_v"):
        sh[k] = f(inp[k])[:L]
    caw = f(inp["conv_a_w"])[:L]
    sh["conv_a_w"] = np.ascontiguousarray(caw.reshape(L, 31, 3, 128).transpose(0, 3, 2, 1))
    for k in ("conv_a_b", "ln_a_g", "ln_a_b", "ln_c_g", "ln_c_b"):
        sh[k] = _cols(f(inp[k])[:L], 3)
    cbw = f(inp["conv_b_w"])[:L]
    sh["conv_b_w"] = np.ascontiguousarray(cbw.reshape(L, 3, 2, 128).transpose(0, 3, 2, 1))
    ws = f(inp["w_s"])[:L]
    sh["w_sT"] = np.ascontiguousarray(ws.reshape(L, 3, 2, 128, 128).transpose(0, 4, 1, 2, 3).reshape(L, 128, 3, 256))
    bs = f(inp["b_s"])[:L]
    bsr = bs.reshape(L, 3, 2, 128)
    sh["b_sb"] = np.ascontiguousarray(np.repeat(bsr.transpose(0, 2, 1, 3), 64, axis=1))
    sk = f(inp["sub_keys"])[:L]
    sh["keysT"] = np.ascontiguousarray(sk.reshape(L, 16, 128, 128).transpose(0, 3, 1, 2))
    sh["expert_uT"] = np.ascontiguousarray(f(inp["expert_u"])[:L].transpose(0, 2, 1))
    return sh


_CACHE = {}


def kernel(**inputs):
    S, L = SEQ, DEPTH
    key = (S, L)
    if key not in _CACHE:
        _CACHE[key] = build(S, L)[0]
    nc = _CACHE[key]
    sh = prep_shared(inputs, L)
    x = np.asarray(inputs["x"], dtype=np.float32)
    p = np.asarray(inputs["p"], dtype=np.float32)
    in_maps = []
    for c in range(NCORES):
        m = dict(sh)
        m["x"] = np.ascontiguousarray(x[c])
        m["p"] = np.ascontiguousarray(p[:, c])
        in_maps.append(m)
    res = run_bass_kernel_spmd(nc, in_maps, core_ids=list(range(NCORES)))
    return np.stack([np.asarray(r["out"], dtype=np.float32) for r in res.results], axis=0)
```

```python
import contextlib
import numpy as np
import concourse.bass as bass
import concourse.mybir as mybir
from concourse.bass_utils import run_bass_kernel_spmd

F32 = mybir.dt.float32
BF16 = mybir.dt.bfloat16
AF = mybir.ActivationFunctionType
ALU = mybir.AluOpType
AX = mybir.AxisListType

D = 1024
NCORES = 8
SEQ = 8192
DEPTH = 2
EPS = 1e-6
NEG = -1.0e30
ENGS = ("pe", "act", "dve", "pool", "sp")


class Buf:
    __slots__ = ("name", "last_w", "readers", "dma_readers")

    def __init__(self, name):
        self.name = name
        self.last_w = None
        self.readers = {}
        self.dma_readers = []


class Op:
    __slots__ = ("eng", "fn", "deps", "signal", "dma", "semkey", "semval", "idx")

    def __init__(self, eng, fn, dma):
        self.eng = eng
        self.fn = fn
        self.deps = []
        self.signal = False
        self.dma = dma
        self.semkey = None
        self.semval = None


class Rec:
    NDMA = 24

    def __init__(self, nc):
        self.nc = nc
        self.ops = {e: [] for e in ENGS}
        self.last_real = {e: None for e in ENGS}
        self.ndma = 0
        self.dma_ops = []

    def op(self, eng, fn, reads=(), writes=(), dma=False):
        o = Op(eng, fn, dma)
        o.idx = len(self.ops[eng])
        deps = []
        for b in reads:
            if b.last_w is not None:
                deps.append(b.last_w)
        for b in writes:
            if b.last_w is not None:
                deps.append(b.last_w)
            deps.extend(b.readers.values())
            deps.extend(b.dma_readers)
        latest = {}
        dmas = []
        seen = set()
        for d in deps:
            if d is o or id(d) in seen:
                continue
            seen.add(id(d))
            if d.dma:
                dmas.append(d)
            else:
                if d.eng == "pe" and eng == "pe" and not dma:
                    continue
                if d.eng not in latest or latest[d.eng].idx < d.idx:
                    latest[d.eng] = d
        for d in list(latest.values()) + dmas:
            o.deps.append(d)
            d.signal = True
        if dma:
            o.signal = True
            k = self.ndma
            self.ndma += 1
            o.semkey = ("dma", k % self.NDMA)
            o.semval = 16 * (k // self.NDMA + 1)
            if k >= self.NDMA:
                o.deps.append(self.dma_ops[k - self.NDMA])
            self.dma_ops.append(o)
        for b in reads:
            if dma:
                b.dma_readers.append(o)
            else:
                b.readers[eng] = o
        for b in writes:
            b.last_w = o
            b.readers = {}
            b.dma_readers = []
        self.ops[eng].append(o)
        self.last_real[eng] = o
        return o

    def barrier(self):
        lasts = [o for o in self.last_real.values() if o is not None]
        lasts += self.dma_ops[-self.NDMA:]
        for d in lasts:
            d.signal = True
        for e in ENGS:
            o = Op(e, None, False)
            seen = set()
            for d in lasts:
                if id(d) in seen:
                    continue
                seen.add(id(d))
                o.deps.append(d)
            self.ops[e].append(o)

    EPOCH = 6000

    def emit(self):
        nc = self.nc
        nep = {}
        for e in ENGS:
            cnt = 0
            for o in self.ops[e]:
                if o.dma or o.fn is None:
                    continue
                if o.signal:
                    o.semkey = ("eng", e, cnt // self.EPOCH)
                    o.semval = cnt % self.EPOCH + 1
                    cnt += 1
            nep[e] = cnt // self.EPOCH + 1
        with contextlib.ExitStack() as st:
            sems = {}
            for e in ENGS:
                for ep in range(nep[e]):
                    sems[("eng", e, ep)] = st.enter_context(nc.semaphore("s_%s%d" % (e, ep)))
            for i in range(self.NDMA):
                sems[("dma", i)] = st.enter_context(nc.semaphore("s_dma%d" % i))
            block = st.enter_context(nc.Block())

            def run(e, eng):
                waited = {}
                for o in self.ops[e]:
                    for d in o.deps:
                        if waited.get(d.semkey, 0) >= d.semval:
                            continue
                        waited[d.semkey] = d.semval
                        eng.wait_ge(sems[d.semkey], d.semval)
                    if o.fn is None:
                        continue
                    ins = o.fn(eng)
                    if o.signal:
                        ins.then_inc(sems[o.semkey], 16 if o.dma else 1)

            block.tensor(lambda eng: run("pe", eng))
            block.scalar(lambda eng: run("act", eng))
            block.vector(lambda eng: run("dve", eng))
            block.gpsimd(lambda eng: run("pool", eng))
            block.sync(lambda eng: run("sp", eng))


def build(S, L, dbg=False):
    assert S % 512 == 0
    NBLK = S // 512
    nc = bass.Bass("TRN2", target_bir_lowering=False)

    def din(name, shape, dt=F32):
        return nc.dram_tensor(name, list(shape), dt, kind="ExternalInput").ap()

    def dscr(name, shape, dt=F32, out=False):
        return nc.dram_tensor(name, list(shape), dt, kind=("ExternalOutput" if out else "Internal")).ap()

    x_d = din("x", [S, D])
    p_d = din("p", [L, S, 256])
    gmix_d = din("g_mix", [L, 128, 8]); gffn_d = din("g_ffn", [L, 128, 8]); gple_d = din("g_ple", [L, 128, 8])
    gfin_d = din("g_final", [128, 8])
    win_d = din("w_in", [L, D, 2304]); wout_d = din("w_out", [L, D, D]); wq_d = din("w_q", [L, D, 2048])
    wpg_d = din("w_pg", [L, D, D]); wpe_d = din("w_pe", [L, 256, D])
    caw_d = din("conv_a_w", [L, 128, 3, 31]); cab_d = din("conv_a_b", [L, 128, 3])
    lag_d = din("ln_a_g", [L, 128, 3]); lab_d = din("ln_a_b", [L, 128, 3])
    cbw_d = din("conv_b_w", [L, 128, 2, 3])
    lcg_d = din("ln_c_g", [L, 128, 3]); lcb_d = din("ln_c_b", [L, 128, 3])
    wst_d = din("w_sT", [L, 128, 3, 256]); bsb_d = din("b_sb", [L, 128, 3, 128])
    keys_d = din("keysT", [L, 128, 16, 128])
    eu_d = din("expert_uT", [L, D, 16384]); ev_d = din("expert_v", [L, 16384, D])
    out_d = nc.dram_tensor("out", [S, D], F32, kind="ExternalOutput").ap()

    hcur_d = dscr("hcur", [8, 128, S], out=dbg)
    h1_d = dscr("h1", [8, 128, S], out=dbg)
    h2_d = dscr("h2", [8, 128, S], out=dbg)
    pT_d = dscr("pT", [L, 2, 128, S])
    nT_d = dscr("nT", [8, 128, S], BF16)
    sS_d = dscr("sS", [S, 2048], out=dbg)
    sTK_d = dscr("sTK", [S, 16], out=dbg)
    ubf_d = dscr("ubf", [L, 8, 128, 16384], BF16)
    vbf_d = dscr("vbf", [L, 128, 128, D], BF16)

    st = contextlib.ExitStack()
    with st:
        CAP = 206 * 1024 + 512
        arena_t = st.enter_context(nc.sbuf_tensor("arena", [128, CAP // 4], F32))
        psb = [st.enter_context(nc.psum_tensor("pb%d" % i, [128, 512], F32)) for i in range(8)]
        PB = [Buf("pb%d" % i) for i in range(8)]
        R = Rec(nc)
        off = [0]

        def alloc(shape, dt=F32):
            n = int(np.prod(shape)) * (2 if dt == BF16 else 4)
            n = (n + 31) // 32 * 32
            a = arena_t[:, off[0] // 4:(off[0] + n) // 4]
            off[0] += n
            assert off[0] <= CAP, ("SBUF arena overflow", off[0])
            if dt == BF16:
                a = a.bitcast(BF16)
            a = a[:, 0:int(np.prod(shape))]
            if len(shape) == 2:
                a = a.rearrange("p (a b) -> p a b", a=shape[0])
            elif len(shape) == 3:
                a = a.rearrange("p (a b c) -> p a b c", a=shape[0], b=shape[1])
            return a

        def mm(out, lhsT, rhs, start, stop, rd, wr):
            return R.op("pe", lambda e: e.matmul(out, lhsT=lhsT, rhs=rhs, start=start, stop=stop), rd, wr)

        def tr(out, in_, ident, rd, wr):
            return R.op("pe", lambda e: e.transpose(out=out, in_=in_, identity=ident), rd, wr)

        def act(out, in_, func, rd, wr, scale=None, bias=None):
            kw = {}
            if scale is not None:
                kw["scale"] = scale
            if bias is not None:
                kw["bias"] = bias
            return R.op("act", lambda e: e.activation(out=out, in_=in_, func=func, **kw), rd, wr)

        def tt(eng, out, in0, in1, op, rd, wr):
            return R.op(eng, lambda e: e.tensor_tensor(out=out, in0=in0, in1=in1, op=op), rd, wr)

        def ts(eng, out, in0, s1, s2, op0, op1, rd, wr):
            if s2 is None:
                return R.op(eng, lambda e: e.tensor_scalar(out=out, in0=in0, scalar1=s1, scalar2=None, op0=op0), rd, wr)
            return R.op(eng, lambda e: e.tensor_scalar(out=out, in0=in0, scalar1=s1, scalar2=s2, op0=op0, op1=op1), rd, wr)

        def stt(eng, out, in0, scalar, in1, op0, op1, rd, wr):
            return R.op(eng, lambda e: e.scalar_tensor_tensor(out=out, in0=in0, scalar=scalar, in1=in1, op0=op0, op1=op1), rd, wr)

        def cp(eng, out, in_, rd, wr):
            if eng == "act":
                return R.op("act", lambda e: e.copy(out=out, in_=in_), rd, wr)
            return R.op(eng, lambda e: e.tensor_copy(out=out, in_=in_), rd, wr)

        def mset(eng, ap, val, wr):
            return R.op(eng, lambda e: e.memset(ap, val), (), wr)

        def dma(out, in_, rd, wr):
            return R.op("sp", lambda e: e.dma_start(out=out, in_=in_), rd, wr, dma=True)

        def recip(out, in_, rd, wr):
            return R.op("dve", lambda e: e.reciprocal(out=out, in_=in_), rd, wr)

        def vmax(out, in_, rd, wr):
            return R.op("dve", lambda e: e.max(out=out, in_=in_), rd, wr)

        def mrep(out, rep, vals, rd, wr):
            return R.op("dve", lambda e: e.match_replace(out=out, in_to_replace=rep, in_values=vals, imm_value=NEG), rd, wr)

        def pbf(i):
            return psb[i][:].bitcast(BF16)

        ident = alloc([128]); identb = alloc([128], BF16); onesb = alloc([128], BF16); blk = alloc([128])
        epst = alloc([1])
        Bc = Buf("consts")
        mset("pool", ident, 0.0, [Bc])
        R.op("pool", lambda e: e.affine_select(out=ident, in_=ident, pattern=[[-1, 128]], compare_op=ALU.not_equal,
                                               fill=1.0, base=0, channel_multiplier=1), [Bc], [Bc])
        cp("dve", identb, ident, [Bc], [Bc])
        mset("pool", onesb, 1.0 / 1024.0, [Bc])
        mset("pool", blk, 0.0, [Bc])
        mset("pool", blk[0:64, 0:64], 1.0 / 64.0, [Bc])
        mset("pool", blk[64:128, 64:128], 1.0 / 64.0, [Bc])
        mset("pool", epst, EPS, [Bc])
        base_off = off[0]

        xs = alloc([1024]); xst = alloc([8, 128]); cst = alloc([2048]); cstb = alloc([2048], BF16)
        pst = alloc([2, 128])
        Bxs, Bxst, Bcst, Bcstb, Bpst = Buf("xs"), Buf("xst"), Buf("cst"), Buf("cstb"), Buf("pst")
        hcur_v = hcur_d.rearrange("c p s -> p c s")
        h1_v = h1_d.rearrange("c p s -> p c s")
        h2_v = h2_d.rearrange("c p s -> p c s")
        nT_v = nT_d.rearrange("c p s -> p c s")
        for tl in range(S // 128):
            ts_ = slice(tl * 128, (tl + 1) * 128)
            dma(xs, x_d[ts_, :], [], [Bxs])
            for c in range(8):
                bk = c // 4
                tr(psb[bk][:, (c % 4) * 128:(c % 4 + 1) * 128], xs[:, c * 128:(c + 1) * 128], ident, [Bxs, Bc], [PB[bk]])
            cp("act", xst[:, 0:4, :], psb[0][:].rearrange("p (a b) -> p a b", a=4), [PB[0]], [Bxst])
            cp("dve", xst[:, 4:8, :], psb[1][:].rearrange("p (a b) -> p a b", a=4), [PB[1]], [Bxst])
            dma(hcur_v[:, :, ts_], xst, [Bxst], [])
            for l in range(L):
                dma(xs[:, 0:256], p_d[l, ts_, :], [], [Bxs])
                for c in range(2):
                    tr(psb[2][:, c * 128:(c + 1) * 128], xs[:, c * 128:(c + 1) * 128], ident, [Bxs, Bc], [PB[2]])
                cp("act", pst, psb[2][:, 0:256].rearrange("p (a b) -> p a b", a=2), [PB[2]], [Bpst])
                dma(pT_d[l].rearrange("c p s -> p c s")[:, :, ts_], pst, [Bpst], [])
        it = 0
        for l in range(L):
            for c in range(8):
                for cb in range(8):
                    dma(cst, eu_d[l, c * 128:(c + 1) * 128, cb * 2048:(cb + 1) * 2048], [], [Bcst])
                    cp("dve" if it % 2 == 0 else "act", cstb, cst, [Bcst], [Bcstb])
                    dma(ubf_d[l, c, :, cb * 2048:(cb + 1) * 2048], cstb, [Bcstb], [])
                    it += 1
            for et in range(0, 128, 2):
                dma(cst.rearrange("p (a b) -> p a b", a=2),
                    ev_d[l, et * 128:(et + 2) * 128, :].rearrange("(a p) d -> p a d", a=2), [], [Bcst])
                cp("dve" if it % 2 == 0 else "act", cstb, cst, [Bcst], [Bcstb])
                dma(vbf_d[l, et:et + 2].rearrange("a p d -> p a d"), cstb.rearrange("p (a b) -> p a b", a=2), [Bcstb], [])
                it += 1
        R.barrier()

        def rmsnorm_to(hT, g, outT, BhT, BoutT, sqb, Bsq, rstd, Brstd, t0buf, Bt0, bank, out_f32=False):
            for c in range(8):
                if c % 2 == 0:
                    act(sqb[:, c, :], hT[:, c, :], AF.Square, [BhT], [Bsq])
                else:
                    tt("pool", sqb[:, c, :], hT[:, c, :], hT[:, c, :], ALU.mult, [BhT], [Bsq])
            for c in range(8):
                mm(psb[bank][:], onesb, sqb[:, c, :], c == 0, c == 7, [Bsq, Bc], [PB[bank]])
            act(t0buf, psb[bank][:], AF.Sqrt, [PB[bank], Bc], [Bt0], bias=epst[:, 0:1])
            recip(rstd, t0buf, [Bt0], [Brstd])
            for c in range(8):
                stt("dve", outT[:, c, :], hT[:, c, :], g[:, c:c + 1], rstd, ALU.mult, ALU.mult,
                    [BhT, Brstd, Bc], [BoutT])

        def load_w_bf(dst, src_d, nk, ncol, stage, Bst, Bdst):
            for k in range(nk):
                dma(stage[:, 0:ncol], src_d[k * 128:(k + 1) * 128, :], [], [Bst])
                cp("dve" if k % 2 == 0 else "act", dst[:, k, :], stage[:, 0:ncol], [Bst], [Bdst])

        for l in range(L):
            off[0] = base_off
            Win = alloc([8, 2304], BF16); Wout = alloc([8, 1024], BF16); Wq = alloc([8, 2048], BF16)
            keysb = alloc([16, 128], BF16); wsT = alloc([3, 256], BF16)
            gmix = alloc([8]); gffn = alloc([8]); caw = alloc([3, 31]); cab = alloc([3]); lag = alloc([3]); lab = alloc([3])
            cbw = alloc([2, 3]); lcg = alloc([3]); lcb = alloc([3]); bsb = alloc([3, 128])
            hT = alloc([8, 512]); aT = alloc([8, 512], BF16); ymix = alloc([8, 512], BF16)
            rstd = alloc([512]); T = [alloc([512]) for _ in range(6)]
            ybuf = [alloc([544]) for _ in range(3)]; ubuf = [alloc([516]) for _ in range(2)]
            uC = alloc([3, 512]); vln = alloc([3, 512], BF16); vlnT = alloc([384], BF16)
            qT = alloc([16, 512], BF16)
            s_sb = alloc([2048]); vtop = alloc([16, 16]); tmpk = alloc([16, 128]); cand = alloc([8, 256]); vals = alloc([8, 16])
            evx = alloc([8, 16]); zz = alloc([8]); tk = alloc([16])
            stage = qT.bitcast(F32) if False else None
            BW = Buf("W"); BhT = Buf("hT"); BaT = Buf("aT"); Bym = Buf("ymix"); Brs = Buf("rstd")
            BT = [Buf("T%d" % i) for i in range(6)]
            Byb = [Buf("yb%d" % i) for i in range(3)]; Bub = [Buf("ub%d" % i) for i in range(2)]
            BuC = Buf("uC"); Bvln = Buf("vln"); BvlnT = Buf("vlnT"); BqT = Buf("qT")
            Bs = Buf("s_sb"); Bv = [Buf("vtop%d" % i) for i in range(16)]; Btk_ = [Buf("tmpk%d" % i) for i in range(16)]; Bcand = Buf("cand"); Bvals = [Buf("vals%d" % i) for i in range(8)]
            Bevx = Buf("evx"); Bzz = Buf("zz"); Btkk = Buf("tk")
            stg = s_sb
            Bstg = Bs
            for k in range(8):
                for hf in range(2):
                    ncol = 1152
                    dma(stg[:, 0:ncol], win_d[l, k * 128:(k + 1) * 128, hf * ncol:(hf + 1) * ncol], [], [Bstg])
                    cp("dve" if hf == 0 else "act", Win[:, k, hf * ncol:(hf + 1) * ncol], stg[:, 0:ncol], [Bstg], [BW])
            load_w_bf(Wout, wout_d[l], 8, 1024, stg, Bstg, BW)
            load_w_bf(Wq, wq_d[l], 8, 2048, stg, Bstg, BW)
            dma(stg.rearrange("p (a b) -> p a b", a=16), keys_d[l], [], [Bstg])
            cp("dve", keysb, stg.rearrange("p (a b) -> p a b", a=16), [Bstg], [BW])
            dma(stg[:, 0:768].rearrange("p (a b) -> p a b", a=3), wst_d[l], [], [Bstg])
            cp("dve", wsT, stg[:, 0:768].rearrange("p (a b) -> p a b", a=3), [Bstg], [BW])
            for c in range(3):
                for hh in range(2):
                    mset("pool", wsT[64:128, c, hh * 128:hh * 128 + 64], 0.0, [BW])
            for (dst, src) in [(gmix, gmix_d[l]), (gffn, gffn_d[l]), (caw, caw_d[l]), (cab, cab_d[l]), (lag, lag_d[l]),
                               (lab, lab_d[l]), (cbw, cbw_d[l]), (lcg, lcg_d[l]), (lcb, lcb_d[l]), (bsb, bsb_d[l])]:
                dma(dst, src, [], [Bc])
            for j in range(3):
                mset("pool", ybuf[j][:, 0:32], 0.0, [Byb[j]])
            for j in range(2):
                mset("pool", ubuf[j][:, 0:4], 0.0, [Bub[j]])

            def zchunk(j, bank):
                for k in range(8):
                    mm(psb[bank][:], Win[:, k, j * 128:(j + 1) * 128], aT[:, k, :], k == 0, k == 7, [BW, BaT], [PB[bank]])

            def gln(src, Bsrc, g, b, j, func, dst, Bdst):
                mm(psb[2][:], blk, src, True, True, [Bsrc, Bc], [PB[2]])
                tt("dve", T[2], src, psb[2][:], ALU.subtract, [Bsrc, PB[2]], [BT[2]])
                act(T[3], T[2], AF.Square, [BT[2]], [BT[3]])
                mm(psb[3][:], blk, T[3], True, True, [BT[3], Bc], [PB[3]])
                act(T[4], psb[3][:], AF.Sqrt, [PB[3], Bc], [BT[4]], bias=epst[:, 0:1])
                recip(T[4], T[4], [BT[4]], [BT[4]])
                tt("dve", T[2], T[2], T[4], ALU.mult, [BT[2], BT[4]], [BT[2]])
                act(dst, T[2], func, [BT[2], Bc], [Bdst], scale=g[:, j:j + 1], bias=b[:, j:j + 1])

            for b in range(NBLK):
                tsl = slice(b * 512, (b + 1) * 512)
                dma(hT, hcur_v[:, :, tsl], [], [BhT])
                rmsnorm_to(hT, gmix, aT, BhT, BaT, ymix, Bym, rstd, Brs, T[0], BT[0], 7)
                for j in range(3):
                    zchunk(j, 0)
                    zchunk(3 + j, 1)
                    act(T[1], psb[1][:], AF.Sigmoid, [PB[1]], [BT[1]])
                    tt("dve", ybuf[j][:, 32:544], psb[0][:], T[1], ALU.mult, [PB[0], BT[1]], [Byb[j]])
                    ts("dve", T[5], ybuf[j][:, 2:514], caw[:, j, 0:1], cab[:, j:j + 1], ALU.mult, ALU.add, [Byb[j], Bc], [BT[5]])
                    for k in range(1, 31):
                        stt("dve", T[5], ybuf[j][:, 2 + k:514 + k], caw[:, j, k:k + 1], T[5], ALU.mult, ALU.add, [Byb[j], Bc, BT[5]], [BT[5]])
                    cp("pool", ybuf[j][:, 0:32], ybuf[j][:, 512:544], [Byb[j]], [Byb[j]])
                    gln(T[5], BT[5], lag, lab, j, AF.Silu, ymix[:, j, :], Bym)
                for j in range(2):
                    zchunk(8 + j, 0)
                    zchunk(10 + j, 1)
                    cp("act", T[1], psb[0][:], [PB[0]], [BT[1]])
                    tt("dve", ubuf[j][:, 4:516], psb[1][:], T[1], ALU.mult, [PB[1], BT[1]], [Bub[j]])
                    ts("dve", T[5], ubuf[j][:, 2:514], cbw[:, j, 0:1], None, ALU.mult, None, [Bub[j], Bc], [BT[5]])
                    for k in range(1, 3):
                        stt("dve", T[5], ubuf[j][:, 2 + k:514 + k], cbw[:, j, k:k + 1], T[5], ALU.mult, ALU.add, [Bub[j], Bc, BT[5]], [BT[5]])
                    cp("pool", ubuf[j][:, 0:4], ubuf[j][:, 512:516], [Bub[j]], [Bub[j]])
                    zchunk(6 + j, 0)
                    tt("dve", ymix[:, 3 + j, :], psb[0][:], T[5], ALU.mult, [PB[0], BT[5]], [Bym])
                for j in range(3):
                    zchunk(15 + j, 0)
                    cp("act", T[5], psb[0][:], [PB[0]], [BT[5]])
                    gln(T[5], BT[5], lcg, lcb, j, AF.Identity, vln[:, j, :], Bvln)
                    zchunk(12 + j, 1)
                    cp("act", uC[:, j, :], psb[1][:], [PB[1]], [BuC])
                for sbk in range(4):
                    csl = slice(sbk * 128, (sbk + 1) * 128)
                    for j in range(3):
                        tr(pbf(4)[:, j * 128:(j + 1) * 128], vln[:, j, csl], identb, [Bvln, Bc], [PB[4]])
                    cp("act", vlnT, pbf(4)[:, 0:384], [PB[4]], [BvlnT])
                    for j in range(3):
                        mm(psb[5][:, 0:256], vlnT[:, j * 128:(j + 1) * 128], wsT[:, j, :], True, True, [BvlnT, BW], [PB[5]])
                        for hh in range(2):
                            rs = slice(hh * 64, (hh + 1) * 64)
                            tt("dve", T[1][rs, 0:128], psb[5][rs, hh * 128:(hh + 1) * 128], bsb[rs, j, :], ALU.add, [PB[5], Bc], [BT[1]])
                            tt("dve", ymix[rs, 5 + j, csl], T[1][rs, 0:128], uC[rs, j, csl], ALU.mult, [BT[1], BuC], [Bym])
                for dc in range(8):
                    bk = 6 + dc % 2
                    for k in range(8):
                        mm(psb[bk][:], Wout[:, k, dc * 128:(dc + 1) * 128], ymix[:, k, :], k == 0, k == 7, [BW, Bym], [PB[bk]])
                    tt("dve", hT[:, dc, :], hT[:, dc, :], psb[bk][:], ALU.add, [BhT, PB[bk]], [BhT])
                dma(h1_v[:, :, tsl], hT, [BhT], [])
                rmsnorm_to(hT, gffn, aT, BhT, BaT, ymix, Bym, rstd, Brs, T[0], BT[0], 7)
                dma(nT_v[:, :, tsl], aT, [BaT], [])
                for qc in range(16):
                    bk = 6 + qc % 2
                    for k in range(8):
                        mm(psb[bk][:], Wq[:, k, qc * 128:(qc + 1) * 128], aT[:, k, :], k == 0, k == 7, [BW, BaT], [PB[bk]])
                    cp("act" if qc % 2 == 0 else "dve", qT[:, qc, :], psb[bk][:], [PB[bk]], [BqT])
                for tl in range(4):
                    csl = slice(tl * 128, (tl + 1) * 128)
                    for g in range(16):
                        mm(psb[g // 4][:, (g % 4) * 128:(g % 4 + 1) * 128], qT[:, g, csl], keysb[:, g, :], True, True,
                           [BqT, BW], [PB[g // 4]])
                    for q4 in range(4):
                        cp("act", s_sb[:, q4 * 512:(q4 + 1) * 512], psb[q4][:], [PB[q4]], [Bs])
                    for g in range(16):
                        vmax(vtop[:, g, 0:8], s_sb[:, g * 128:(g + 1) * 128], [Bs], [Bv[g]])
                    for g in range(16):
                        mrep(tmpk[:, g, :], vtop[:, g, 0:8], s_sb[:, g * 128:(g + 1) * 128], [Bs, Bv[g]], [Btk_[g]])
                    for g in range(16):
                        vmax(vtop[:, g, 8:16], tmpk[:, g, :], [Btk_[g]], [Bv[g]])
                    v4 = vtop.rearrange("p (h two) k -> p h two k", two=2)
                    c4 = cand.rearrange("p h (i j) -> p h i j", i=16)
                    tt("dve", c4, v4[:, :, 0, :].unsqueeze(3).to_broadcast([128, 8, 16, 16]),
                       v4[:, :, 1, :].unsqueeze(2).to_broadcast([128, 8, 16, 16]), ALU.add, Bv, [Bcand])
                    tmpc = tmpk.rearrange("p (h two) k -> p h (two k)", two=2)
                    for h in range(8):
                        vmax(vals[:, h, 0:8], cand[:, h, :], [Bcand], [Bvals[h]])
                    for h in range(8):
                        mrep(tmpc[:, h, :], vals[:, h, 0:8], cand[:, h, :], [Bcand, Bvals[h]], [Btk_[2 * h], Btk_[2 * h + 1]])
                    for h in range(8):
                        vmax(vals[:, h, 8:16], tmpc[:, h, :], [Btk_[2 * h], Btk_[2 * h + 1]], [Bvals[h]])
                    tt("dve", evx, vals, vals[:, :, 0:1].to_broadcast([128, 8, 16]), ALU.subtract, Bvals, [Bevx])
                    act(evx, evx, AF.Exp, [Bevx], [Bevx])
                    R.op("dve", lambda e: e.reduce_sum(out=zz, in_=evx, axis=AX.X), [Bevx], [Bzz])
                    act(zz, zz, AF.Ln, [Bzz], [Bzz])
                    ts("dve", tk[:, 0:8], vals[:, :, 15], -1.0e-5, None, ALU.add, None, Bvals, [Btkk])
                    tt("dve", zz, zz, vals[:, :, 0], ALU.add, [Bzz] + Bvals, [Bzz])
                    ts("dve", tk[:, 8:16], zz, -1.0, None, ALU.mult, None, [Bzz], [Btkk])
                    rows = slice(b * 512 + tl * 128, b * 512 + (tl + 1) * 128)
                    dma(sS_d[rows, :], s_sb, [Bs], [])
                    dma(sTK_d[rows, :], tk, [Btkk], [])
            R.barrier()

            off[0] = base_off
            NWB = 32
            nTb = alloc([8, 512], BF16)
            s4 = [alloc([2048]) for _ in range(4)]; tk4 = [alloc([16]) for _ in range(4)]
            cc = [alloc([8]) for _ in range(4)]; Dg = [alloc([8, 128], BF16) for _ in range(4)]
            accO = [alloc([1024]) for _ in range(4)]
            Ub = [alloc([8, 512], BF16) for _ in range(2)]; Vb = [alloc([4, 1024], BF16) for _ in range(2)]
            gelT = [alloc([4, 512], BF16) for _ in range(2)]
            Xb = [alloc([8, 4, 128]) for _ in range(3)]; Eb = [alloc([8, 4, 128], BF16) for _ in range(2)]
            Mb = [alloc([8, 512], BF16) for _ in range(2)]
            HT = [alloc([4, 128], BF16) for _ in range(2)]
            h1T = Xb[0].rearrange("p a b c -> p (a b c)").rearrange("p (a b) -> p a b", a=8)
            BnT = Buf("nTb"); Bs4 = [Buf("s4%d" % i) for i in range(4)]; BaccO = [Buf("accO%d" % i) for i in range(4)]
            BDg = [Buf("Dg%d" % i) for i in range(4)]
            BUb = [Buf("Ub%d" % i) for i in range(2)]; BVb = [Buf("Vb%d" % i) for i in range(2)]
            BX = [Buf("X%d" % i) for i in range(3)]; BEb = [Buf("Eb%d" % i) for i in range(2)]; BM = [Buf("M%d" % i) for i in range(2)]
            Bgel = [Buf("gelT%d" % i) for i in range(2)]; BHT = [Buf("HT%d" % i) for i in range(2)]
            Bh1 = BX[0]
            ubf_v = ubf_d[l].rearrange("c p e -> p c e")
            vbf_v = vbf_d[l].rearrange("a p d -> p a d")
            s4v = [t_.rearrange("p (h two k) -> p h two k", two=2, k=128) for t_ in s4]
            for tb in range(NBLK):
                tsl = slice(tb * 512, (tb + 1) * 512)
                dma(nTb, nT_v[:, :, tsl], [], [BnT])
                for tl in range(4):
                    rows = slice(tb * 512 + tl * 128, tb * 512 + (tl + 1) * 128)
                    dma(s4[tl], sS_d[rows, :], [], [Bs4[tl]])
                    dma(tk4[tl], sTK_d[rows, :], [], [Bs4[tl]])
                    mset("pool", accO[tl], 0.0, [BaccO[tl]])
                    tt("dve", s4v[tl][:, :, 0, :], s4v[tl][:, :, 0, :], tk4[tl][:, 0:8].unsqueeze(2).to_broadcast([128, 8, 128]),
                       ALU.subtract, [Bs4[tl]], [Bs4[tl]])
                    tt("dve", cc[tl], tk4[tl][:, 0:8], tk4[tl][:, 8:16], ALU.add, [Bs4[tl]], [BDg[tl]])
                    act(cc[tl], cc[tl], AF.Exp, [BDg[tl]], [BDg[tl]])
                    for h in range(8):
                        ts("pool", Dg[tl][:, h, :], identb, cc[tl][:, h:h + 1], None, ALU.mult, None, [BDg[tl], Bc], [BDg[tl]])

                steps = [(wb, tl) for wb in range(NWB) for tl in range(4)]
                G_ = len(steps)

                def ldw(wb):
                    dma(Ub[wb % 2], ubf_v[:, :, wb * 512:(wb + 1) * 512], [], [BUb[wb % 2]])
                    dma(Vb[wb % 2], vbf_v[:, wb * 4:(wb + 1) * 4, :], [], [BVb[wb % 2]])

                def stA(wb):
                    for et in range(4):
                        bk = et % 2
                        for k in range(8):
                            mm(psb[bk][:], Ub[wb % 2][:, k, et * 128:(et + 1) * 128], nTb[:, k, :], k == 0, k == 7,
                               [BnT, BUb[wb % 2]], [PB[bk]])
                        act(gelT[wb % 2][:, et, :], psb[bk][:], AF.Gelu_apprx_tanh, [PB[bk]], [Bgel[wb % 2]])

                def stX(g):
                    wb, tl = steps[g]
                    eng = "dve" if tl == 3 else "pool"
                    tt(eng, Xb[g % 3],
                       s4v[tl][:, :, 0, wb * 4:(wb + 1) * 4].unsqueeze(3).to_broadcast([128, 8, 4, 128]),
                       s4v[tl][:, :, 1, :].unsqueeze(2).to_broadcast([128, 8, 4, 128]), ALU.add, [Bs4[tl]], [BX[g % 3]])

                def stE(g):
                    act(Eb[g % 2], Xb[g % 3], AF.Exp, [BX[g % 3]], [BEb[g % 2]])

                def stM(g):
                    stt("dve", Mb[g % 2], Xb[g % 3].rearrange("p h a b -> p h (a b)"), 0.0,
                        Eb[g % 2].rearrange("p h a b -> p h (a b)"), ALU.is_ge, ALU.mult, [BX[g % 3], BEb[g % 2]], [BM[g % 2]])

                def stG(g):
                    wb, tl = steps[g]
                    bk = 2 + g % 2
                    for et in range(4):
                        for h in range(8):
                            mm(psb[bk][:, et * 128:(et + 1) * 128], Mb[g % 2][:, h, et * 128:(et + 1) * 128], Dg[tl][:, h, :],
                               (et == 0 and h == 0), h == 7, [BM[g % 2], BDg[tl]], [PB[bk]])

                def stTail(g):
                    wb, tl = steps[g]
                    csl = slice(tl * 128, (tl + 1) * 128)
                    bk = 2 + g % 2
                    ht = HT[g % 2]
                    tt("dve", ht, gelT[wb % 2][:, :, csl], psb[bk][:].rearrange("p (a b) -> p a b", a=4), ALU.mult,
                       [Bgel[wb % 2], PB[bk]], [BHT[g % 2]])
                    ob = 4 + 2 * (g % 2)
                    for et in range(4):
                        for hf in range(2):
                            mm(psb[ob + hf][:], ht[:, et, :], Vb[wb % 2][:, et, hf * 512:(hf + 1) * 512], et == 0, et == 3,
                               [BHT[g % 2], BVb[wb % 2]], [PB[ob + hf]])
                    for hf in range(2):
                        tt("dve", accO[tl][:, hf * 512:(hf + 1) * 512], accO[tl][:, hf * 512:(hf + 1) * 512], psb[ob + hf][:],
                           ALU.add, [BaccO[tl], PB[ob + hf]], [BaccO[tl]])

                ldw(0)
                stA(0)
                stX(0); stX(1); stE(0)
                for g in range(G_):
                    wb, tl = steps[g]
                    if g + 2 < G_:
                        stX(g + 2)
                    if g + 1 < G_:
                        stE(g + 1)
                    stM(g)
                    stG(g)
                    if g >= 1:
                        stTail(g - 1)
                    if tl == 1 and wb + 1 < NWB:
                        ldw(wb + 1)
                    if tl == 3 and wb + 1 < NWB:
                        stA(wb + 1)
                stTail(G_ - 1)
                dma(h1T, h1_v[:, :, tsl], [], [Bh1])
                for tl in range(4):
                    csl = slice(tl * 128, (tl + 1) * 128)
                    for dc in range(8):
                        bk = 6 + (dc // 4) % 2
                        tr(psb[bk][:, (dc % 4) * 128:(dc % 4 + 1) * 128], accO[tl][:, dc * 128:(dc + 1) * 128], ident,
                           [BaccO[tl], Bc], [PB[bk]])
                        if dc % 4 == 3:
                            d0 = dc - 3
                            tt("dve", h1T[:, d0:d0 + 4, csl], h1T[:, d0:d0 + 4, csl],
                               psb[bk][:].rearrange("p (a b) -> p a b", a=4), ALU.add, [Bh1, PB[bk]], [Bh1])
                dma(h2_v[:, :, tsl], h1T, [Bh1], [])
            R.barrier()

            off[0] = base_off
            Wpg = alloc([8, 1024], BF16); Wpe = alloc([2, 1024], BF16); gple = alloc([8]); gfin = alloc([8])
            stg = alloc([1024]); hT = alloc([8, 512]); aT = alloc([8, 512], BF16); sqb = alloc([8, 512], BF16)
            rstd = alloc([512]); T0 = alloc([512]); T1 = alloc([512]); pTf = alloc([2, 512]); pTb = alloc([2, 512], BF16)
            oT = alloc([8, 512]); otok = alloc([1024])
            BW = Buf("Wc"); Bstg = Buf("stgc"); BhT = Buf("hTc"); BaT = Buf("aTc"); Bsq = Buf("sqc"); Brs = Buf("rsc")
            BT0 = Buf("T0c"); BT1 = Buf("T1c"); BpTf = Buf("pTf"); BpTb = Buf("pTb"); BoT = Buf("oT"); Botok = Buf("otok")
            load_w_bf(Wpg, wpg_d[l], 8, 1024, stg, Bstg, BW)
            load_w_bf(Wpe, wpe_d[l], 2, 1024, stg, Bstg, BW)
            dma(gple, gple_d[l], [], [Bc])
            dma(gfin, gfin_d, [], [Bc])
            last = (l == L - 1)
            for b in range(NBLK):
                tsl = slice(b * 512, (b + 1) * 512)
                dma(hT, h2_v[:, :, tsl], [], [BhT])
                dma(pTf, pT_d[l].rearrange("c p s -> p c s")[:, :, tsl], [], [BpTf])
                cp("pool", pTb, pTf, [BpTf], [BpTb])
                rmsnorm_to(hT, gple, aT, BhT, BaT, sqb, Bsq, rstd, Brs, T0, BT0, 7)
                for dc in range(8):
                    b0, b1 = (dc % 2) * 2, (dc % 2) * 2 + 1
                    for k in range(8):
                        mm(psb[b0][:], Wpg[:, k, dc * 128:(dc + 1) * 128], aT[:, k, :], k == 0, k == 7, [BW, BaT], [PB[b0]])
                    for k in range(2):
                        mm(psb[b1][:], Wpe[:, k, dc * 128:(dc + 1) * 128], pTb[:, k, :], k == 0, k == 1, [BW, BpTb], [PB[b1]])
                    act(T1, psb[b0][:], AF.Sigmoid, [PB[b0]], [BT1])
                    tt("dve", T1, T1, psb[b1][:], ALU.mult, [BT1, PB[b1]], [BT1])
                    tt("dve", hT[:, dc, :], hT[:, dc, :], T1, ALU.add, [BhT, BT1], [BhT])
                if not last:
                    dma(hcur_v[:, :, tsl], hT, [BhT], [])
                else:
                    if dbg:
                        dma(hcur_v[:, :, tsl], hT, [BhT], [])
                    for c in range(8):
                        act(sqb[:, c, :], hT[:, c, :], AF.Square, [BhT], [Bsq])
                    for c in range(8):
                        mm(psb[7][:], onesb, sqb[:, c, :], c == 0, c == 7, [Bsq, Bc], [PB[7]])
                    act(T0, psb[7][:], AF.Sqrt, [PB[7], Bc], [BT0], bias=epst[:, 0:1])
                    recip(rstd, T0, [BT0], [Brs])
                    for c in range(8):
                        stt("dve", oT[:, c, :], hT[:, c, :], gfin[:, c:c + 1], rstd, ALU.mult, ALU.mult,
                            [BhT, Brs, Bc], [BoT])
                    for tl in range(4):
                        csl = slice(tl * 128, (tl + 1) * 128)
                        for dc in range(8):
                            bk = 4 + (dc // 4)
                            tr(psb[bk][:, (dc % 4) * 128:(dc % 4 + 1) * 128], oT[:, dc, csl], ident, [BoT, Bc], [PB[bk]])
                        cp("act", otok[:, 0:512], psb[4][:], [PB[4]], [Botok])
                        cp("dve", otok[:, 512:1024], psb[5][:], [PB[5]], [Botok])
                        dma(out_d[b * 512 + tl * 128:b * 512 + (tl + 1) * 128, :], otok, [Botok], [])
            R.barrier()
        R.emit()
    return nc, R


def _cols(v, n):
    L_ = v.shape[0]
    return np.ascontiguousarray(v.reshape(L_, n, 128).transpose(0, 2, 1))


def prep_shared(inp, L):
    f = lambda a: np.ascontiguousarray(np.asarray(a, dtype=np.float32))
    sh = {}
    sh["g_mix"] = _cols(f(inp["g_mix"])[:L], 8); sh["g_ffn"] = _cols(f(inp["g_ffn"])[:L], 8); sh["g_ple"] = _cols(f(inp["g_ple"])[:L], 8)
    sh["g_final"] = _cols(f(inp["g_final"])[None], 8)[0]
    for k in ("w_in", "w_out", "w_q", "w_pg", "w_pe", "expert_v"):
        sh[k] = f(inp[k])[:L]
    caw = f(inp["conv_a_w"])[:L]
    sh["conv_a_w"] = np.ascontiguousarray(caw.reshape(L, 31, 3, 128).transpose(0, 3, 2, 1))
    for k in ("conv_a_b", "ln_a_g", "ln_a_b", "ln_c_g", "ln_c_b"):
        sh[k] = _cols(f(inp[k])[:L], 3)
    cbw = f(inp["conv_b_w"])[:L]
    sh["conv_b_w"] = np.ascontiguousarray(cbw.reshape(L, 3, 2, 128).transpose(0, 3, 2, 1))
    ws = f(inp["w_s"])[:L]
    sh["w_sT"] = np.ascontiguousarray(ws.reshape(L, 3, 2, 128, 128).transpose(0, 4, 1, 2, 3).reshape(L, 128, 3, 256))
    bs = f(inp["b_s"])[:L]
    bsr = bs.reshape(L, 3, 2, 128)
    sh["b_sb"] = np.ascontiguousarray(np.repeat(bsr.transpose(0, 2, 1, 3), 64, axis=1))
    sk = f(inp["sub_keys"])[:L]
    sh["keysT"] = np.ascontiguousarray(sk.reshape(L, 16, 128, 128).transpose(0, 3, 1, 2))
    sh["expert_uT"] = np.ascontiguousarray(f(inp["expert_u"])[:L].transpose(0, 2, 1))
    return sh


_CACHE = {}


def kernel(**inputs):
    S, L = SEQ, DEPTH
    key = (S, L)
    if key not in _CACHE:
        _CACHE[key] = build(S, L)[0]
    nc = _CACHE[key]
    sh = prep_shared(inputs, L)
    x = np.asarray(inputs["x"], dtype=np.float32)
    p = np.asarray(inputs["p"], dtype=np.float32)
    in_maps = []
    for c in range(NCORES):
        m = dict(sh)
        m["x"] = np.ascontiguousarray(x[c])
        m["p"] = np.ascontiguousarray(p[:, c])
        in_maps.append(m)
    res = run_bass_kernel_spmd(nc, in_maps, core_ids=list(range(NCORES)))
    return np.stack([np.asarray(r["out"], dtype=np.float32) for r in res.results], axis=0)
```

```python
import contextlib
import numpy as np
import concourse.bass as bass
import concourse.mybir as mybir
from concourse.bass_utils import run_bass_kernel_spmd

F32 = mybir.dt.float32
BF16 = mybir.dt.bfloat16
AF = mybir.ActivationFunctionType
ALU = mybir.AluOpType
AX = mybir.AxisListType

D = 1024
NCORES = 8
SEQ = 8192
DEPTH = 2
EPS = 1e-6
NEG = -1.0e30
ENGS = ("pe", "act", "dve", "pool", "sp")


class Buf:
    __slots__ = ("name", "last_w", "readers", "dma_readers")

    def __init__(self, name):
        self.name = name
        self.last_w = None
        self.readers = {}
        self.dma_readers = []


class Op:
    __slots__ = ("eng", "fn", "deps", "signal", "dma", "semkey", "semval", "idx")

    def __init__(self, eng, fn, dma):
        self.eng = eng
        self.fn = fn
        self.deps = []
        self.signal = False
        self.dma = dma
        self.semkey = None
        self.semval = None


class Rec:
    NDMA = 24

    def __init__(self, nc):
        self.nc = nc
        self.ops = {e: [] for e in ENGS}
        self.last_real = {e: None for e in ENGS}
        self.ndma = 0
        self.dma_ops = []

    def op(self, eng, fn, reads=(), writes=(), dma=False):
        o = Op(eng, fn, dma)
        o.idx = len(self.ops[eng])
        deps = []
        for b in reads:
            if b.last_w is not None:
                deps.append(b.last_w)
        for b in writes:
            if b.last_w is not None:
                deps.append(b.last_w)
            deps.extend(b.readers.values())
            deps.extend(b.dma_readers)
        latest = {}
        dmas = []
        seen = set()
        for d in deps:
            if d is o or id(d) in seen:
                continue
            seen.add(id(d))
            if d.dma:
                dmas.append(d)
            else:
                if d.eng == "pe" and eng == "pe" and not dma:
                    continue
                if d.eng not in latest or latest[d.eng].idx < d.idx:
                    latest[d.eng] = d
        for d in list(latest.values()) + dmas:
            o.deps.append(d)
            d.signal = True
        if dma:
            o.signal = True
            k = self.ndma
            self.ndma += 1
            o.semkey = ("dma", k % self.NDMA)
            o.semval = 16 * (k // self.NDMA + 1)
            if k >= self.NDMA:
                o.deps.append(self.dma_ops[k - self.NDMA])
            self.dma_ops.append(o)
        for b in reads:
            if dma:
                b.dma_readers.append(o)
            else:
                b.readers[eng] = o
        for b in writes:
            b.last_w = o
            b.readers = {}
            b.dma_readers = []
        self.ops[eng].append(o)
        self.last_real[eng] = o
        return o

    def barrier(self):
        lasts = [o for o in self.last_real.values() if o is not None]
        lasts += self.dma_ops[-self.NDMA:]
        for d in lasts:
            d.signal = True
        for e in ENGS:
            o = Op(e, None, False)
            seen = set()
            for d in lasts:
                if id(d) in seen:
                    continue
                seen.add(id(d))
                o.deps.append(d)
            self.ops[e].append(o)

    EPOCH = 6000

    def emit(self):
        nc = self.nc
        nep = {}
        for e in ENGS:
            cnt = 0
            for o in self.ops[e]:
                if o.dma or o.fn is None:
                    continue
                if o.signal:
                    o.semkey = ("eng", e, cnt // self.EPOCH)
                    o.semval = cnt % self.EPOCH + 1
                    cnt += 1
            nep[e] = cnt // self.EPOCH + 1
        with contextlib.ExitStack() as st:
            sems = {}
            for e in ENGS:
                for ep in range(nep[e]):
                    sems[("eng", e, ep)] = st.enter_context(nc.semaphore("s_%s%d" % (e, ep)))
            for i in range(self.NDMA):
                sems[("dma", i)] = st.enter_context(nc.semaphore("s_dma%d" % i))
            block = st.enter_context(nc.Block())

            def run(e, eng):
                waited = {}
                for o in self.ops[e]:
                    for d in o.deps:
                        if waited.get(d.semkey, 0) >= d.semval:
                            continue
                        waited[d.semkey] = d.semval
                        eng.wait_ge(sems[d.semkey], d.semval)
                    if o.fn is None:
                        continue
                    ins = o.fn(eng)
                    if o.signal:
                        ins.then_inc(sems[o.semkey], 16 if o.dma else 1)

            block.tensor(lambda eng: run("pe", eng))
            block.scalar(lambda eng: run("act", eng))
            block.vector(lambda eng: run("dve", eng))
            block.gpsimd(lambda eng: run("pool", eng))
            block.sync(lambda eng: run("sp", eng))


def build(S, L, dbg=False):
    assert S % 512 == 0
    NBLK = S // 512
    nc = bass.Bass("TRN2", target_bir_lowering=False)

    def din(name, shape, dt=F32):
        return nc.dram_tensor(name, list(shape), dt, kind="ExternalInput").ap()

    def dscr(name, shape, dt=F32, out=False):
        return nc.dram_tensor(name, list(shape), dt, kind=("ExternalOutput" if out else "Internal")).ap()

    x_d = din("x", [S, D])
    p_d = din("p", [L, S, 256])
    gmix_d = din("g_mix", [L, 128, 8]); gffn_d = din("g_ffn", [L, 128, 8]); gple_d = din("g_ple", [L, 128, 8])
    gfin_d = din("g_final", [128, 8])
    win_d = din("w_in", [L, D, 2304]); wout_d = din("w_out", [L, D, D]); wq_d = din("w_q", [L, D, 2048])
    wpg_d = din("w_pg", [L, D, D]); wpe_d = din("w_pe", [L, 256, D])
    caw_d = din("conv_a_w", [L, 128, 3, 31]); cab_d = din("conv_a_b", [L, 128, 3])
    lag_d = din("ln_a_g", [L, 128, 3]); lab_d = din("ln_a_b", [L, 128, 3])
    cbw_d = din("conv_b_w", [L, 128, 2, 3])
    lcg_d = din("ln_c_g", [L, 128, 3]); lcb_d = din("ln_c_b", [L, 128, 3])
    wst_d = din("w_sT", [L, 128, 3, 256]); bsb_d = din("b_sb", [L, 128, 3, 128])
    keys_d = din("keysT", [L, 128, 16, 128])
    eu_d = din("expert_uT", [L, D, 16384]); ev_d = din("expert_v", [L, 16384, D])
    out_d = nc.dram_tensor("out", [S, D], F32, kind="ExternalOutput").ap()

    hcur_d = dscr("hcur", [8, 128, S], out=dbg)
    h1_d = dscr("h1", [8, 128, S], out=dbg)
    h2_d = dscr("h2", [8, 128, S], out=dbg)
    pT_d = dscr("pT", [L, 2, 128, S])
    nT_d = dscr("nT", [8, 128, S], BF16)
    sS_d = dscr("sS", [S, 2048], out=dbg)
    sTK_d = dscr("sTK", [S, 16], out=dbg)
    ubf_d = dscr("ubf", [L, 8, 128, 16384], BF16)
    vbf_d = dscr("vbf", [L, 128, 128, D], BF16)

    st = contextlib.ExitStack()
    with st:
        CAP = 206 * 1024 + 512
        arena_t = st.enter_context(nc.sbuf_tensor("arena", [128, CAP // 4], F32))
        psb = [st.enter_context(nc.psum_tensor("pb%d" % i, [128, 512], F32)) for i in range(8)]
        PB = [Buf("pb%d" % i) for i in range(8)]
        R = Rec(nc)
        off = [0]

        def alloc(shape, dt=F32):
            n = int(np.prod(shape)) * (2 if dt == BF16 else 4)
            n = (n + 31) // 32 * 32
            a = arena_t[:, off[0] // 4:(off[0] + n) // 4]
            off[0] += n
            assert off[0] <= CAP, ("SBUF arena overflow", off[0])
            if dt == BF16:
                a = a.bitcast(BF16)
            a = a[:, 0:int(np.prod(shape))]
            if len(shape) == 2:
                a = a.rearrange("p (a b) -> p a b", a=shape[0])
            elif len(shape) == 3:
                a = a.rearrange("p (a b c) -> p a b c", a=shape[0], b=shape[1])
            return a

        def mm(out, lhsT, rhs, start, stop, rd, wr):
            return R.op("pe", lambda e: e.matmul(out, lhsT=lhsT, rhs=rhs, start=start, stop=stop), rd, wr)

        def tr(out, in_, ident, rd, wr):
            return R.op("pe", lambda e: e.transpose(out=out, in_=in_, identity=ident), rd, wr)

        def act(out, in_, func, rd, wr, scale=None, bias=None):
            kw = {}
            if scale is not None:
                kw["scale"] = scale
            if bias is not None:
                kw["bias"] = bias
            return R.op("act", lambda e: e.activation(out=out, in_=in_, func=func, **kw), rd, wr)

        def tt(eng, out, in0, in1, op, rd, wr):
            return R.op(eng, lambda e: e.tensor_tensor(out=out, in0=in0, in1=in1, op=op), rd, wr)

        def ts(eng, out, in0, s1, s2, op0, op1, rd, wr):
            if s2 is None:
                return R.op(eng, lambda e: e.tensor_scalar(out=out, in0=in0, scalar1=s1, scalar2=None, op0=op0), rd, wr)
            return R.op(eng, lambda e: e.tensor_scalar(out=out, in0=in0, scalar1=s1, scalar2=s2, op0=op0, op1=op1), rd, wr)

        def stt(eng, out, in0, scalar, in1, op0, op1, rd, wr):
            return R.op(eng, lambda e: e.scalar_tensor_tensor(out=out, in0=in0, scalar=scalar, in1=in1, op0=op0, op1=op1), rd, wr)

        def cp(eng, out, in_, rd, wr):
            if eng == "act":
                return R.op("act", lambda e: e.copy(out=out, in_=in_), rd, wr)
            return R.op(eng, lambda e: e.tensor_copy(out=out, in_=in_), rd, wr)

        def mset(eng, ap, val, wr):
            return R.op(eng, lambda e: e.memset(ap, val), (), wr)

        def dma(out, in_, rd, wr):
            return R.op("sp", lambda e: e.dma_start(out=out, in_=in_), rd, wr, dma=True)

        def recip(out, in_, rd, wr):
            return R.op("dve", lambda e: e.reciprocal(out=out, in_=in_), rd, wr)

        def vmax(out, in_, rd, wr):
            return R.op("dve", lambda e: e.max(out=out, in_=in_), rd, wr)

        def mrep(out, rep, vals, rd, wr):
            return R.op("dve", lambda e: e.match_replace(out=out, in_to_replace=rep, in_values=vals, imm_value=NEG), rd, wr)

        def pbf(i):
            return psb[i][:].bitcast(BF16)

        ident = alloc([128]); identb = alloc([128], BF16); onesb = alloc([128], BF16); blk = alloc([128])
        epst = alloc([1])
        Bc = Buf("consts")
        mset("pool", ident, 0.0, [Bc])
        R.op("pool", lambda e: e.affine_select(out=ident, in_=ident, pattern=[[-1, 128]], compare_op=ALU.not_equal,
                                               fill=1.0, base=0, channel_multiplier=1), [Bc], [Bc])
        cp("dve", identb, ident, [Bc], [Bc])
        mset("pool", onesb, 1.0 / 1024.0, [Bc])
        mset("pool", blk, 0.0, [Bc])
        mset("pool", blk[0:64, 0:64], 1.0 / 64.0, [Bc])
        mset("pool", blk[64:128, 64:128], 1.0 / 64.0, [Bc])
        mset("pool", epst, EPS, [Bc])
        base_off = off[0]

        xs = alloc([1024]); xst = alloc([8, 128]); cst = alloc([2048]); cstb = alloc([2048], BF16)
        pst = alloc([2, 128])
        Bxs, Bxst, Bcst, Bcstb, Bpst = Buf("xs"), Buf("xst"), Buf("cst"), Buf("cstb"), Buf("pst")
        hcur_v = hcur_d.rearrange("c p s -> p c s")
        h1_v = h1_d.rearrange("c p s -> p c s")
        h2_v = h2_d.rearrange("c p s -> p c s")
        nT_v = nT_d.rearrange("c p s -> p c s")
        for tl in range(S // 128):
            ts_ = slice(tl * 128, (tl + 1) * 128)
            dma(xs, x_d[ts_, :], [], [Bxs])
            for c in range(8):
                bk = c // 4
                tr(psb[bk][:, (c % 4) * 128:(c % 4 + 1) * 128], xs[:, c * 128:(c + 1) * 128], ident, [Bxs, Bc], [PB[bk]])
            cp("act", xst[:, 0:4, :], psb[0][:].rearrange("p (a b) -> p a b", a=4), [PB[0]], [Bxst])
            cp("dve", xst[:, 4:8, :], psb[1][:].rearrange("p (a b) -> p a b", a=4), [PB[1]], [Bxst])
            dma(hcur_v[:, :, ts_], xst, [Bxst], [])
            for l in range(L):
                dma(xs[:, 0:256], p_d[l, ts_, :], [], [Bxs])
                for c in range(2):
                    tr(psb[2][:, c * 128:(c + 1) * 128], xs[:, c * 128:(c + 1) * 128], ident, [Bxs, Bc], [PB[2]])
                cp("act", pst, psb[2][:, 0:256].rearrange("p (a b) -> p a b", a=2), [PB[2]], [Bpst])
                dma(pT_d[l].rearrange("c p s -> p c s")[:, :, ts_], pst, [Bpst], [])
        it = 0
        for l in range(L):
            for c in range(8):
                for cb in range(8):
                    dma(cst, eu_d[l, c * 128:(c + 1) * 128, cb * 2048:(cb + 1) * 2048], [], [Bcst])
                    cp("dve" if it % 2 == 0 else "act", cstb, cst, [Bcst], [Bcstb])
                    dma(ubf_d[l, c, :, cb * 2048:(cb + 1) * 2048], cstb, [Bcstb], [])
                    it += 1
            for et in range(0, 128, 2):
                dma(cst.rearrange("p (a b) -> p a b", a=2),
                    ev_d[l, et * 128:(et + 2) * 128, :].rearrange("(a p) d -> p a d", a=2), [], [Bcst])
                cp("dve" if it % 2 == 0 else "act", cstb, cst, [Bcst], [Bcstb])
                dma(vbf_d[l, et:et + 2].rearrange("a p d -> p a d"), cstb.rearrange("p (a b) -> p a b", a=2), [Bcstb], [])
                it += 1
        R.barrier()

        def rmsnorm_to(hT, g, outT, BhT, BoutT, sqb, Bsq, rstd, Brstd, t0buf, Bt0, bank, out_f32=False):
            for c in range(8):
                if c % 2 == 0:
                    act(sqb[:, c, :], hT[:, c, :], AF.Square, [BhT], [Bsq])
                else:
                    tt("pool", sqb[:, c, :], hT[:, c, :], hT[:, c, :], ALU.mult, [BhT], [Bsq])
            for c in range(8):
                mm(psb[bank][:], onesb, sqb[:, c, :], c == 0, c == 7, [Bsq, Bc], [PB[bank]])
            act(t0buf, psb[bank][:], AF.Sqrt, [PB[bank], Bc], [Bt0], bias=epst[:, 0:1])
            recip(rstd, t0buf, [Bt0], [Brstd])
            for c in range(8):
                stt("dve", outT[:, c, :], hT[:, c, :], g[:, c:c + 1], rstd, ALU.mult, ALU.mult,
                    [BhT, Brstd, Bc], [BoutT])

        def load_w_bf(dst, src_d, nk, ncol, stage, Bst, Bdst):
            for k in range(nk):
                dma(stage[:, 0:ncol], src_d[k * 128:(k + 1) * 128, :], [], [Bst])
                cp("dve" if k % 2 == 0 else "act", dst[:, k, :], stage[:, 0:ncol], [Bst], [Bdst])

        for l in range(L):
            off[0] = base_off
            Win = alloc([8, 2304], BF16); Wout = alloc([8, 1024], BF16); Wq = alloc([8, 2048], BF16)
            keysb = alloc([16, 128], BF16); wsT = alloc([3, 256], BF16)
            gmix = alloc([8]); gffn = alloc([8]); caw = alloc([3, 31]); cab = alloc([3]); lag = alloc([3]); lab = alloc([3])
            cbw = alloc([2, 3]); lcg = alloc([3]); lcb = alloc([3]); bsb = alloc([3, 128])
            hT = alloc([8, 512]); aT = alloc([8, 512], BF16); ymix = alloc([8, 512], BF16)
            rstd = alloc([512]); T = [alloc([512]) for _ in range(6)]
            ybuf = [alloc([544]) for _ in range(3)]; ubuf = [alloc([516]) for _ in range(2)]
            uC = alloc([3, 512]); vln = alloc([3, 512], BF16); vlnT = alloc([384], BF16)
            qT = alloc([16, 512], BF16)
            s_sb = alloc([2048]); vtop = alloc([16, 16]); tmpk = alloc([16, 128]); cand = alloc([8, 256]); vals = alloc([8, 16])
            evx = alloc([8, 16]); zz = alloc([8]); tk = alloc([16])
            stage = qT.bitcast(F32) if False else None
            BW = Buf("W"); BhT = Buf("hT"); BaT = Buf("aT"); Bym = Buf("ymix"); Brs = Buf("rstd")
            BT = [Buf("T%d" % i) for i in range(6)]
            Byb = [Buf("yb%d" % i) for i in range(3)]; Bub = [Buf("ub%d" % i) for i in range(2)]
            BuC = Buf("uC"); Bvln = Buf("vln"); BvlnT = Buf("vlnT"); BqT = Buf("qT")
            Bs = Buf("s_sb"); Bv = [Buf("vtop%d" % i) for i in range(16)]; Btk_ = [Buf("tmpk%d" % i) for i in range(16)]; Bcand = Buf("cand"); Bvals = [Buf("vals%d" % i) for i in range(8)]
            Bevx = Buf("evx"); Bzz = Buf("zz"); Btkk = Buf("tk")
            stg = s_sb
            Bstg = Bs
            for k in range(8):
                for hf in range(2):
                    ncol = 1152
                    dma(stg[:, 0:ncol], win_d[l, k * 128:(k + 1) * 128, hf * ncol:(hf + 1) * ncol], [], [Bstg])
                    cp("dve" if hf == 0 else "act", Win[:, k, hf * ncol:(hf + 1) * ncol], stg[:, 0:ncol], [Bstg], [BW])
            load_w_bf(Wout, wout_d[l], 8, 1024, stg, Bstg, BW)
            load_w_bf(Wq, wq_d[l], 8, 2048, stg, Bstg, BW)
            dma(stg.rearrange("p (a b) -> p a b", a=16), keys_d[l], [], [Bstg])
            cp("dve", keysb, stg.rearrange("p (a b) -> p a b", a=16), [Bstg], [BW])
            dma(stg[:, 0:768].rearrange("p (a b) -> p a b", a=3), wst_d[l], [], [Bstg])
            cp("dve", wsT, stg[:, 0:768].rearrange("p (a b) -> p a b", a=3), [Bstg], [BW])
            for c in range(3):
                for hh in range(2):
                    mset("pool", wsT[64:128, c, hh * 128:hh * 128 + 64], 0.0, [BW])
            for (dst, src) in [(gmix, gmix_d[l]), (gffn, gffn_d[l]), (caw, caw_d[l]), (cab, cab_d[l]), (lag, lag_d[l]),
                               (lab, lab_d[l]), (cbw, cbw_d[l]), (lcg, lcg_d[l]), (lcb, lcb_d[l]), (bsb, bsb_d[l])]:
                dma(dst, src, [], [Bc])
            for j in range(3):
                mset("pool", ybuf[j][:, 0:32], 0.0, [Byb[j]])
            for j in range(2):
                mset("pool", ubuf[j][:, 0:4], 0.0, [Bub[j]])

            def zchunk(j, bank):
                for k in range(8):
                    mm(psb[bank][:], Win[:, k, j * 128:(j + 1) * 128], aT[:, k, :], k == 0, k == 7, [BW, BaT], [PB[bank]])

            def gln(src, Bsrc, g, b, j, func, dst, Bdst):
                mm(psb[2][:], blk, src, True, True, [Bsrc, Bc], [PB[2]])
                tt("dve", T[2], src, psb[2][:], ALU.subtract, [Bsrc, PB[2]], [BT[2]])
                act(T[3], T[2], AF.Square, [BT[2]], [BT[3]])
                mm(psb[3][:], blk, T[3], True, True, [BT[3], Bc], [PB[3]])
                act(T[4], psb[3][:], AF.Sqrt, [PB[3], Bc], [BT[4]], bias=epst[:, 0:1])
                recip(T[4], T[4], [BT[4]], [BT[4]])
                tt("dve", T[2], T[2], T[4], ALU.mult, [BT[2], BT[4]], [BT[2]])
                act(dst, T[2], func, [BT[2], Bc], [Bdst], scale=g[:, j:j + 1], bias=b[:, j:j + 1])

            for b in range(NBLK):
                tsl = slice(b * 512, (b + 1) * 512)
                dma(hT, hcur_v[:, :, tsl], [], [BhT])
                rmsnorm_to(hT, gmix, aT, BhT, BaT, ymix, Bym, rstd, Brs, T[0], BT[0], 7)
                for j in range(3):
                    zchunk(j, 0)
                    zchunk(3 + j, 1)
                    act(T[1], psb[1][:], AF.Sigmoid, [PB[1]], [BT[1]])
                    tt("dve", ybuf[j][:, 32:544], psb[0][:], T[1], ALU.mult, [PB[0], BT[1]], [Byb[j]])
                    ts("dve", T[5], ybuf[j][:, 2:514], caw[:, j, 0:1], cab[:, j:j + 1], ALU.mult, ALU.add, [Byb[j], Bc], [BT[5]])
                    for k in range(1, 31):
                        stt("dve", T[5], ybuf[j][:, 2 + k:514 + k], caw[:, j, k:k + 1], T[5], ALU.mult, ALU.add, [Byb[j], Bc, BT[5]], [BT[5]])
                    cp("pool", ybuf[j][:, 0:32], ybuf[j][:, 512:544], [Byb[j]], [Byb[j]])
                    gln(T[5], BT[5], lag, lab, j, AF.Silu, ymix[:, j, :], Bym)
                for j in range(2):
                    zchunk(8 + j, 0)
                    zchunk(10 + j, 1)
                    cp("act", T[1], psb[0][:], [PB[0]], [BT[1]])
                    tt("dve", ubuf[j][:, 4:516], psb[1][:], T[1], ALU.mult, [PB[1], BT[1]], [Bub[j]])
                    ts("dve", T[5], ubuf[j][:, 2:514], cbw[:, j, 0:1], None, ALU.mult, None, [Bub[j], Bc], [BT[5]])
                    for k in range(1, 3):
                        stt("dve", T[5], ubuf[j][:, 2 + k:514 + k], cbw[:, j, k:k + 1], T[5], ALU.mult, ALU.add, [Bub[j], Bc, BT[5]], [BT[5]])
                    cp("pool", ubuf[j][:, 0:4], ubuf[j][:, 512:516], [Bub[j]], [Bub[j]])
                    zchunk(6 + j, 0)
                    tt("dve", ymix[:, 3 + j, :], psb[0][:], T[5], ALU.mult, [PB[0], BT[5]], [Bym])
                for j in range(3):
                    zchunk(15 + j, 0)
                    cp("act", T[5], psb[0][:], [PB[0]], [BT[5]])
                    gln(T[5], BT[5], lcg, lcb, j, AF.Identity, vln[:, j, :], Bvln)
                    zchunk(12 + j, 1)
                    cp("act", uC[:, j, :], psb[1][:], [PB[1]], [BuC])
                for sbk in range(4):
                    csl = slice(sbk * 128, (sbk + 1) * 128)
                    for j in range(3):
                        tr(pbf(4)[:, j * 128:(j + 1) * 128], vln[:, j, csl], identb, [Bvln, Bc], [PB[4]])
                    cp("act", vlnT, pbf(4)[:, 0:384], [PB[4]], [BvlnT])
                    for j in range(3):
                        mm(psb[5][:, 0:256], vlnT[:, j * 128:(j + 1) * 128], wsT[:, j, :], True, True, [BvlnT, BW], [PB[5]])
                        for hh in range(2):
                            rs = slice(hh * 64, (hh + 1) * 64)
                            tt("dve", T[1][rs, 0:128], psb[5][rs, hh * 128:(hh + 1) * 128], bsb[rs, j, :], ALU.add, [PB[5], Bc], [BT[1]])
                            tt("dve", ymix[rs, 5 + j, csl], T[1][rs, 0:128], uC[rs, j, csl], ALU.mult, [BT[1], BuC], [Bym])
                for dc in range(8):
                    bk = 6 + dc % 2
                    for k in range(8):
                        mm(psb[bk][:], Wout[:, k, dc * 128:(dc + 1) * 128], ymix[:, k, :], k == 0, k == 7, [BW, Bym], [PB[bk]])
                    tt("dve", hT[:, dc, :], hT[:, dc, :], psb[bk][:], ALU.add, [BhT, PB[bk]], [BhT])
                dma(h1_v[:, :, tsl], hT, [BhT], [])
                rmsnorm_to(hT, gffn, aT, BhT, BaT, ymix, Bym, rstd, Brs, T[0], BT[0], 7)
                dma(nT_v[:, :, tsl], aT, [BaT], [])
                for qc in range(16):
                    bk = 6 + qc % 2
                    for k in range(8):
                        mm(psb[bk][:], Wq[:, k, qc * 128:(qc + 1) * 128], aT[:, k, :], k == 0, k == 7, [BW, BaT], [PB[bk]])
                    cp("act" if qc % 2 == 0 else "dve", qT[:, qc, :], psb[bk][:], [PB[bk]], [BqT])
                for tl in range(4):
                    csl = slice(tl * 128, (tl + 1) * 128)
                    for g in range(16):
                        mm(psb[g // 4][:, (g % 4) * 128:(g % 4 + 1) * 128], qT[:, g, csl], keysb[:, g, :], True, True,
                           [BqT, BW], [PB[g // 4]])
                    for q4 in range(4):
                        cp("act", s_sb[:, q4 * 512:(q4 + 1) * 512], psb[q4][:], [PB[q4]], [Bs])
                    for g in range(16):
                        vmax(vtop[:, g, 0:8], s_sb[:, g * 128:(g + 1) * 128], [Bs], [Bv[g]])
                    for g in range(16):
                        mrep(tmpk[:, g, :], vtop[:, g, 0:8], s_sb[:, g * 128:(g + 1) * 128], [Bs, Bv[g]], [Btk_[g]])
                    for g in range(16):
                        vmax(vtop[:, g, 8:16], tmpk[:, g, :], [Btk_[g]], [Bv[g]])
                    v4 = vtop.rearrange("p (h two) k -> p h two k", two=2)
                    c4 = cand.rearrange("p h (i j) -> p h i j", i=16)
                    tt("dve", c4, v4[:, :, 0, :].unsqueeze(3).to_broadcast([128, 8, 16, 16]),
                       v4[:, :, 1, :].unsqueeze(2).to_broadcast([128, 8, 16, 16]), ALU.add, Bv, [Bcand])
                    tmpc = tmpk.rearrange("p (h two) k -> p h (two k)", two=2)
                    for h in range(8):
                        vmax(vals[:, h, 0:8], cand[:, h, :], [Bcand], [Bvals[h]])
                    for h in range(8):
                        mrep(tmpc[:, h, :], vals[:, h, 0:8], cand[:, h, :], [Bcand, Bvals[h]], [Btk_[2 * h], Btk_[2 * h + 1]])
                    for h in range(8):
                        vmax(vals[:, h, 8:16], tmpc[:, h, :], [Btk_[2 * h], Btk_[2 * h + 1]], [Bvals[h]])
                    tt("dve", evx, vals, vals[:, :, 0:1].to_broadcast([128, 8, 16]), ALU.subtract, Bvals, [Bevx])
                    act(evx, evx, AF.Exp, [Bevx], [Bevx])
                    R.op("dve", lambda e: e.reduce_sum(out=zz, in_=evx, axis=AX.X), [Bevx], [Bzz])
                    act(zz, zz, AF.Ln, [Bzz], [Bzz])
                    ts("dve", tk[:, 0:8], vals[:, :, 15], -1.0e-5, None, ALU.add, None, Bvals, [Btkk])
                    tt("dve", zz, zz, vals[:, :, 0], ALU.add, [Bzz] + Bvals, [Bzz])
                    ts("dve", tk[:, 8:16], zz, -1.0, None, ALU.mult, None, [Bzz], [Btkk])
                    rows = slice(b * 512 + tl * 128, b * 512 + (tl + 1) * 128)
                    dma(sS_d[rows, :], s_sb, [Bs], [])
                    dma(sTK_d[rows, :], tk, [Btkk], [])
            R.barrier()

            off[0] = base_off
            NWB = 32
            nTb = alloc([8, 512], BF16)
            s4 = [alloc([2048]) for _ in range(4)]; tk4 = [alloc([16]) for _ in range(4)]
            cc = [alloc([8]) for _ in range(4)]; Dg = [alloc([8, 128], BF16) for _ in range(4)]
            accO = [alloc([1024]) for _ in range(4)]
            Ub = [alloc([8, 512], BF16) for _ in range(2)]; Vb = [alloc([4, 1024], BF16) for _ in range(2)]
            gelT = [alloc([4, 512], BF16) for _ in range(2)]
            Xb = [alloc([8, 4, 128]) for _ in range(3)]; Eb = [alloc([8, 4, 128], BF16) for _ in range(2)]
            Mb = [alloc([8, 512], BF16) for _ in range(2)]
            HT = [alloc([4, 128], BF16) for _ in range(2)]
            h1T = Xb[0].rearrange("p a b c -> p (a b c)").rearrange("p (a b) -> p a b", a=8)
            BnT = Buf("nTb"); Bs4 = [Buf("s4%d" % i) for i in range(4)]; BaccO = [Buf("accO%d" % i) for i in range(4)]
            BDg = [Buf("Dg%d" % i) for i in range(4)]
            BUb = [Buf("Ub%d" % i) for i in range(2)]; BVb = [Buf("Vb%d" % i) for i in range(2)]
            BX = [Buf("X%d" % i) for i in range(3)]; BEb = [Buf("Eb%d" % i) for i in range(2)]; BM = [Buf("M%d" % i) for i in range(2)]
            Bgel = [Buf("gelT%d" % i) for i in range(2)]; BHT = [Buf("HT%d" % i) for i in range(2)]
            Bh1 = BX[0]
            ubf_v = ubf_d[l].rearrange("c p e -> p c e")
            vbf_v = vbf_d[l].rearrange("a p d -> p a d")
            s4v = [t_.rearrange("p (h two k) -> p h two k", two=2, k=128) for t_ in s4]
            for tb in range(NBLK):
                tsl = slice(tb * 512, (tb + 1) * 512)
                dma(nTb, nT_v[:, :, tsl], [], [BnT])
                for tl in range(4):
                    rows = slice(tb * 512 + tl * 128, tb * 512 + (tl + 1) * 128)
                    dma(s4[tl], sS_d[rows, :], [], [Bs4[tl]])
                    dma(tk4[tl], sTK_d[rows, :], [], [Bs4[tl]])
                    mset("pool", accO[tl], 0.0, [BaccO[tl]])
                    tt("dve", s4v[tl][:, :, 0, :], s4v[tl][:, :, 0, :], tk4[tl][:, 0:8].unsqueeze(2).to_broadcast([128, 8, 128]),
                       ALU.subtract, [Bs4[tl]], [Bs4[tl]])
                    tt("dve", cc[tl], tk4[tl][:, 0:8], tk4[tl][:, 8:16], ALU.add, [Bs4[tl]], [BDg[tl]])
                    act(cc[tl], cc[tl], AF.Exp, [BDg[tl]], [BDg[tl]])
                    for h in range(8):
                        ts("pool", Dg[tl][:, h, :], identb, cc[tl][:, h:h + 1], None, ALU.mult, None, [BDg[tl], Bc], [BDg[tl]])

                steps = [(wb, tl) for wb in range(NWB) for tl in range(4)]
                G_ = len(steps)

                def ldw(wb):
                    dma(Ub[wb % 2], ubf_v[:, :, wb * 512:(wb + 1) * 512], [], [BUb[wb % 2]])
                    dma(Vb[wb % 2], vbf_v[:, wb * 4:(wb + 1) * 4, :], [], [BVb[wb % 2]])

                def stA(wb):
                    for et in range(4):
                        bk = et % 2
                        for k in range(8):
                            mm(psb[bk][:], Ub[wb % 2][:, k, et * 128:(et + 1) * 128], nTb[:, k, :], k == 0, k == 7,
                               [BnT, BUb[wb % 2]], [PB[bk]])
                        act(gelT[wb % 2][:, et, :], psb[bk][:], AF.Gelu_apprx_tanh, [PB[bk]], [Bgel[wb % 2]])

                def stX(g):
                    wb, tl = steps[g]
                    eng = "dve" if tl == 3 else "pool"
                    tt(eng, Xb[g % 3],
                       s4v[tl][:, :, 0, wb * 4:(wb + 1) * 4].unsqueeze(3).to_broadcast([128, 8, 4, 128]),
                       s4v[tl][:, :, 1, :].unsqueeze(2).to_broadcast([128, 8, 4, 128]), ALU.add, [Bs4[tl]], [BX[g % 3]])

                def stE(g):
                    act(Eb[g % 2], Xb[g % 3], AF.Exp, [BX[g % 3]], [BEb[g % 2]])

                def stM(g):
                    stt("dve", Mb[g % 2], Xb[g % 3].rearrange("p h a b -> p h (a b)"), 0.0,
                        Eb[g % 2].rearrange("p h a b -> p h (a b)"), ALU.is_ge, ALU.mult, [BX[g % 3], BEb[g % 2]], [BM[g % 2]])

                def stG(g):
                    wb, tl = steps[g]
                    bk = 2 + g % 2
                    for et in range(4):
                        for h in range(8):
                            mm(psb[bk][:, et * 128:(et + 1) * 128], Mb[g % 2][:, h, et * 128:(et + 1) * 128], Dg[tl][:, h, :],
                               (et == 0 and h == 0), h == 7, [BM[g % 2], BDg[tl]], [PB[bk]])

                def stHT(g):
                    wb, tl = steps[g]
                    csl = slice(tl * 128, (tl + 1) * 128)
                    bk = 2 + g % 2
                    tt("dve", HT[g % 2], gelT[wb % 2][:, :, csl], psb[bk][:].rearrange("p (a b) -> p a b", a=4), ALU.mult,
                       [Bgel[wb % 2], PB[bk]], [BHT[g % 2]])

                def stV(g):
                    wb, tl = steps[g]
                    ht = HT[g % 2]
                    ob = 4 + 2 * (g % 2)
                    for et in range(4):
                        for hf in range(2):
                            mm(psb[ob + hf][:], ht[:, et, :], Vb[wb % 2][:, et, hf * 512:(hf + 1) * 512], et == 0, et == 3,
                               [BHT[g % 2], BVb[wb % 2]], [PB[ob + hf]])

                def stAcc(g):
                    wb, tl = steps[g]
                    ob = 4 + 2 * (g % 2)
                    for hf in range(2):
                        tt("dve", accO[tl][:, hf * 512:(hf + 1) * 512], accO[tl][:, hf * 512:(hf + 1) * 512], psb[ob + hf][:],
                           ALU.add, [BaccO[tl], PB[ob + hf]], [BaccO[tl]])

                ldw(0)
                stA(0)
                stX(0); stX(1); stE(0)
                for g in range(G_ + 3):
                    if g + 2 < G_:
                        stX(g + 2)
                    if g + 1 < G_:
                        stE(g + 1)
                    if 0 <= g - 2 < G_:
                        stHT(g - 2)
                    if g < G_:
                        stM(g)
                    if 0 <= g - 3 < G_:
                        stAcc(g - 3)
                    if 0 <= g - 1 < G_:
                        stG(g - 1)
                    if 0 <= g - 2 < G_:
                        stV(g - 2)
                    if g < G_:
                        wb, tl = steps[g]
                        if tl == 1 and wb + 1 < NWB:
                            ldw(wb + 1)
                        if tl == 3 and wb + 1 < NWB:
                            stA(wb + 1)
                dma(h1T, h1_v[:, :, tsl], [], [Bh1])
                for tl in range(4):
                    csl = slice(tl * 128, (tl + 1) * 128)
                    for dc in range(8):
                        bk = 6 + (dc // 4) % 2
                        tr(psb[bk][:, (dc % 4) * 128:(dc % 4 + 1) * 128], accO[tl][:, dc * 128:(dc + 1) * 128], ident,
                           [BaccO[tl], Bc], [PB[bk]])
                        if dc % 4 == 3:
                            d0 = dc - 3
                            tt("dve", h1T[:, d0:d0 + 4, csl], h1T[:, d0:d0 + 4, csl],
                               psb[bk][:].rearrange("p (a b) -> p a b", a=4), ALU.add, [Bh1, PB[bk]], [Bh1])
                dma(h2_v[:, :, tsl], h1T, [Bh1], [])
            R.barrier()

            off[0] = base_off
            Wpg = alloc([8, 1024], BF16); Wpe = alloc([2, 1024], BF16); gple = alloc([8]); gfin = alloc([8])
            stg = alloc([1024]); hT = alloc([8, 512]); aT = alloc([8, 512], BF16); sqb = alloc([8, 512], BF16)
            rstd = alloc([512]); T0 = alloc([512]); T1 = alloc([512]); pTf = alloc([2, 512]); pTb = alloc([2, 512], BF16)
            oT = alloc([8, 512]); otok = alloc([1024])
            BW = Buf("Wc"); Bstg = Buf("stgc"); BhT = Buf("hTc"); BaT = Buf("aTc"); Bsq = Buf("sqc"); Brs = Buf("rsc")
            BT0 = Buf("T0c"); BT1 = Buf("T1c"); BpTf = Buf("pTf"); BpTb = Buf("pTb"); BoT = Buf("oT"); Botok = Buf("otok")
            load_w_bf(Wpg, wpg_d[l], 8, 1024, stg, Bstg, BW)
            load_w_bf(Wpe, wpe_d[l], 2, 1024, stg, Bstg, BW)
            dma(gple, gple_d[l], [], [Bc])
            dma(gfin, gfin_d, [], [Bc])
            last = (l == L - 1)
            for b in range(NBLK):
                tsl = slice(b * 512, (b + 1) * 512)
                dma(hT, h2_v[:, :, tsl], [], [BhT])
                dma(pTf, pT_d[l].rearrange("c p s -> p c s")[:, :, tsl], [], [BpTf])
                cp("pool", pTb, pTf, [BpTf], [BpTb])
                rmsnorm_to(hT, gple, aT, BhT, BaT, sqb, Bsq, rstd, Brs, T0, BT0, 7)
                for dc in range(8):
                    b0, b1 = (dc % 2) * 2, (dc % 2) * 2 + 1
                    for k in range(8):
                        mm(psb[b0][:], Wpg[:, k, dc * 128:(dc + 1) * 128], aT[:, k, :], k == 0, k == 7, [BW, BaT], [PB[b0]])
                    for k in range(2):
                        mm(psb[b1][:], Wpe[:, k, dc * 128:(dc + 1) * 128], pTb[:, k, :], k == 0, k == 1, [BW, BpTb], [PB[b1]])
                    act(T1, psb[b0][:], AF.Sigmoid, [PB[b0]], [BT1])
                    tt("dve", T1, T1, psb[b1][:], ALU.mult, [BT1, PB[b1]], [BT1])
                    tt("dve", hT[:, dc, :], hT[:, dc, :], T1, ALU.add, [BhT, BT1], [BhT])
                if not last:
                    dma(hcur_v[:, :, tsl], hT, [BhT], [])
                else:
                    if dbg:
                        dma(hcur_v[:, :, tsl], hT, [BhT], [])
                    for c in range(8):
                        act(sqb[:, c, :], hT[:, c, :], AF.Square, [BhT], [Bsq])
                    for c in range(8):
                        mm(psb[7][:], onesb, sqb[:, c, :], c == 0, c == 7, [Bsq, Bc], [PB[7]])
                    act(T0, psb[7][:], AF.Sqrt, [PB[7], Bc], [BT0], bias=epst[:, 0:1])
                    recip(rstd, T0, [BT0], [Brs])
                    for c in range(8):
                        stt("dve", oT[:, c, :], hT[:, c, :], gfin[:, c:c + 1], rstd, ALU.mult, ALU.mult,
                            [BhT, Brs, Bc], [BoT])
                    for tl in range(4):
                        csl = slice(tl * 128, (tl + 1) * 128)
                        for dc in range(8):
                            bk = 4 + (dc // 4)
                            tr(psb[bk][:, (dc % 4) * 128:(dc % 4 + 1) * 128], oT[:, dc, csl], ident, [BoT, Bc], [PB[bk]])
                        cp("act", otok[:, 0:512], psb[4][:], [PB[4]], [Botok])
                        cp("dve", otok[:, 512:1024], psb[5][:], [PB[5]], [Botok])
                        dma(out_d[b * 512 + tl * 128:b * 512 + (tl + 1) * 128, :], otok, [Botok], [])
            R.barrier()
        R.emit()
    return nc, R


def _cols(v, n):
    L_ = v.shape[0]
    return np.ascontiguousarray(v.reshape(L_, n, 128).transpose(0, 2, 1))


def prep_shared(inp, L):
    f = lambda a: np.ascontiguousarray(np.asarray(a, dtype=np.float32))
    sh = {}
    sh["g_mix"] = _cols(f(inp["g_mix"])[:L], 8); sh["g_ffn"] = _cols(f(inp["g_ffn"])[:L], 8); sh["g_ple"] = _cols(f(inp["g_ple"])[:L], 8)
    sh["g_final"] = _cols(f(inp["g_final"])[None], 8)[0]
    for k in ("w_in", "w_out", "w_q", "w_pg", "w_pe", "expert_v"):
        sh[k] = f(inp[k])[:L]
    caw = f(inp["conv_a_w"])[:L]
    sh["conv_a_w"] = np.ascontiguousarray(caw.reshape(L, 31, 3, 128).transpose(0, 3, 2, 1))
    for k in ("conv_a_b", "ln_a_g", "ln_a_b", "ln_c_g", "ln_c_b"):
        sh[k] = _cols(f(inp[k])[:L], 3)
    cbw = f(inp["conv_b_w"])[:L]
    sh["conv_b_w"] = np.ascontiguousarray(cbw.reshape(L, 3, 2, 128).transpose(0, 3, 2, 1))
    ws = f(inp["w_s"])[:L]
    sh["w_sT"] = np.ascontiguousarray(ws.reshape(L, 3, 2, 128, 128).transpose(0, 4, 1, 2, 3).reshape(L, 128, 3, 256))
    bs = f(inp["b_s"])[:L]
    bsr = bs.reshape(L, 3, 2, 128)
    sh["b_sb"] = np.ascontiguousarray(np.repeat(bsr.transpose(0, 2, 1, 3), 64, axis=1))
    sk = f(inp["sub_keys"])[:L]
    sh["keysT"] = np.ascontiguousarray(sk.reshape(L, 16, 128, 128).transpose(0, 3, 1, 2))
    sh["expert_uT"] = np.ascontiguousarray(f(inp["expert_u"])[:L].transpose(0, 2, 1))
    return sh


_CACHE = {}


def kernel(**inputs):
    S, L = SEQ, DEPTH
    key = (S, L)
    if key not in _CACHE:
        _CACHE[key] = build(S, L)[0]
    nc = _CACHE[key]
    sh = prep_shared(inputs, L)
    x = np.asarray(inputs["x"], dtype=np.float32)
    p = np.asarray(inputs["p"], dtype=np.float32)
    in_maps = []
    for c in range(NCORES):
        m = dict(sh)
        m["x"] = np.ascontiguousarray(x[c])
        m["p"] = np.ascontiguousarray(p[:, c])
        in_maps.append(m)
    res = run_bass_kernel_spmd(nc, in_maps, core_ids=list(range(NCORES)))
    return np.stack([np.asarray(r["out"], dtype=np.float32) for r in res.results], axis=0)
```

```python
import contextlib
import numpy as np
import concourse.bass as bass
import concourse.mybir as mybir
from concourse.bass_utils import run_bass_kernel_spmd

F32 = mybir.dt.float32
BF16 = mybir.dt.bfloat16
AF = mybir.ActivationFunctionType
ALU = mybir.AluOpType
AX = mybir.AxisListType

D = 1024
NCORES = 8
SEQ = 8192
DEPTH = 2
EPS = 1e-6
NEG = -1.0e30
ENGS = ("pe", "act", "dve", "pool", "sp")


class Buf:
    __slots__ = ("name", "last_w", "readers", "dma_readers")

    def __init__(self, name):
        self.name = name
        self.last_w = None
        self.readers = {}
        self.dma_readers = []


class Op:
    __slots__ = ("eng", "fn", "deps", "signal", "dma", "semkey", "semval", "idx")

    def __init__(self, eng, fn, dma):
        self.eng = eng
        self.fn = fn
        self.deps = []
        self.signal = False
        self.dma = dma
        self.semkey = None
        self.semval = None


class Rec:
    NDMA = 24

    def __init__(self, nc):
        self.nc = nc
        self.ops = {e: [] for e in ENGS}
        self.last_real = {e: None for e in ENGS}
        self.ndma = 0
        self.dma_ops = []

    def op(self, eng, fn, reads=(), writes=(), dma=False):
        o = Op(eng, fn, dma)
        o.idx = len(self.ops[eng])
        deps = []
        for b in reads:
            if b.last_w is not None:
                deps.append(b.last_w)
        for b in writes:
            if b.last_w is not None:
                deps.append(b.last_w)
            deps.extend(b.readers.values())
            deps.extend(b.dma_readers)
        latest = {}
        dmas = []
        seen = set()
        for d in deps:
            if d is o or id(d) in seen:
                continue
            seen.add(id(d))
            if d.dma:
                dmas.append(d)
            else:
                if d.eng == "pe" and eng == "pe" and not dma:
                    continue
                if d.eng not in latest or latest[d.eng].idx < d.idx:
                    latest[d.eng] = d
        for d in list(latest.values()) + dmas:
            o.deps.append(d)
            d.signal = True
        if dma:
            o.signal = True
            k = self.ndma
            self.ndma += 1
            o.semkey = ("dma", k % self.NDMA)
            o.semval = 16 * (k // self.NDMA + 1)
            if k >= self.NDMA:
                o.deps.append(self.dma_ops[k - self.NDMA])
            self.dma_ops.append(o)
        for b in reads:
            if dma:
                b.dma_readers.append(o)
            else:
                b.readers[eng] = o
        for b in writes:
            b.last_w = o
            b.readers = {}
            b.dma_readers = []
        self.ops[eng].append(o)
        self.last_real[eng] = o
        return o

    def barrier(self):
        lasts = [o for o in self.last_real.values() if o is not None]
        lasts += self.dma_ops[-self.NDMA:]
        for d in lasts:
            d.signal = True
        for e in ENGS:
            o = Op(e, None, False)
            seen = set()
            for d in lasts:
                if id(d) in seen:
                    continue
                seen.add(id(d))
                o.deps.append(d)
            self.ops[e].append(o)

    EPOCH = 6000

    def emit(self):
        nc = self.nc
        nep = {}
        for e in ENGS:
            cnt = 0
            for o in self.ops[e]:
                if o.dma or o.fn is None:
                    continue
                if o.signal:
                    o.semkey = ("eng", e, cnt // self.EPOCH)
                    o.semval = cnt % self.EPOCH + 1
                    cnt += 1
            nep[e] = cnt // self.EPOCH + 1
        with contextlib.ExitStack() as st:
            sems = {}
            for e in ENGS:
                for ep in range(nep[e]):
                    sems[("eng", e, ep)] = st.enter_context(nc.semaphore("s_%s%d" % (e, ep)))
            for i in range(self.NDMA):
                sems[("dma", i)] = st.enter_context(nc.semaphore("s_dma%d" % i))
            block = st.enter_context(nc.Block())

            def run(e, eng):
                waited = {}
                for o in self.ops[e]:
                    for d in o.deps:
                        if waited.get(d.semkey, 0) >= d.semval:
                            continue
                        waited[d.semkey] = d.semval
                        eng.wait_ge(sems[d.semkey], d.semval)
                    if o.fn is None:
                        continue
                    ins = o.fn(eng)
                    if o.signal:
                        ins.then_inc(sems[o.semkey], 16 if o.dma else 1)

            block.tensor(lambda eng: run("pe", eng))
            block.scalar(lambda eng: run("act", eng))
            block.vector(lambda eng: run("dve", eng))
            block.gpsimd(lambda eng: run("pool", eng))
            block.sync(lambda eng: run("sp", eng))


def build(S, L, dbg=False):
    assert S % 512 == 0
    NBLK = S // 512
    nc = bass.Bass("TRN2", target_bir_lowering=False)

    def din(name, shape, dt=F32):
        return nc.dram_tensor(name, list(shape), dt, kind="ExternalInput").ap()

    def dscr(name, shape, dt=F32, out=False):
        return nc.dram_tensor(name, list(shape), dt, kind=("ExternalOutput" if out else "Internal")).ap()

    x_d = din("x", [S, D])
    p_d = din("p", [L, S, 256])
    gmix_d = din("g_mix", [L, 128, 8]); gffn_d = din("g_ffn", [L, 128, 8]); gple_d = din("g_ple", [L, 128, 8])
    gfin_d = din("g_final", [128, 8])
    win_d = din("w_in", [L, D, 2304]); wout_d = din("w_out", [L, D, D]); wq_d = din("w_q", [L, D, 2048])
    wpg_d = din("w_pg", [L, D, D]); wpe_d = din("w_pe", [L, 256, D])
    caw_d = din("conv_a_w", [L, 128, 3, 31]); cab_d = din("conv_a_b", [L, 128, 3])
    lag_d = din("ln_a_g", [L, 128, 3]); lab_d = din("ln_a_b", [L, 128, 3])
    cbw_d = din("conv_b_w", [L, 128, 2, 3])
    lcg_d = din("ln_c_g", [L, 128, 3]); lcb_d = din("ln_c_b", [L, 128, 3])
    wst_d = din("w_sT", [L, 128, 3, 256]); bsb_d = din("b_sb", [L, 128, 3, 128])
    keys_d = din("keysT", [L, 128, 16, 128])
    eu_d = din("expert_uT", [L, D, 16384]); ev_d = din("expert_v", [L, 16384, D])
    out_d = nc.dram_tensor("out", [S, D], F32, kind="ExternalOutput").ap()

    hcur_d = dscr("hcur", [8, 128, S], out=dbg)
    h1_d = dscr("h1", [8, 128, S], out=dbg)
    h2_d = dscr("h2", [8, 128, S], out=dbg)
    pT_d = dscr("pT", [L, 2, 128, S])
    nT_d = dscr("nT", [8, 128, S], BF16)
    sS_d = dscr("sS", [S, 2048], out=dbg)
    sTK_d = dscr("sTK", [S, 16], out=dbg)
    ubf_d = dscr("ubf", [L, 8, 128, 16384], BF16)
    vbf_d = dscr("vbf", [L, 128, 128, D], BF16)

    st = contextlib.ExitStack()
    with st:
        CAP = 206 * 1024 + 512
        arena_t = st.enter_context(nc.sbuf_tensor("arena", [128, CAP // 4], F32))
        psb = [st.enter_context(nc.psum_tensor("pb%d" % i, [128, 512], F32)) for i in range(8)]
        PB = [Buf("pb%d" % i) for i in range(8)]
        R = Rec(nc)
        off = [0]

        def alloc(shape, dt=F32):
            n = int(np.prod(shape)) * (2 if dt == BF16 else 4)
            n = (n + 31) // 32 * 32
            a = arena_t[:, off[0] // 4:(off[0] + n) // 4]
            off[0] += n
            assert off[0] <= CAP, ("SBUF arena overflow", off[0])
            if dt == BF16:
                a = a.bitcast(BF16)
            a = a[:, 0:int(np.prod(shape))]
            if len(shape) == 2:
                a = a.rearrange("p (a b) -> p a b", a=shape[0])
            elif len(shape) == 3:
                a = a.rearrange("p (a b c) -> p a b c", a=shape[0], b=shape[1])
            return a

        def mm(out, lhsT, rhs, start, stop, rd, wr):
            return R.op("pe", lambda e: e.matmul(out, lhsT=lhsT, rhs=rhs, start=start, stop=stop), rd, wr)

        def tr(out, in_, ident, rd, wr):
            return R.op("pe", lambda e: e.transpose(out=out, in_=in_, identity=ident), rd, wr)

        def act(out, in_, func, rd, wr, scale=None, bias=None):
            kw = {}
            if scale is not None:
                kw["scale"] = scale
            if bias is not None:
                kw["bias"] = bias
            return R.op("act", lambda e: e.activation(out=out, in_=in_, func=func, **kw), rd, wr)

        def tt(eng, out, in0, in1, op, rd, wr):
            return R.op(eng, lambda e: e.tensor_tensor(out=out, in0=in0, in1=in1, op=op), rd, wr)

        def ts(eng, out, in0, s1, s2, op0, op1, rd, wr):
            if s2 is None:
                return R.op(eng, lambda e: e.tensor_scalar(out=out, in0=in0, scalar1=s1, scalar2=None, op0=op0), rd, wr)
            return R.op(eng, lambda e: e.tensor_scalar(out=out, in0=in0, scalar1=s1, scalar2=s2, op0=op0, op1=op1), rd, wr)

        def stt(eng, out, in0, scalar, in1, op0, op1, rd, wr):
            return R.op(eng, lambda e: e.scalar_tensor_tensor(out=out, in0=in0, scalar=scalar, in1=in1, op0=op0, op1=op1), rd, wr)

        def cp(eng, out, in_, rd, wr):
            if eng == "act":
                return R.op("act", lambda e: e.copy(out=out, in_=in_), rd, wr)
            return R.op(eng, lambda e: e.tensor_copy(out=out, in_=in_), rd, wr)

        def mset(eng, ap, val, wr):
            return R.op(eng, lambda e: e.memset(ap, val), (), wr)

        def dma(out, in_, rd, wr):
            return R.op("sp", lambda e: e.dma_start(out=out, in_=in_), rd, wr, dma=True)

        def recip(out, in_, rd, wr):
            return R.op("dve", lambda e: e.reciprocal(out=out, in_=in_), rd, wr)

        def vmax(out, in_, rd, wr):
            return R.op("dve", lambda e: e.max(out=out, in_=in_), rd, wr)

        def mrep(out, rep, vals, rd, wr):
            return R.op("dve", lambda e: e.match_replace(out=out, in_to_replace=rep, in_values=vals, imm_value=NEG), rd, wr)

        def pbf(i):
            return psb[i][:].bitcast(BF16)

        ident = alloc([128]); identb = alloc([128], BF16); onesb = alloc([128], BF16); blk = alloc([128])
        epst = alloc([1])
        Bc = Buf("consts")
        mset("pool", ident, 0.0, [Bc])
        R.op("pool", lambda e: e.affine_select(out=ident, in_=ident, pattern=[[-1, 128]], compare_op=ALU.not_equal,
                                               fill=1.0, base=0, channel_multiplier=1), [Bc], [Bc])
        cp("dve", identb, ident, [Bc], [Bc])
        mset("pool", onesb, 1.0 / 1024.0, [Bc])
        mset("pool", blk, 0.0, [Bc])
        mset("pool", blk[0:64, 0:64], 1.0 / 64.0, [Bc])
        mset("pool", blk[64:128, 64:128], 1.0 / 64.0, [Bc])
        mset("pool", epst, EPS, [Bc])
        base_off = off[0]

        xs = alloc([1024]); xst = alloc([8, 128]); cst = alloc([2048]); cstb = alloc([2048], BF16)
        pst = alloc([2, 128])
        Bxs, Bxst, Bcst, Bcstb, Bpst = Buf("xs"), Buf("xst"), Buf("cst"), Buf("cstb"), Buf("pst")
        hcur_v = hcur_d.rearrange("c p s -> p c s")
        h1_v = h1_d.rearrange("c p s -> p c s")
        h2_v = h2_d.rearrange("c p s -> p c s")
        nT_v = nT_d.rearrange("c p s -> p c s")
        for tl in range(S // 128):
            ts_ = slice(tl * 128, (tl + 1) * 128)
            dma(xs, x_d[ts_, :], [], [Bxs])
            for c in range(8):
                bk = c // 4
                tr(psb[bk][:, (c % 4) * 128:(c % 4 + 1) * 128], xs[:, c * 128:(c + 1) * 128], ident, [Bxs, Bc], [PB[bk]])
            cp("act", xst[:, 0:4, :], psb[0][:].rearrange("p (a b) -> p a b", a=4), [PB[0]], [Bxst])
            cp("dve", xst[:, 4:8, :], psb[1][:].rearrange("p (a b) -> p a b", a=4), [PB[1]], [Bxst])
            dma(hcur_v[:, :, ts_], xst, [Bxst], [])
            for l in range(L):
                dma(xs[:, 0:256], p_d[l, ts_, :], [], [Bxs])
                for c in range(2):
                    tr(psb[2][:, c * 128:(c + 1) * 128], xs[:, c * 128:(c + 1) * 128], ident, [Bxs, Bc], [PB[2]])
                cp("act", pst, psb[2][:, 0:256].rearrange("p (a b) -> p a b", a=2), [PB[2]], [Bpst])
                dma(pT_d[l].rearrange("c p s -> p c s")[:, :, ts_], pst, [Bpst], [])
        it = 0
        for l in range(L):
            for c in range(8):
                for cb in range(8):
                    dma(cst, eu_d[l, c * 128:(c + 1) * 128, cb * 2048:(cb + 1) * 2048], [], [Bcst])
                    cp("dve" if it % 2 == 0 else "act", cstb, cst, [Bcst], [Bcstb])
                    dma(ubf_d[l, c, :, cb * 2048:(cb + 1) * 2048], cstb, [Bcstb], [])
                    it += 1
            for et in range(0, 128, 2):
                dma(cst.rearrange("p (a b) -> p a b", a=2),
                    ev_d[l, et * 128:(et + 2) * 128, :].rearrange("(a p) d -> p a d", a=2), [], [Bcst])
                cp("dve" if it % 2 == 0 else "act", cstb, cst, [Bcst], [Bcstb])
                dma(vbf_d[l, et:et + 2].rearrange("a p d -> p a d"), cstb.rearrange("p (a b) -> p a b", a=2), [Bcstb], [])
                it += 1
        R.barrier()

        def rmsnorm_to(hT, g, outT, BhT, BoutT, sqb, Bsq, rstd, Brstd, t0buf, Bt0, bank, out_f32=False):
            for c in range(8):
                if c % 2 == 0:
                    act(sqb[:, c, :], hT[:, c, :], AF.Square, [BhT], [Bsq])
                else:
                    tt("pool", sqb[:, c, :], hT[:, c, :], hT[:, c, :], ALU.mult, [BhT], [Bsq])
            for c in range(8):
                mm(psb[bank][:], onesb, sqb[:, c, :], c == 0, c == 7, [Bsq, Bc], [PB[bank]])
            act(t0buf, psb[bank][:], AF.Sqrt, [PB[bank], Bc], [Bt0], bias=epst[:, 0:1])
            recip(rstd, t0buf, [Bt0], [Brstd])
            for c in range(8):
                stt("dve", outT[:, c, :], hT[:, c, :], g[:, c:c + 1], rstd, ALU.mult, ALU.mult,
                    [BhT, Brstd, Bc], [BoutT])

        def load_w_bf(dst, src_d, nk, ncol, stage, Bst, Bdst):
            for k in range(nk):
                dma(stage[:, 0:ncol], src_d[k * 128:(k + 1) * 128, :], [], [Bst])
                cp("dve" if k % 2 == 0 else "act", dst[:, k, :], stage[:, 0:ncol], [Bst], [Bdst])

        for l in range(L):
            off[0] = base_off
            Win = alloc([8, 2304], BF16); Wout = alloc([8, 1024], BF16); Wq = alloc([8, 2048], BF16)
            keysb = alloc([16, 128], BF16); wsT = alloc([3, 256], BF16)
            gmix = alloc([8]); gffn = alloc([8]); caw = alloc([3, 31]); cab = alloc([3]); lag = alloc([3]); lab = alloc([3])
            cbw = alloc([2, 3]); lcg = alloc([3]); lcb = alloc([3]); bsb = alloc([3, 128])
            hT = alloc([8, 512]); aT = alloc([8, 512], BF16); ymix = alloc([8, 512], BF16)
            rstd = alloc([512]); T = [alloc([512]) for _ in range(6)]
            ybuf = [alloc([544]) for _ in range(3)]; ubuf = [alloc([516]) for _ in range(2)]
            uC = alloc([3, 512]); vln = alloc([3, 512], BF16); vlnT = alloc([384], BF16)
            qT = alloc([16, 512], BF16)
            s_sb = alloc([2048]); vtop = alloc([16, 16]); tmpk = alloc([16, 128]); cand = alloc([8, 256]); vals = alloc([8, 16])
            evx = alloc([8, 16]); zz = alloc([8]); tk = alloc([16])
            stage = qT.bitcast(F32) if False else None
            BW = Buf("W"); BhT = Buf("hT"); BaT = Buf("aT"); Bym = Buf("ymix"); Brs = Buf("rstd")
            BT = [Buf("T%d" % i) for i in range(6)]
            Byb = [Buf("yb%d" % i) for i in range(3)]; Bub = [Buf("ub%d" % i) for i in range(2)]
            BuC = Buf("uC"); Bvln = Buf("vln"); BvlnT = Buf("vlnT"); BqT = Buf("qT")
            Bs = Buf("s_sb"); Bv = [Buf("vtop%d" % i) for i in range(16)]; Btk_ = [Buf("tmpk%d" % i) for i in range(16)]; Bcand = Buf("cand"); Bvals = [Buf("vals%d" % i) for i in range(8)]
            Bevx = Buf("evx"); Bzz = Buf("zz"); Btkk = Buf("tk")
            stg = s_sb
            Bstg = Bs
            for k in range(8):
                for hf in range(2):
                    ncol = 1152
                    dma(stg[:, 0:ncol], win_d[l, k * 128:(k + 1) * 128, hf * ncol:(hf + 1) * ncol], [], [Bstg])
                    cp("dve" if hf == 0 else "act", Win[:, k, hf * ncol:(hf + 1) * ncol], stg[:, 0:ncol], [Bstg], [BW])
            load_w_bf(Wout, wout_d[l], 8, 1024, stg, Bstg, BW)
            load_w_bf(Wq, wq_d[l], 8, 2048, stg, Bstg, BW)
            dma(stg.rearrange("p (a b) -> p a b", a=16), keys_d[l], [], [Bstg])
            cp("dve", keysb, stg.rearrange("p (a b) -> p a b", a=16), [Bstg], [BW])
            dma(stg[:, 0:768].rearrange("p (a b) -> p a b", a=3), wst_d[l], [], [Bstg])
            cp("dve", wsT, stg[:, 0:768].rearrange("p (a b) -> p a b", a=3), [Bstg], [BW])
            for c in range(3):
                for hh in range(2):
                    mset("pool", wsT[64:128, c, hh * 128:hh * 128 + 64], 0.0, [BW])
            for (dst, src) in [(gmix, gmix_d[l]), (gffn, gffn_d[l]), (caw, caw_d[l]), (cab, cab_d[l]), (lag, lag_d[l]),
                               (lab, lab_d[l]), (cbw, cbw_d[l]), (lcg, lcg_d[l]), (lcb, lcb_d[l]), (bsb, bsb_d[l])]:
                dma(dst, src, [], [Bc])
            for j in range(3):
                mset("pool", ybuf[j][:, 0:32], 0.0, [Byb[j]])
            for j in range(2):
                mset("pool", ubuf[j][:, 0:4], 0.0, [Bub[j]])

            def zchunk(j, bank):
                for k in range(8):
                    mm(psb[bank][:], Win[:, k, j * 128:(j + 1) * 128], aT[:, k, :], k == 0, k == 7, [BW, BaT], [PB[bank]])

            def gln(src, Bsrc, g, b, j, func, dst, Bdst):
                mm(psb[2][:], blk, src, True, True, [Bsrc, Bc], [PB[2]])
                tt("dve", T[2], src, psb[2][:], ALU.subtract, [Bsrc, PB[2]], [BT[2]])
                act(T[3], T[2], AF.Square, [BT[2]], [BT[3]])
                mm(psb[3][:], blk, T[3], True, True, [BT[3], Bc], [PB[3]])
                act(T[4], psb[3][:], AF.Sqrt, [PB[3], Bc], [BT[4]], bias=epst[:, 0:1])
                recip(T[4], T[4], [BT[4]], [BT[4]])
                tt("dve", T[2], T[2], T[4], ALU.mult, [BT[2], BT[4]], [BT[2]])
                act(dst, T[2], func, [BT[2], Bc], [Bdst], scale=g[:, j:j + 1], bias=b[:, j:j + 1])

            for b in range(NBLK):
                tsl = slice(b * 512, (b + 1) * 512)
                dma(hT, hcur_v[:, :, tsl], [], [BhT])
                rmsnorm_to(hT, gmix, aT, BhT, BaT, ymix, Bym, rstd, Brs, T[0], BT[0], 7)
                for j in range(3):
                    zchunk(j, 0)
                    zchunk(3 + j, 1)
                    act(T[1], psb[1][:], AF.Sigmoid, [PB[1]], [BT[1]])
                    tt("dve", ybuf[j][:, 32:544], psb[0][:], T[1], ALU.mult, [PB[0], BT[1]], [Byb[j]])
                    ts("dve", T[5], ybuf[j][:, 2:514], caw[:, j, 0:1], cab[:, j:j + 1], ALU.mult, ALU.add, [Byb[j], Bc], [BT[5]])
                    for k in range(1, 31):
                        stt("dve", T[5], ybuf[j][:, 2 + k:514 + k], caw[:, j, k:k + 1], T[5], ALU.mult, ALU.add, [Byb[j], Bc, BT[5]], [BT[5]])
                    cp("pool", ybuf[j][:, 0:32], ybuf[j][:, 512:544], [Byb[j]], [Byb[j]])
                    gln(T[5], BT[5], lag, lab, j, AF.Silu, ymix[:, j, :], Bym)
                for j in range(2):
                    zchunk(8 + j, 0)
                    zchunk(10 + j, 1)
                    cp("act", T[1], psb[0][:], [PB[0]], [BT[1]])
                    tt("dve", ubuf[j][:, 4:516], psb[1][:], T[1], ALU.mult, [PB[1], BT[1]], [Bub[j]])
                    ts("dve", T[5], ubuf[j][:, 2:514], cbw[:, j, 0:1], None, ALU.mult, None, [Bub[j], Bc], [BT[5]])
                    for k in range(1, 3):
                        stt("dve", T[5], ubuf[j][:, 2 + k:514 + k], cbw[:, j, k:k + 1], T[5], ALU.mult, ALU.add, [Bub[j], Bc, BT[5]], [BT[5]])
                    cp("pool", ubuf[j][:, 0:4], ubuf[j][:, 512:516], [Bub[j]], [Bub[j]])
                    zchunk(6 + j, 0)
                    tt("dve", ymix[:, 3 + j, :], psb[0][:], T[5], ALU.mult, [PB[0], BT[5]], [Bym])
                for j in range(3):
                    zchunk(15 + j, 0)
                    cp("act", T[5], psb[0][:], [PB[0]], [BT[5]])
                    gln(T[5], BT[5], lcg, lcb, j, AF.Identity, vln[:, j, :], Bvln)
                    zchunk(12 + j, 1)
                    cp("act", uC[:, j, :], psb[1][:], [PB[1]], [BuC])
                for sbk in range(4):
                    csl = slice(sbk * 128, (sbk + 1) * 128)
                    for j in range(3):
                        tr(pbf(4)[:, j * 128:(j + 1) * 128], vln[:, j, csl], identb, [Bvln, Bc], [PB[4]])
                    cp("act", vlnT, pbf(4)[:, 0:384], [PB[4]], [BvlnT])
                    for j in range(3):
                        mm(psb[5][:, 0:256], vlnT[:, j * 128:(j + 1) * 128], wsT[:, j, :], True, True, [BvlnT, BW], [PB[5]])
                        for hh in range(2):
                            rs = slice(hh * 64, (hh + 1) * 64)
                            tt("dve", T[1][rs, 0:128], psb[5][rs, hh * 128:(hh + 1) * 128], bsb[rs, j, :], ALU.add, [PB[5], Bc], [BT[1]])
                            tt("dve", ymix[rs, 5 + j, csl], T[1][rs, 0:128], uC[rs, j, csl], ALU.mult, [BT[1], BuC], [Bym])
                for dc in range(8):
                    bk = 6 + dc % 2
                    for k in range(8):
                        mm(psb[bk][:], Wout[:, k, dc * 128:(dc + 1) * 128], ymix[:, k, :], k == 0, k == 7, [BW, Bym], [PB[bk]])
                    tt("dve", hT[:, dc, :], hT[:, dc, :], psb[bk][:], ALU.add, [BhT, PB[bk]], [BhT])
                dma(h1_v[:, :, tsl], hT, [BhT], [])
                rmsnorm_to(hT, gffn, aT, BhT, BaT, ymix, Bym, rstd, Brs, T[0], BT[0], 7)
                dma(nT_v[:, :, tsl], aT, [BaT], [])
                for qc in range(16):
                    bk = 6 + qc % 2
                    for k in range(8):
                        mm(psb[bk][:], Wq[:, k, qc * 128:(qc + 1) * 128], aT[:, k, :], k == 0, k == 7, [BW, BaT], [PB[bk]])
                    cp("act" if qc % 2 == 0 else "dve", qT[:, qc, :], psb[bk][:], [PB[bk]], [BqT])
                for tl in range(4):
                    csl = slice(tl * 128, (tl + 1) * 128)
                    for g in range(16):
                        mm(psb[g // 4][:, (g % 4) * 128:(g % 4 + 1) * 128], qT[:, g, csl], keysb[:, g, :], True, True,
                           [BqT, BW], [PB[g // 4]])
                    for q4 in range(4):
                        cp("act", s_sb[:, q4 * 512:(q4 + 1) * 512], psb[q4][:], [PB[q4]], [Bs])
                    for g in range(16):
                        vmax(vtop[:, g, 0:8], s_sb[:, g * 128:(g + 1) * 128], [Bs], [Bv[g]])
                    for g in range(16):
                        mrep(tmpk[:, g, :], vtop[:, g, 0:8], s_sb[:, g * 128:(g + 1) * 128], [Bs, Bv[g]], [Btk_[g]])
                    for g in range(16):
                        vmax(vtop[:, g, 8:16], tmpk[:, g, :], [Btk_[g]], [Bv[g]])
                    v4 = vtop.rearrange("p (h two) k -> p h two k", two=2)
                    c4 = cand.rearrange("p h (i j) -> p h i j", i=16)
                    tt("dve", c4, v4[:, :, 0, :].unsqueeze(3).to_broadcast([128, 8, 16, 16]),
                       v4[:, :, 1, :].unsqueeze(2).to_broadcast([128, 8, 16, 16]), ALU.add, Bv, [Bcand])
                    tmpc = tmpk.rearrange("p (h two) k -> p h (two k)", two=2)
                    for h in range(8):
                        vmax(vals[:, h, 0:8], cand[:, h, :], [Bcand], [Bvals[h]])
                    for h in range(8):
                        mrep(tmpc[:, h, :], vals[:, h, 0:8], cand[:, h, :], [Bcand, Bvals[h]], [Btk_[2 * h], Btk_[2 * h + 1]])
                    for h in range(8):
                        vmax(vals[:, h, 8:16], tmpc[:, h, :], [Btk_[2 * h], Btk_[2 * h + 1]], [Bvals[h]])
                    tt("dve", evx, vals, vals[:, :, 0:1].to_broadcast([128, 8, 16]), ALU.subtract, Bvals, [Bevx])
                    act(evx, evx, AF.Exp, [Bevx], [Bevx])
                    R.op("dve", lambda e: e.reduce_sum(out=zz, in_=evx, axis=AX.X), [Bevx], [Bzz])
                    act(zz, zz, AF.Ln, [Bzz], [Bzz])
                    ts("dve", tk[:, 0:8], vals[:, :, 15], -1.0e-5, None, ALU.add, None, Bvals, [Btkk])
                    tt("dve", zz, zz, vals[:, :, 0], ALU.add, [Bzz] + Bvals, [Bzz])
                    ts("dve", tk[:, 8:16], zz, -1.0, None, ALU.mult, None, [Bzz], [Btkk])
                    rows = slice(b * 512 + tl * 128, b * 512 + (tl + 1) * 128)
                    dma(sS_d[rows, :], s_sb, [Bs], [])
                    dma(sTK_d[rows, :], tk, [Btkk], [])
            R.barrier()

            off[0] = base_off
            NWB = 32
            nTb = alloc([8, 512], BF16)
            s4 = [alloc([2048]) for _ in range(4)]; tk4 = [alloc([16]) for _ in range(4)]
            cc = [alloc([8]) for _ in range(4)]; Dg = [alloc([8, 128], BF16) for _ in range(4)]
            accO = [alloc([1024]) for _ in range(4)]
            Ub = [alloc([8, 512], BF16) for _ in range(2)]; Vb = [alloc([4, 1024], BF16) for _ in range(2)]
            gelT = [alloc([4, 512], BF16) for _ in range(2)]
            Xb = [alloc([8, 4, 128]) for _ in range(3)]; Eb = [alloc([8, 4, 128], BF16) for _ in range(2)]; Emb = [alloc([8, 4, 128], BF16) for _ in range(2)]
            Mb = [alloc([8, 512], BF16) for _ in range(2)]
            HT = [alloc([4, 128], BF16) for _ in range(2)]
            h1T = Xb[0].rearrange("p a b c -> p (a b c)").rearrange("p (a b) -> p a b", a=8)
            BnT = Buf("nTb"); Bs4 = [Buf("s4%d" % i) for i in range(4)]; BaccO = [Buf("accO%d" % i) for i in range(4)]
            BDg = [Buf("Dg%d" % i) for i in range(4)]
            BUb = [Buf("Ub%d" % i) for i in range(2)]; BVb = [Buf("Vb%d" % i) for i in range(2)]
            BX = [Buf("X%d" % i) for i in range(3)]; BEb = [Buf("Eb%d" % i) for i in range(2)]; BM = [Buf("M%d" % i) for i in range(2)]
            Bgel = [Buf("gelT%d" % i) for i in range(2)]; BHT = [Buf("HT%d" % i) for i in range(2)]
            Bh1 = BX[0]
            ubf_v = ubf_d[l].rearrange("c p e -> p c e")
            vbf_v = vbf_d[l].rearrange("a p d -> p a d")
            s4v = [t_.rearrange("p (h two k) -> p h two k", two=2, k=128) for t_ in s4]
            for tb in range(NBLK):
                tsl = slice(tb * 512, (tb + 1) * 512)
                dma(nTb, nT_v[:, :, tsl], [], [BnT])
                for tl in range(4):
                    rows = slice(tb * 512 + tl * 128, tb * 512 + (tl + 1) * 128)
                    dma(s4[tl], sS_d[rows, :], [], [Bs4[tl]])
                    dma(tk4[tl], sTK_d[rows, :], [], [Bs4[tl]])
                    mset("pool", accO[tl], 0.0, [BaccO[tl]])
                    tt("dve", s4v[tl][:, :, 0, :], s4v[tl][:, :, 0, :], tk4[tl][:, 0:8].unsqueeze(2).to_broadcast([128, 8, 128]),
                       ALU.subtract, [Bs4[tl]], [Bs4[tl]])
                    tt("dve", cc[tl], tk4[tl][:, 0:8], tk4[tl][:, 8:16], ALU.add, [Bs4[tl]], [BDg[tl]])
                    act(cc[tl], cc[tl], AF.Exp, [BDg[tl]], [BDg[tl]])
                    for h in range(8):
                        ts("pool", Dg[tl][:, h, :], identb, cc[tl][:, h:h + 1], None, ALU.mult, None, [BDg[tl], Bc], [BDg[tl]])

                steps = [(wb, tl) for wb in range(NWB) for tl in range(4)]
                G_ = len(steps)

                def ldw(wb):
                    dma(Ub[wb % 2], ubf_v[:, :, wb * 512:(wb + 1) * 512], [], [BUb[wb % 2]])
                    dma(Vb[wb % 2], vbf_v[:, wb * 4:(wb + 1) * 4, :], [], [BVb[wb % 2]])

                def stA(wb):
                    for et in range(4):
                        bk = et % 2
                        for k in range(8):
                            mm(psb[bk][:], Ub[wb % 2][:, k, et * 128:(et + 1) * 128], nTb[:, k, :], k == 0, k == 7,
                               [BnT, BUb[wb % 2]], [PB[bk]])
                        act(gelT[wb % 2][:, et, :], psb[bk][:], AF.Gelu_apprx_tanh, [PB[bk]], [Bgel[wb % 2]])

                def stX(g):
                    wb, tl = steps[g]
                    eng = "dve" if tl == 3 else "pool"
                    tt(eng, Xb[g % 3],
                       s4v[tl][:, :, 0, wb * 4:(wb + 1) * 4].unsqueeze(3).to_broadcast([128, 8, 4, 128]),
                       s4v[tl][:, :, 1, :].unsqueeze(2).to_broadcast([128, 8, 4, 128]), ALU.add, [Bs4[tl]], [BX[g % 3]])

                def stE(g):
                    act(Eb[g % 2], Xb[g % 3], AF.Exp, [BX[g % 3]], [BEb[g % 2]])
                    act(Emb[g % 2], Xb[g % 3], AF.Exp, [BX[g % 3]], [BEb[g % 2]], scale=1048576.0)

                def stM(g):
                    tt("dve", Mb[g % 2], Emb[g % 2].rearrange("p h a b -> p h (a b)"),
                       Eb[g % 2].rearrange("p h a b -> p h (a b)"), ALU.min, [BEb[g % 2]], [BM[g % 2]])

                def stG(g):
                    wb, tl = steps[g]
                    bk = 2 + g % 2
                    for et in range(4):
                        for h in range(8):
                            mm(psb[bk][:, et * 128:(et + 1) * 128], Mb[g % 2][:, h, et * 128:(et + 1) * 128], Dg[tl][:, h, :],
                               (et == 0 and h == 0), h == 7, [BM[g % 2], BDg[tl]], [PB[bk]])

                def stHT(g):
                    wb, tl = steps[g]
                    csl = slice(tl * 128, (tl + 1) * 128)
                    bk = 2 + g % 2
                    tt("dve", HT[g % 2], gelT[wb % 2][:, :, csl], psb[bk][:].rearrange("p (a b) -> p a b", a=4), ALU.mult,
                       [Bgel[wb % 2], PB[bk]], [BHT[g % 2]])

                def stV(g):
                    wb, tl = steps[g]
                    ht = HT[g % 2]
                    ob = 4 + 2 * (g % 2)
                    for et in range(4):
                        for hf in range(2):
                            mm(psb[ob + hf][:], ht[:, et, :], Vb[wb % 2][:, et, hf * 512:(hf + 1) * 512], et == 0, et == 3,
                               [BHT[g % 2], BVb[wb % 2]], [PB[ob + hf]])

                def stAcc(g):
                    wb, tl = steps[g]
                    ob = 4 + 2 * (g % 2)
                    for hf in range(2):
                        tt("dve", accO[tl][:, hf * 512:(hf + 1) * 512], accO[tl][:, hf * 512:(hf + 1) * 512], psb[ob + hf][:],
                           ALU.add, [BaccO[tl], PB[ob + hf]], [BaccO[tl]])

                ldw(0)
                stA(0)
                stX(0); stX(1); stE(0)
                for g in range(G_ + 3):
                    if g + 2 < G_:
                        stX(g + 2)
                    if g + 1 < G_:
                        stE(g + 1)
                    if 0 <= g - 2 < G_:
                        stHT(g - 2)
                    if g < G_:
                        stM(g)
                    if 0 <= g - 3 < G_:
                        stAcc(g - 3)
                    if 0 <= g - 1 < G_:
                        stG(g - 1)
                    if 0 <= g - 2 < G_:
                        stV(g - 2)
                    if g < G_:
                        wb, tl = steps[g]
                        if tl == 1 and wb + 1 < NWB:
                            ldw(wb + 1)
                        if tl == 3 and wb + 1 < NWB:
                            stA(wb + 1)
                dma(h1T, h1_v[:, :, tsl], [], [Bh1])
                for tl in range(4):
                    csl = slice(tl * 128, (tl + 1) * 128)
                    for dc in range(8):
                        bk = 6 + (dc // 4) % 2
                        tr(psb[bk][:, (dc % 4) * 128:(dc % 4 + 1) * 128], accO[tl][:, dc * 128:(dc + 1) * 128], ident,
                           [BaccO[tl], Bc], [PB[bk]])
                        if dc % 4 == 3:
                            d0 = dc - 3
                            tt("dve", h1T[:, d0:d0 + 4, csl], h1T[:, d0:d0 + 4, csl],
                               psb[bk][:].rearrange("p (a b) -> p a b", a=4), ALU.add, [Bh1, PB[bk]], [Bh1])
                dma(h2_v[:, :, tsl], h1T, [Bh1], [])
            R.barrier()

            off[0] = base_off
            Wpg = alloc([8, 1024], BF16); Wpe = alloc([2, 1024], BF16); gple = alloc([8]); gfin = alloc([8])
            stg = alloc([1024]); hT = alloc([8, 512]); aT = alloc([8, 512], BF16); sqb = alloc([8, 512], BF16)
            rstd = alloc([512]); T0 = alloc([512]); T1 = alloc([512]); pTf = alloc([2, 512]); pTb = alloc([2, 512], BF16)
            oT = alloc([8, 512]); otok = alloc([1024])
            BW = Buf("Wc"); Bstg = Buf("stgc"); BhT = Buf("hTc"); BaT = Buf("aTc"); Bsq = Buf("sqc"); Brs = Buf("rsc")
            BT0 = Buf("T0c"); BT1 = Buf("T1c"); BpTf = Buf("pTf"); BpTb = Buf("pTb"); BoT = Buf("oT"); Botok = Buf("otok")
            load_w_bf(Wpg, wpg_d[l], 8, 1024, stg, Bstg, BW)
            load_w_bf(Wpe, wpe_d[l], 2, 1024, stg, Bstg, BW)
            dma(gple, gple_d[l], [], [Bc])
            dma(gfin, gfin_d, [], [Bc])
            last = (l == L - 1)
            for b in range(NBLK):
                tsl = slice(b * 512, (b + 1) * 512)
                dma(hT, h2_v[:, :, tsl], [], [BhT])
                dma(pTf, pT_d[l].rearrange("c p s -> p c s")[:, :, tsl], [], [BpTf])
                cp("pool", pTb, pTf, [BpTf], [BpTb])
                rmsnorm_to(hT, gple, aT, BhT, BaT, sqb, Bsq, rstd, Brs, T0, BT0, 7)
                for dc in range(8):
                    b0, b1 = (dc % 2) * 2, (dc % 2) * 2 + 1
                    for k in range(8):
                        mm(psb[b0][:], Wpg[:, k, dc * 128:(dc + 1) * 128], aT[:, k, :], k == 0, k == 7, [BW, BaT], [PB[b0]])
                    for k in range(2):
                        mm(psb[b1][:], Wpe[:, k, dc * 128:(dc + 1) * 128], pTb[:, k, :], k == 0, k == 1, [BW, BpTb], [PB[b1]])
                    act(T1, psb[b0][:], AF.Sigmoid, [PB[b0]], [BT1])
                    tt("dve", T1, T1, psb[b1][:], ALU.mult, [BT1, PB[b1]], [BT1])
                    tt("dve", hT[:, dc, :], hT[:, dc, :], T1, ALU.add, [BhT, BT1], [BhT])
                if not last:
                    dma(hcur_v[:, :, tsl], hT, [BhT], [])
                else:
                    if dbg:
                        dma(hcur_v[:, :, tsl], hT, [BhT], [])
                    for c in range(8):
                        act(sqb[:, c, :], hT[:, c, :], AF.Square, [BhT], [Bsq])
                    for c in range(8):
                        mm(psb[7][:], onesb, sqb[:, c, :], c == 0, c == 7, [Bsq, Bc], [PB[7]])
                    act(T0, psb[7][:], AF.Sqrt, [PB[7], Bc], [BT0], bias=epst[:, 0:1])
                    recip(rstd, T0, [BT0], [Brs])
                    for c in range(8):
                        stt("dve", oT[:, c, :], hT[:, c, :], gfin[:, c:c + 1], rstd, ALU.mult, ALU.mult,
                            [BhT, Brs, Bc], [BoT])
                    for tl in range(4):
                        csl = slice(tl * 128, (tl + 1) * 128)
                        for dc in range(8):
                            bk = 4 + (dc // 4)
                            tr(psb[bk][:, (dc % 4) * 128:(dc % 4 + 1) * 128], oT[:, dc, csl], ident, [BoT, Bc], [PB[bk]])
                        cp("act", otok[:, 0:512], psb[4][:], [PB[4]], [Botok])
                        cp("dve", otok[:, 512:1024], psb[5][:], [PB[5]], [Botok])
                        dma(out_d[b * 512 + tl * 128:b * 512 + (tl + 1) * 128, :], otok, [Botok], [])
            R.barrier()
        R.emit()
    return nc, R


def _cols(v, n):
    L_ = v.shape[0]
    return np.ascontiguousarray(v.reshape(L_, n, 128).transpose(0, 2, 1))


def prep_shared(inp, L):
    f = lambda a: np.ascontiguousarray(np.asarray(a, dtype=np.float32))
    sh = {}
    sh["g_mix"] = _cols(f(inp["g_mix"])[:L], 8); sh["g_ffn"] = _cols(f(inp["g_ffn"])[:L], 8); sh["g_ple"] = _cols(f(inp["g_ple"])[:L], 8)
    sh["g_final"] = _cols(f(inp["g_final"])[None], 8)[0]
    for k in ("w_in", "w_out", "w_q", "w_pg", "w_pe", "expert_v"):
        sh[k] = f(inp[k])[:L]
    caw = f(inp["conv_a_w"])[:L]
    sh["conv_a_w"] = np.ascontiguousarray(caw.reshape(L, 31, 3, 128).transpose(0, 3, 2, 1))
    for k in ("conv_a_b", "ln_a_g", "ln_a_b", "ln_c_g", "ln_c_b"):
        sh[k] = _cols(f(inp[k])[:L], 3)
    cbw = f(inp["conv_b_w"])[:L]
    sh["conv_b_w"] = np.ascontiguousarray(cbw.reshape(L, 3, 2, 128).transpose(0, 3, 2, 1))
    ws = f(inp["w_s"])[:L]
    sh["w_sT"] = np.ascontiguousarray(ws.reshape(L, 3, 2, 128, 128).transpose(0, 4, 1, 2, 3).reshape(L, 128, 3, 256))
    bs = f(inp["b_s"])[:L]
    bsr = bs.reshape(L, 3, 2, 128)
    sh["b_sb"] = np.ascontiguousarray(np.repeat(bsr.transpose(0, 2, 1, 3), 64, axis=1))
    sk = f(inp["sub_keys"])[:L]
    sh["keysT"] = np.ascontiguousarray(sk.reshape(L, 16, 128, 128).transpose(0, 3, 1, 2))
    sh["expert_uT"] = np.ascontiguousarray(f(inp["expert_u"])[:L].transpose(0, 2, 1))
    return sh


_CACHE = {}


def kernel(**inputs):
    S, L = SEQ, DEPTH
    key = (S, L)
    if key not in _CACHE:
        _CACHE[key] = build(S, L)[0]
    nc = _CACHE[key]
    sh = prep_shared(inputs, L)
    x = np.asarray(inputs["x"], dtype=np.float32)
    p = np.asarray(inputs["p"], dtype=np.float32)
    in_maps = []
    for c in range(NCORES):
        m = dict(sh)
        m["x"] = np.ascontiguousarray(x[c])
        m["p"] = np.ascontiguousarray(p[:, c])
        in_maps.append(m)
    res = run_bass_kernel_spmd(nc, in_maps, core_ids=list(range(NCORES)))
    return np.stack([np.asarray(r["out"], dtype=np.float32) for r in res.results], axis=0)
```

```python
import contextlib
import numpy as np
import concourse.bass as bass
import concourse.mybir as mybir
from concourse.bass_utils import run_bass_kernel_spmd

F32 = mybir.dt.float32
BF16 = mybir.dt.bfloat16
AF = mybir.ActivationFunctionType
ALU = mybir.AluOpType
AX = mybir.AxisListType

D = 1024
NCORES = 8
SEQ = 8192
DEPTH = 2
EPS = 1e-6
NEG = -1.0e30
ENGS = ("pe", "act", "dve", "pool", "sp")


class Buf:
    __slots__ = ("name", "last_w", "readers", "dma_readers")

    def __init__(self, name):
        self.name = name
        self.last_w = None
        self.readers = {}
        self.dma_readers = []


class Op:
    __slots__ = ("eng", "fn", "deps", "signal", "dma", "semkey", "semval", "idx")

    def __init__(self, eng, fn, dma):
        self.eng = eng
        self.fn = fn
        self.deps = []
        self.signal = False
        self.dma = dma
        self.semkey = None
        self.semval = None


class Rec:
    NDMA = 24

    def __init__(self, nc):
        self.nc = nc
        self.ops = {e: [] for e in ENGS}
        self.last_real = {e: None for e in ENGS}
        self.ndma = 0
        self.dma_ops = []

    def op(self, eng, fn, reads=(), writes=(), dma=False):
        o = Op(eng, fn, dma)
        o.idx = len(self.ops[eng])
        deps = []
        for b in reads:
            if b.last_w is not None:
                deps.append(b.last_w)
        for b in writes:
            if b.last_w is not None:
                deps.append(b.last_w)
            deps.extend(b.readers.values())
            deps.extend(b.dma_readers)
        latest = {}
        dmas = []
        seen = set()
        for d in deps:
            if d is o or id(d) in seen:
                continue
            seen.add(id(d))
            if d.dma:
                dmas.append(d)
            else:
                if d.eng == "pe" and eng == "pe" and not dma:
                    continue
                if d.eng not in latest or latest[d.eng].idx < d.idx:
                    latest[d.eng] = d
        for d in list(latest.values()) + dmas:
            o.deps.append(d)
            d.signal = True
        if dma:
            o.signal = True
            k = self.ndma
            self.ndma += 1
            o.semkey = ("dma", k % self.NDMA)
            o.semval = 16 * (k // self.NDMA + 1)
            if k >= self.NDMA:
                o.deps.append(self.dma_ops[k - self.NDMA])
            self.dma_ops.append(o)
        for b in reads:
            if dma:
                b.dma_readers.append(o)
            else:
                b.readers[eng] = o
        for b in writes:
            b.last_w = o
            b.readers = {}
            b.dma_readers = []
        self.ops[eng].append(o)
        self.last_real[eng] = o
        return o

    def barrier(self):
        lasts = [o for o in self.last_real.values() if o is not None]
        lasts += self.dma_ops[-self.NDMA:]
        for d in lasts:
            d.signal = True
        for e in ENGS:
            o = Op(e, None, False)
            seen = set()
            for d in lasts:
                if id(d) in seen:
                    continue
                seen.add(id(d))
                o.deps.append(d)
            self.ops[e].append(o)

    EPOCH = 6000

    def emit(self):
        nc = self.nc
        nep = {}
        for e in ENGS:
            cnt = 0
            for o in self.ops[e]:
                if o.dma or o.fn is None:
                    continue
                if o.signal:
                    o.semkey = ("eng", e, cnt // self.EPOCH)
                    o.semval = cnt % self.EPOCH + 1
                    cnt += 1
            nep[e] = cnt // self.EPOCH + 1
        with contextlib.ExitStack() as st:
            sems = {}
            for e in ENGS:
                for ep in range(nep[e]):
                    sems[("eng", e, ep)] = st.enter_context(nc.semaphore("s_%s%d" % (e, ep)))
            for i in range(self.NDMA):
                sems[("dma", i)] = st.enter_context(nc.semaphore("s_dma%d" % i))
            block = st.enter_context(nc.Block())

            def run(e, eng):
                waited = {}
                for o in self.ops[e]:
                    for d in o.deps:
                        if waited.get(d.semkey, 0) >= d.semval:
                            continue
                        waited[d.semkey] = d.semval
                        eng.wait_ge(sems[d.semkey], d.semval)
                    if o.fn is None:
                        continue
                    ins = o.fn(eng)
                    if o.signal:
                        ins.then_inc(sems[o.semkey], 16 if o.dma else 1)

            block.tensor(lambda eng: run("pe", eng))
            block.scalar(lambda eng: run("act", eng))
            block.vector(lambda eng: run("dve", eng))
            block.gpsimd(lambda eng: run("pool", eng))
            block.sync(lambda eng: run("sp", eng))


def build(S, L, dbg=False):
    assert S % 512 == 0
    NBLK = S // 512
    nc = bass.Bass("TRN2", target_bir_lowering=False)

    def din(name, shape, dt=F32):
        return nc.dram_tensor(name, list(shape), dt, kind="ExternalInput").ap()

    def dscr(name, shape, dt=F32, out=False):
        return nc.dram_tensor(name, list(shape), dt, kind=("ExternalOutput" if out else "Internal")).ap()

    x_d = din("x", [S, D])
    p_d = din("p", [L, S, 256])
    gmix_d = din("g_mix", [L, 128, 8]); gffn_d = din("g_ffn", [L, 128, 8]); gple_d = din("g_ple", [L, 128, 8])
    gfin_d = din("g_final", [128, 8])
    win_d = din("w_in", [L, D, 2304]); wout_d = din("w_out", [L, D, D]); wq_d = din("w_q", [L, D, 2048])
    wpg_d = din("w_pg", [L, D, D]); wpe_d = din("w_pe", [L, 256, D])
    caw_d = din("conv_a_w", [L, 128, 3, 31]); cab_d = din("conv_a_b", [L, 128, 3])
    lag_d = din("ln_a_g", [L, 128, 3]); lab_d = din("ln_a_b", [L, 128, 3])
    cbw_d = din("conv_b_w", [L, 128, 2, 3])
    lcg_d = din("ln_c_g", [L, 128, 3]); lcb_d = din("ln_c_b", [L, 128, 3])
    wst_d = din("w_sT", [L, 128, 3, 256]); bsb_d = din("b_sb", [L, 128, 3, 128])
    keys_d = din("keysT", [L, 128, 16, 128])
    eu_d = din("expert_uT", [L, D, 16384]); ev_d = din("expert_v", [L, 16384, D])
    out_d = nc.dram_tensor("out", [S, D], F32, kind="ExternalOutput").ap()

    hcur_d = dscr("hcur", [8, 128, S], out=dbg)
    h1_d = dscr("h1", [8, 128, S], out=dbg)
    h2_d = dscr("h2", [8, 128, S], out=dbg)
    pT_d = dscr("pT", [L, 2, 128, S])
    nT_d = dscr("nT", [8, 128, S], BF16)
    sS_d = dscr("sS", [S, 2048], out=dbg)
    sTK_d = dscr("sTK", [S, 16], out=dbg)
    ubf_d = dscr("ubf", [L, 8, 128, 16384], BF16)
    vbf_d = dscr("vbf", [L, 128, 128, D], BF16)

    st = contextlib.ExitStack()
    with st:
        CAP = 206 * 1024 + 512
        arena_t = st.enter_context(nc.sbuf_tensor("arena", [128, CAP // 4], F32))
        psb = [st.enter_context(nc.psum_tensor("pb%d" % i, [128, 512], F32)) for i in range(8)]
        PB = [Buf("pb%d" % i) for i in range(8)]
        R = Rec(nc)
        off = [0]

        def alloc(shape, dt=F32):
            n = int(np.prod(shape)) * (2 if dt == BF16 else 4)
            n = (n + 31) // 32 * 32
            a = arena_t[:, off[0] // 4:(off[0] + n) // 4]
            off[0] += n
            assert off[0] <= CAP, ("SBUF arena overflow", off[0])
            if dt == BF16:
                a = a.bitcast(BF16)
            a = a[:, 0:int(np.prod(shape))]
            if len(shape) == 2:
                a = a.rearrange("p (a b) -> p a b", a=shape[0])
            elif len(shape) == 3:
                a = a.rearrange("p (a b c) -> p a b c", a=shape[0], b=shape[1])
            return a

        def mm(out, lhsT, rhs, start, stop, rd, wr):
            return R.op("pe", lambda e: e.matmul(out, lhsT=lhsT, rhs=rhs, start=start, stop=stop), rd, wr)

        def tr(out, in_, ident, rd, wr):
            return R.op("pe", lambda e: e.transpose(out=out, in_=in_, identity=ident), rd, wr)

        def act(out, in_, func, rd, wr, scale=None, bias=None):
            kw = {}
            if scale is not None:
                kw["scale"] = scale
            if bias is not None:
                kw["bias"] = bias
            return R.op("act", lambda e: e.activation(out=out, in_=in_, func=func, **kw), rd, wr)

        def tt(eng, out, in0, in1, op, rd, wr):
            return R.op(eng, lambda e: e.tensor_tensor(out=out, in0=in0, in1=in1, op=op), rd, wr)

        def ts(eng, out, in0, s1, s2, op0, op1, rd, wr):
            if s2 is None:
                return R.op(eng, lambda e: e.tensor_scalar(out=out, in0=in0, scalar1=s1, scalar2=None, op0=op0), rd, wr)
            return R.op(eng, lambda e: e.tensor_scalar(out=out, in0=in0, scalar1=s1, scalar2=s2, op0=op0, op1=op1), rd, wr)

        def stt(eng, out, in0, scalar, in1, op0, op1, rd, wr):
            return R.op(eng, lambda e: e.scalar_tensor_tensor(out=out, in0=in0, scalar=scalar, in1=in1, op0=op0, op1=op1), rd, wr)

        def cp(eng, out, in_, rd, wr):
            if eng == "act":
                return R.op("act", lambda e: e.copy(out=out, in_=in_), rd, wr)
            return R.op(eng, lambda e: e.tensor_copy(out=out, in_=in_), rd, wr)

        def mset(eng, ap, val, wr):
            return R.op(eng, lambda e: e.memset(ap, val), (), wr)

        def dma(out, in_, rd, wr):
            return R.op("sp", lambda e: e.dma_start(out=out, in_=in_), rd, wr, dma=True)

        def recip(out, in_, rd, wr):
            return R.op("dve", lambda e: e.reciprocal(out=out, in_=in_), rd, wr)

        def vmax(out, in_, rd, wr):
            return R.op("dve", lambda e: e.max(out=out, in_=in_), rd, wr)

        def mrep(out, rep, vals, rd, wr):
            return R.op("dve", lambda e: e.match_replace(out=out, in_to_replace=rep, in_values=vals, imm_value=NEG), rd, wr)

        def pbf(i):
            return psb[i][:].bitcast(BF16)

        ident = alloc([128]); identb = alloc([128], BF16); onesb = alloc([128], BF16); blk = alloc([128])
        epst = alloc([1])
        Bc = Buf("consts")
        mset("pool", ident, 0.0, [Bc])
        R.op("pool", lambda e: e.affine_select(out=ident, in_=ident, pattern=[[-1, 128]], compare_op=ALU.not_equal,
                                               fill=1.0, base=0, channel_multiplier=1), [Bc], [Bc])
        cp("dve", identb, ident, [Bc], [Bc])
        mset("pool", onesb, 1.0 / 1024.0, [Bc])
        mset("pool", blk, 0.0, [Bc])
        mset("pool", blk[0:64, 0:64], 1.0 / 64.0, [Bc])
        mset("pool", blk[64:128, 64:128], 1.0 / 64.0, [Bc])
        mset("pool", epst, EPS, [Bc])
        base_off = off[0]

        xs = alloc([1024]); xst = alloc([8, 128]); cst = alloc([2048]); cstb = alloc([2048], BF16)
        pst = alloc([2, 128])
        Bxs, Bxst, Bcst, Bcstb, Bpst = Buf("xs"), Buf("xst"), Buf("cst"), Buf("cstb"), Buf("pst")
        hcur_v = hcur_d.rearrange("c p s -> p c s")
        h1_v = h1_d.rearrange("c p s -> p c s")
        h2_v = h2_d.rearrange("c p s -> p c s")
        nT_v = nT_d.rearrange("c p s -> p c s")
        for tl in range(S // 128):
            ts_ = slice(tl * 128, (tl + 1) * 128)
            dma(xs, x_d[ts_, :], [], [Bxs])
            for c in range(8):
                bk = c // 4
                tr(psb[bk][:, (c % 4) * 128:(c % 4 + 1) * 128], xs[:, c * 128:(c + 1) * 128], ident, [Bxs, Bc], [PB[bk]])
            cp("act", xst[:, 0:4, :], psb[0][:].rearrange("p (a b) -> p a b", a=4), [PB[0]], [Bxst])
            cp("dve", xst[:, 4:8, :], psb[1][:].rearrange("p (a b) -> p a b", a=4), [PB[1]], [Bxst])
            dma(hcur_v[:, :, ts_], xst, [Bxst], [])
            for l in range(L):
                dma(xs[:, 0:256], p_d[l, ts_, :], [], [Bxs])
                for c in range(2):
                    tr(psb[2][:, c * 128:(c + 1) * 128], xs[:, c * 128:(c + 1) * 128], ident, [Bxs, Bc], [PB[2]])
                cp("act", pst, psb[2][:, 0:256].rearrange("p (a b) -> p a b", a=2), [PB[2]], [Bpst])
                dma(pT_d[l].rearrange("c p s -> p c s")[:, :, ts_], pst, [Bpst], [])
        cst2 = [cst, alloc([2048])]; cstb2 = [cstb, alloc([2048], BF16)]
        Bcst2 = [Bcst, Buf("cst1")]; Bcstb2 = [Bcstb, Buf("cstb1")]
        it = 0
        for l in range(L):
            for c in range(8):
                for cb in range(8):
                    ci = it % 2
                    dma(cst2[ci], eu_d[l, c * 128:(c + 1) * 128, cb * 2048:(cb + 1) * 2048], [], [Bcst2[ci]])
                    cp("dve" if it % 4 < 2 else "act", cstb2[ci], cst2[ci], [Bcst2[ci]], [Bcstb2[ci]])
                    dma(ubf_d[l, c, :, cb * 2048:(cb + 1) * 2048], cstb2[ci], [Bcstb2[ci]], [])
                    it += 1
            for et in range(0, 128, 2):
                ci = it % 2
                dma(cst2[ci].rearrange("p (a b) -> p a b", a=2),
                    ev_d[l, et * 128:(et + 2) * 128, :].rearrange("(a p) d -> p a d", a=2), [], [Bcst2[ci]])
                cp("dve" if it % 4 < 2 else "act", cstb2[ci], cst2[ci], [Bcst2[ci]], [Bcstb2[ci]])
                dma(vbf_d[l, et:et + 2].rearrange("a p d -> p a d"), cstb2[ci].rearrange("p (a b) -> p a b", a=2), [Bcstb2[ci]], [])
                it += 1
        R.barrier()

        def rmsnorm_to(hT, g, outT, BhT, BoutT, sqb, Bsq, rstd, Brstd, t0buf, Bt0, bank, out_f32=False):
            for c in range(8):
                if c % 2 == 0:
                    act(sqb[:, c, :], hT[:, c, :], AF.Square, [BhT], [Bsq])
                else:
                    tt("pool", sqb[:, c, :], hT[:, c, :], hT[:, c, :], ALU.mult, [BhT], [Bsq])
            for c in range(8):
                mm(psb[bank][:], onesb, sqb[:, c, :], c == 0, c == 7, [Bsq, Bc], [PB[bank]])
            act(t0buf, psb[bank][:], AF.Sqrt, [PB[bank], Bc], [Bt0], bias=epst[:, 0:1])
            recip(rstd, t0buf, [Bt0], [Brstd])
            for c in range(8):
                stt("dve", outT[:, c, :], hT[:, c, :], g[:, c:c + 1], rstd, ALU.mult, ALU.mult,
                    [BhT, Brstd, Bc], [BoutT])

        def load_w_bf(dst, src_d, nk, ncol, stage, Bst, Bdst):
            for k in range(nk):
                dma(stage[:, 0:ncol], src_d[k * 128:(k + 1) * 128, :], [], [Bst])
                cp("dve" if k % 2 == 0 else "act", dst[:, k, :], stage[:, 0:ncol], [Bst], [Bdst])

        for l in range(L):
            off[0] = base_off
            Win = alloc([8, 2304], BF16); Wout = alloc([8, 1024], BF16); Wq = alloc([8, 2048], BF16)
            keysb = alloc([16, 128], BF16); wsT = alloc([3, 256], BF16)
            gmix = alloc([8]); gffn = alloc([8]); caw = alloc([3, 31]); cab = alloc([3]); lag = alloc([3]); lab = alloc([3])
            cbw = alloc([2, 3]); lcg = alloc([3]); lcb = alloc([3]); bsb = alloc([3, 128])
            hT = alloc([8, 512]); aT = alloc([8, 512], BF16); ymix = alloc([8, 512], BF16)
            rstd = alloc([512]); T = [alloc([512]) for _ in range(6)]
            ybuf = [alloc([544]) for _ in range(3)]; ubuf = [alloc([516]) for _ in range(2)]
            uC = alloc([3, 512]); vln = alloc([3, 512], BF16); vlnT = alloc([384], BF16)
            qT = alloc([16, 512], BF16)
            s_sb = alloc([2048]); vtop = alloc([16, 16]); tmpk = alloc([16, 128]); cand = alloc([8, 256]); vals = alloc([8, 16])
            evx = alloc([8, 16]); zz = alloc([8]); tk = alloc([16])
            stage = qT.bitcast(F32) if False else None
            BW = Buf("W"); BhT = Buf("hT"); BaT = Buf("aT"); Bym = Buf("ymix"); Brs = Buf("rstd")
            BT = [Buf("T%d" % i) for i in range(6)]
            Byb = [Buf("yb%d" % i) for i in range(3)]; Bub = [Buf("ub%d" % i) for i in range(2)]
            BuC = Buf("uC"); Bvln = Buf("vln"); BvlnT = Buf("vlnT"); BqT = Buf("qT")
            Bs = Buf("s_sb"); Bv = [Buf("vtop%d" % i) for i in range(16)]; Btk_ = [Buf("tmpk%d" % i) for i in range(16)]; Bcand = Buf("cand"); Bvals = [Buf("vals%d" % i) for i in range(8)]
            Bevx = Buf("evx"); Bzz = Buf("zz"); Btkk = Buf("tk")
            stg = s_sb
            Bstg = Bs
            for k in range(8):
                for hf in range(2):
                    ncol = 1152
                    dma(stg[:, 0:ncol], win_d[l, k * 128:(k + 1) * 128, hf * ncol:(hf + 1) * ncol], [], [Bstg])
                    cp("dve" if hf == 0 else "act", Win[:, k, hf * ncol:(hf + 1) * ncol], stg[:, 0:ncol], [Bstg], [BW])
            load_w_bf(Wout, wout_d[l], 8, 1024, stg, Bstg, BW)
            load_w_bf(Wq, wq_d[l], 8, 2048, stg, Bstg, BW)
            dma(stg.rearrange("p (a b) -> p a b", a=16), keys_d[l], [], [Bstg])
            cp("dve", keysb, stg.rearrange("p (a b) -> p a b", a=16), [Bstg], [BW])
            dma(stg[:, 0:768].rearrange("p (a b) -> p a b", a=3), wst_d[l], [], [Bstg])
            cp("dve", wsT, stg[:, 0:768].rearrange("p (a b) -> p a b", a=3), [Bstg], [BW])
            for c in range(3):
                for hh in range(2):
                    mset("pool", wsT[64:128, c, hh * 128:hh * 128 + 64], 0.0, [BW])
            for (dst, src) in [(gmix, gmix_d[l]), (gffn, gffn_d[l]), (caw, caw_d[l]), (cab, cab_d[l]), (lag, lag_d[l]),
                               (lab, lab_d[l]), (cbw, cbw_d[l]), (lcg, lcg_d[l]), (lcb, lcb_d[l]), (bsb, bsb_d[l])]:
                dma(dst, src, [], [Bc])
            for j in range(3):
                mset("pool", ybuf[j][:, 0:32], 0.0, [Byb[j]])
            for j in range(2):
                mset("pool", ubuf[j][:, 0:4], 0.0, [Bub[j]])

            def zchunk(j, bank):
                for k in range(8):
                    mm(psb[bank][:], Win[:, k, j * 128:(j + 1) * 128], aT[:, k, :], k == 0, k == 7, [BW, BaT], [PB[bank]])

            def gln(src, Bsrc, g, b, j, func, dst, Bdst):
                mm(psb[2][:], blk, src, True, True, [Bsrc, Bc], [PB[2]])
                tt("dve", T[2], src, psb[2][:], ALU.subtract, [Bsrc, PB[2]], [BT[2]])
                act(T[3], T[2], AF.Square, [BT[2]], [BT[3]])
                mm(psb[3][:], blk, T[3], True, True, [BT[3], Bc], [PB[3]])
                act(T[4], psb[3][:], AF.Sqrt, [PB[3], Bc], [BT[4]], bias=epst[:, 0:1])
                recip(T[4], T[4], [BT[4]], [BT[4]])
                tt("dve", T[2], T[2], T[4], ALU.mult, [BT[2], BT[4]], [BT[2]])
                act(dst, T[2], func, [BT[2], Bc], [Bdst], scale=g[:, j:j + 1], bias=b[:, j:j + 1])

            for b in range(NBLK):
                tsl = slice(b * 512, (b + 1) * 512)
                dma(hT, hcur_v[:, :, tsl], [], [BhT])
                rmsnorm_to(hT, gmix, aT, BhT, BaT, ymix, Bym, rstd, Brs, T[0], BT[0], 7)
                for j in range(3):
                    zchunk(j, 0)
                    zchunk(3 + j, 1)
                    act(T[1], psb[1][:], AF.Sigmoid, [PB[1]], [BT[1]])
                    tt("dve", ybuf[j][:, 32:544], psb[0][:], T[1], ALU.mult, [PB[0], BT[1]], [Byb[j]])
                    ts("dve", T[5], ybuf[j][:, 2:514], caw[:, j, 0:1], cab[:, j:j + 1], ALU.mult, ALU.add, [Byb[j], Bc], [BT[5]])
                    for k in range(1, 31):
                        stt("dve", T[5], ybuf[j][:, 2 + k:514 + k], caw[:, j, k:k + 1], T[5], ALU.mult, ALU.add, [Byb[j], Bc, BT[5]], [BT[5]])
                    cp("pool", ybuf[j][:, 0:32], ybuf[j][:, 512:544], [Byb[j]], [Byb[j]])
                    gln(T[5], BT[5], lag, lab, j, AF.Silu, ymix[:, j, :], Bym)
                for j in range(2):
                    zchunk(8 + j, 0)
                    zchunk(10 + j, 1)
                    cp("act", T[1], psb[0][:], [PB[0]], [BT[1]])
                    tt("dve", ubuf[j][:, 4:516], psb[1][:], T[1], ALU.mult, [PB[1], BT[1]], [Bub[j]])
                    ts("dve", T[5], ubuf[j][:, 2:514], cbw[:, j, 0:1], None, ALU.mult, None, [Bub[j], Bc], [BT[5]])
                    for k in range(1, 3):
                        stt("dve", T[5], ubuf[j][:, 2 + k:514 + k], cbw[:, j, k:k + 1], T[5], ALU.mult, ALU.add, [Bub[j], Bc, BT[5]], [BT[5]])
                    cp("pool", ubuf[j][:, 0:4], ubuf[j][:, 512:516], [Bub[j]], [Bub[j]])
                    zchunk(6 + j, 0)
                    tt("dve", ymix[:, 3 + j, :], psb[0][:], T[5], ALU.mult, [PB[0], BT[5]], [Bym])
                for j in range(3):
                    zchunk(15 + j, 0)
                    cp("act", T[5], psb[0][:], [PB[0]], [BT[5]])
                    gln(T[5], BT[5], lcg, lcb, j, AF.Identity, vln[:, j, :], Bvln)
                    zchunk(12 + j, 1)
                    cp("act", uC[:, j, :], psb[1][:], [PB[1]], [BuC])
                for sbk in range(4):
                    csl = slice(sbk * 128, (sbk + 1) * 128)
                    for j in range(3):
                        tr(pbf(4)[:, j * 128:(j + 1) * 128], vln[:, j, csl], identb, [Bvln, Bc], [PB[4]])
                    cp("act", vlnT, pbf(4)[:, 0:384], [PB[4]], [BvlnT])
                    for j in range(3):
                        mm(psb[5][:, 0:256], vlnT[:, j * 128:(j + 1) * 128], wsT[:, j, :], True, True, [BvlnT, BW], [PB[5]])
                        for hh in range(2):
                            rs = slice(hh * 64, (hh + 1) * 64)
                            tt("dve", T[1][rs, 0:128], psb[5][rs, hh * 128:(hh + 1) * 128], bsb[rs, j, :], ALU.add, [PB[5], Bc], [BT[1]])
                            tt("dve", ymix[rs, 5 + j, csl], T[1][rs, 0:128], uC[rs, j, csl], ALU.mult, [BT[1], BuC], [Bym])
                for dc in range(8):
                    bk = 6 + dc % 2
                    for k in range(8):
                        mm(psb[bk][:], Wout[:, k, dc * 128:(dc + 1) * 128], ymix[:, k, :], k == 0, k == 7, [BW, Bym], [PB[bk]])
                    tt("dve", hT[:, dc, :], hT[:, dc, :], psb[bk][:], ALU.add, [BhT, PB[bk]], [BhT])
                dma(h1_v[:, :, tsl], hT, [BhT], [])
                rmsnorm_to(hT, gffn, aT, BhT, BaT, ymix, Bym, rstd, Brs, T[0], BT[0], 7)
                dma(nT_v[:, :, tsl], aT, [BaT], [])
                for qc in range(16):
                    bk = 6 + qc % 2
                    for k in range(8):
                        mm(psb[bk][:], Wq[:, k, qc * 128:(qc + 1) * 128], aT[:, k, :], k == 0, k == 7, [BW, BaT], [PB[bk]])
                    cp("act" if qc % 2 == 0 else "dve", qT[:, qc, :], psb[bk][:], [PB[bk]], [BqT])
                for tl in range(4):
                    csl = slice(tl * 128, (tl + 1) * 128)
                    for g in range(16):
                        mm(psb[g // 4][:, (g % 4) * 128:(g % 4 + 1) * 128], qT[:, g, csl], keysb[:, g, :], True, True,
                           [BqT, BW], [PB[g // 4]])
                    for q4 in range(4):
                        cp("act", s_sb[:, q4 * 512:(q4 + 1) * 512], psb[q4][:], [PB[q4]], [Bs])
                    for g in range(16):
                        vmax(vtop[:, g, 0:8], s_sb[:, g * 128:(g + 1) * 128], [Bs], [Bv[g]])
                    for g in range(16):
                        mrep(tmpk[:, g, :], vtop[:, g, 0:8], s_sb[:, g * 128:(g + 1) * 128], [Bs, Bv[g]], [Btk_[g]])
                    for g in range(16):
                        vmax(vtop[:, g, 8:16], tmpk[:, g, :], [Btk_[g]], [Bv[g]])
                    v4 = vtop.rearrange("p (h two) k -> p h two k", two=2)
                    c4 = cand.rearrange("p h (i j) -> p h i j", i=16)
                    tt("dve", c4, v4[:, :, 0, :].unsqueeze(3).to_broadcast([128, 8, 16, 16]),
                       v4[:, :, 1, :].unsqueeze(2).to_broadcast([128, 8, 16, 16]), ALU.add, Bv, [Bcand])
                    tmpc = tmpk.rearrange("p (h two) k -> p h (two k)", two=2)
                    for h in range(8):
                        vmax(vals[:, h, 0:8], cand[:, h, :], [Bcand], [Bvals[h]])
                    for h in range(8):
                        mrep(tmpc[:, h, :], vals[:, h, 0:8], cand[:, h, :], [Bcand, Bvals[h]], [Btk_[2 * h], Btk_[2 * h + 1]])
                    for h in range(8):
                        vmax(vals[:, h, 8:16], tmpc[:, h, :], [Btk_[2 * h], Btk_[2 * h + 1]], [Bvals[h]])
                    tt("dve", evx, vals, vals[:, :, 0:1].to_broadcast([128, 8, 16]), ALU.subtract, Bvals, [Bevx])
                    act(evx, evx, AF.Exp, [Bevx], [Bevx])
                    R.op("dve", lambda e: e.reduce_sum(out=zz, in_=evx, axis=AX.X), [Bevx], [Bzz])
                    act(zz, zz, AF.Ln, [Bzz], [Bzz])
                    ts("dve", tk[:, 0:8], vals[:, :, 15], -1.0e-5, None, ALU.add, None, Bvals, [Btkk])
                    tt("dve", zz, zz, vals[:, :, 0], ALU.add, [Bzz] + Bvals, [Bzz])
                    ts("dve", tk[:, 8:16], zz, -1.0, None, ALU.mult, None, [Bzz], [Btkk])
                    rows = slice(b * 512 + tl * 128, b * 512 + (tl + 1) * 128)
                    dma(sS_d[rows, :], s_sb, [Bs], [])
                    dma(sTK_d[rows, :], tk, [Btkk], [])
            R.barrier()

            off[0] = base_off
            NWB = 32
            nTb = alloc([8, 512], BF16)
            s4 = [alloc([2048]) for _ in range(4)]; tk4 = [alloc([16]) for _ in range(4)]
            cc = [alloc([8]) for _ in range(4)]; Dg = [alloc([8, 128], BF16) for _ in range(4)]
            accO = [alloc([1024]) for _ in range(4)]
            Ub = [alloc([8, 512], BF16) for _ in range(2)]; Vb = [alloc([4, 1024], BF16) for _ in range(2)]
            gelT = [alloc([4, 512], BF16) for _ in range(2)]
            Xb = [alloc([8, 4, 128]) for _ in range(3)]; Eb = [alloc([8, 4, 128], BF16) for _ in range(2)]; Emb = [alloc([8, 4, 128], BF16) for _ in range(2)]
            Mb = [alloc([8, 512], BF16) for _ in range(2)]
            HT = [alloc([4, 128], BF16) for _ in range(2)]
            h1T = Xb[0].rearrange("p a b c -> p (a b c)").rearrange("p (a b) -> p a b", a=8)
            BnT = Buf("nTb"); Bs4 = [Buf("s4%d" % i) for i in range(4)]; BaccO = [Buf("accO%d" % i) for i in range(4)]
            BDg = [Buf("Dg%d" % i) for i in range(4)]
            BUb = [Buf("Ub%d" % i) for i in range(2)]; BVb = [Buf("Vb%d" % i) for i in range(2)]
            BX = [Buf("X%d" % i) for i in range(3)]; BEb = [Buf("Eb%d" % i) for i in range(2)]; BM = [Buf("M%d" % i) for i in range(2)]
            Bgel = [Buf("gelT%d" % i) for i in range(2)]; BHT = [Buf("HT%d" % i) for i in range(2)]
            Bh1 = BX[0]
            ubf_v = ubf_d[l].rearrange("c p e -> p c e")
            vbf_v = vbf_d[l].rearrange("a p d -> p a d")
            s4v = [t_.rearrange("p (h two k) -> p h two k", two=2, k=128) for t_ in s4]
            for tb in range(NBLK):
                tsl = slice(tb * 512, (tb + 1) * 512)
                dma(nTb, nT_v[:, :, tsl], [], [BnT])
                for tl in range(4):
                    rows = slice(tb * 512 + tl * 128, tb * 512 + (tl + 1) * 128)
                    dma(s4[tl], sS_d[rows, :], [], [Bs4[tl]])
                    dma(tk4[tl], sTK_d[rows, :], [], [Bs4[tl]])
                    mset("pool", accO[tl], 0.0, [BaccO[tl]])
                    tt("dve", s4v[tl][:, :, 0, :], s4v[tl][:, :, 0, :], tk4[tl][:, 0:8].unsqueeze(2).to_broadcast([128, 8, 128]),
                       ALU.subtract, [Bs4[tl]], [Bs4[tl]])
                    tt("dve", cc[tl], tk4[tl][:, 0:8], tk4[tl][:, 8:16], ALU.add, [Bs4[tl]], [BDg[tl]])
                    act(cc[tl], cc[tl], AF.Exp, [BDg[tl]], [BDg[tl]])
                    for h in range(8):
                        ts("pool", Dg[tl][:, h, :], identb, cc[tl][:, h:h + 1], None, ALU.mult, None, [BDg[tl], Bc], [BDg[tl]])

                steps = [(wb, tl) for wb in range(NWB) for tl in range(4)]
                G_ = len(steps)

                def ldw(wb):
                    dma(Ub[wb % 2], ubf_v[:, :, wb * 512:(wb + 1) * 512], [], [BUb[wb % 2]])
                    dma(Vb[wb % 2], vbf_v[:, wb * 4:(wb + 1) * 4, :], [], [BVb[wb % 2]])

                def stA(wb):
                    for et in range(4):
                        bk = et % 2
                        for k in range(8):
                            mm(psb[bk][:], Ub[wb % 2][:, k, et * 128:(et + 1) * 128], nTb[:, k, :], k == 0, k == 7,
                               [BnT, BUb[wb % 2]], [PB[bk]])
                        act(gelT[wb % 2][:, et, :], psb[bk][:], AF.Gelu_apprx_tanh, [PB[bk]], [Bgel[wb % 2]])

                def stX(g):
                    wb, tl = steps[g]
                    eng = "dve" if tl == 3 else "pool"
                    tt(eng, Xb[g % 3],
                       s4v[tl][:, :, 0, wb * 4:(wb + 1) * 4].unsqueeze(3).to_broadcast([128, 8, 4, 128]),
                       s4v[tl][:, :, 1, :].unsqueeze(2).to_broadcast([128, 8, 4, 128]), ALU.add, [Bs4[tl]], [BX[g % 3]])

                def stE(g):
                    act(Eb[g % 2], Xb[g % 3], AF.Exp, [BX[g % 3]], [BEb[g % 2]])
                    if steps[g][1] != 1:
                        act(Emb[g % 2], Xb[g % 3], AF.Exp, [BX[g % 3]], [BEb[g % 2]], scale=1048576.0)

                def stM(g):
                    if steps[g][1] != 1:
                        tt("dve", Mb[g % 2], Emb[g % 2].rearrange("p h a b -> p h (a b)"),
                           Eb[g % 2].rearrange("p h a b -> p h (a b)"), ALU.min, [BEb[g % 2]], [BM[g % 2]])
                    else:
                        stt("dve", Mb[g % 2], Xb[g % 3].rearrange("p h a b -> p h (a b)"), 0.0,
                            Eb[g % 2].rearrange("p h a b -> p h (a b)"), ALU.is_ge, ALU.mult, [BX[g % 3], BEb[g % 2]], [BM[g % 2]])

                def stG(g):
                    wb, tl = steps[g]
                    bk = 2 + g % 2
                    for et in range(4):
                        for h in range(8):
                            mm(psb[bk][:, et * 128:(et + 1) * 128], Mb[g % 2][:, h, et * 128:(et + 1) * 128], Dg[tl][:, h, :],
                               (et == 0 and h == 0), h == 7, [BM[g % 2], BDg[tl]], [PB[bk]])

                def stHT(g):
                    wb, tl = steps[g]
                    csl = slice(tl * 128, (tl + 1) * 128)
                    bk = 2 + g % 2
                    tt("dve", HT[g % 2], gelT[wb % 2][:, :, csl], psb[bk][:].rearrange("p (a b) -> p a b", a=4), ALU.mult,
                       [Bgel[wb % 2], PB[bk]], [BHT[g % 2]])

                def stV(g):
                    wb, tl = steps[g]
                    ht = HT[g % 2]
                    ob = 4 + 2 * (g % 2)
                    for et in range(4):
                        for hf in range(2):
                            mm(psb[ob + hf][:], ht[:, et, :], Vb[wb % 2][:, et, hf * 512:(hf + 1) * 512], et == 0, et == 3,
                               [BHT[g % 2], BVb[wb % 2]], [PB[ob + hf]])

                def stAcc(g):
                    wb, tl = steps[g]
                    ob = 4 + 2 * (g % 2)
                    for hf in range(2):
                        tt("dve", accO[tl][:, hf * 512:(hf + 1) * 512], accO[tl][:, hf * 512:(hf + 1) * 512], psb[ob + hf][:],
                           ALU.add, [BaccO[tl], PB[ob + hf]], [BaccO[tl]])

                ldw(0)
                stA(0)
                stX(0); stX(1); stE(0)
                for g in range(G_ + 3):
                    if g + 2 < G_:
                        stX(g + 2)
                    if g + 1 < G_:
                        stE(g + 1)
                    if 0 <= g - 2 < G_:
                        stHT(g - 2)
                    if g < G_:
                        stM(g)
                    if 0 <= g - 3 < G_:
                        stAcc(g - 3)
                    if 0 <= g - 1 < G_:
                        stG(g - 1)
                    if 0 <= g - 2 < G_:
                        stV(g - 2)
                    if g < G_:
                        wb, tl = steps[g]
                        if tl == 1 and wb + 1 < NWB:
                            ldw(wb + 1)
                        if tl == 3 and wb + 1 < NWB:
                            stA(wb + 1)
                dma(h1T, h1_v[:, :, tsl], [], [Bh1])
                for tl in range(4):
                    csl = slice(tl * 128, (tl + 1) * 128)
                    for dc in range(8):
                        bk = 6 + (dc // 4) % 2
                        tr(psb[bk][:, (dc % 4) * 128:(dc % 4 + 1) * 128], accO[tl][:, dc * 128:(dc + 1) * 128], ident,
                           [BaccO[tl], Bc], [PB[bk]])
                        if dc % 4 == 3:
                            d0 = dc - 3
                            tt("dve", h1T[:, d0:d0 + 4, csl], h1T[:, d0:d0 + 4, csl],
                               psb[bk][:].rearrange("p (a b) -> p a b", a=4), ALU.add, [Bh1, PB[bk]], [Bh1])
                dma(h2_v[:, :, tsl], h1T, [Bh1], [])
            R.barrier()

            off[0] = base_off
            Wpg = alloc([8, 1024], BF16); Wpe = alloc([2, 1024], BF16); gple = alloc([8]); gfin = alloc([8])
            stg = alloc([1024]); hT = alloc([8, 512]); aT = alloc([8, 512], BF16); sqb = alloc([8, 512], BF16)
            rstd = alloc([512]); T0 = alloc([512]); T1 = alloc([512]); pTf = alloc([2, 512]); pTb = alloc([2, 512], BF16)
            oT = alloc([8, 512]); otok = alloc([1024])
            BW = Buf("Wc"); Bstg = Buf("stgc"); BhT = Buf("hTc"); BaT = Buf("aTc"); Bsq = Buf("sqc"); Brs = Buf("rsc")
            BT0 = Buf("T0c"); BT1 = Buf("T1c"); BpTf = Buf("pTf"); BpTb = Buf("pTb"); BoT = Buf("oT"); Botok = Buf("otok")
            load_w_bf(Wpg, wpg_d[l], 8, 1024, stg, Bstg, BW)
            load_w_bf(Wpe, wpe_d[l], 2, 1024, stg, Bstg, BW)
            dma(gple, gple_d[l], [], [Bc])
            dma(gfin, gfin_d, [], [Bc])
            last = (l == L - 1)
            for b in range(NBLK):
                tsl = slice(b * 512, (b + 1) * 512)
                dma(hT, h2_v[:, :, tsl], [], [BhT])
                dma(pTf, pT_d[l].rearrange("c p s -> p c s")[:, :, tsl], [], [BpTf])
                cp("pool", pTb, pTf, [BpTf], [BpTb])
                rmsnorm_to(hT, gple, aT, BhT, BaT, sqb, Bsq, rstd, Brs, T0, BT0, 7)
                for dc in range(8):
                    b0, b1 = (dc % 2) * 2, (dc % 2) * 2 + 1
                    for k in range(8):
                        mm(psb[b0][:], Wpg[:, k, dc * 128:(dc + 1) * 128], aT[:, k, :], k == 0, k == 7, [BW, BaT], [PB[b0]])
                    for k in range(2):
                        mm(psb[b1][:], Wpe[:, k, dc * 128:(dc + 1) * 128], pTb[:, k, :], k == 0, k == 1, [BW, BpTb], [PB[b1]])
                    act(T1, psb[b0][:], AF.Sigmoid, [PB[b0]], [BT1])
                    tt("dve", T1, T1, psb[b1][:], ALU.mult, [BT1, PB[b1]], [BT1])
                    tt("dve", hT[:, dc, :], hT[:, dc, :], T1, ALU.add, [BhT, BT1], [BhT])
                if not last:
                    dma(hcur_v[:, :, tsl], hT, [BhT], [])
                else:
                    if dbg:
                        dma(hcur_v[:, :, tsl], hT, [BhT], [])
                    for c in range(8):
                        act(sqb[:, c, :], hT[:, c, :], AF.Square, [BhT], [Bsq])
                    for c in range(8):
                        mm(psb[7][:], onesb, sqb[:, c, :], c == 0, c == 7, [Bsq, Bc], [PB[7]])
                    act(T0, psb[7][:], AF.Sqrt, [PB[7], Bc], [BT0], bias=epst[:, 0:1])
                    recip(rstd, T0, [BT0], [Brs])
                    for c in range(8):
                        stt("dve", oT[:, c, :], hT[:, c, :], gfin[:, c:c + 1], rstd, ALU.mult, ALU.mult,
                            [BhT, Brs, Bc], [BoT])
                    for tl in range(4):
                        csl = slice(tl * 128, (tl + 1) * 128)
                        for dc in range(8):
                            bk = 4 + (dc // 4)
                            tr(psb[bk][:, (dc % 4) * 128:(dc % 4 + 1) * 128], oT[:, dc, csl], ident, [BoT, Bc], [PB[bk]])
                        cp("act", otok[:, 0:512], psb[4][:], [PB[4]], [Botok])
                        cp("dve", otok[:, 512:1024], psb[5][:], [PB[5]], [Botok])
                        dma(out_d[b * 512 + tl * 128:b * 512 + (tl + 1) * 128, :], otok, [Botok], [])
            R.barrier()
        R.emit()
    return nc, R


def _cols(v, n):
    L_ = v.shape[0]
    return np.ascontiguousarray(v.reshape(L_, n, 128).transpose(0, 2, 1))


def prep_shared(inp, L):
    f = lambda a: np.ascontiguousarray(np.asarray(a, dtype=np.float32))
    sh = {}
    sh["g_mix"] = _cols(f(inp["g_mix"])[:L], 8); sh["g_ffn"] = _cols(f(inp["g_ffn"])[:L], 8); sh["g_ple"] = _cols(f(inp["g_ple"])[:L], 8)
    sh["g_final"] = _cols(f(inp["g_final"])[None], 8)[0]
    for k in ("w_in", "w_out", "w_q", "w_pg", "w_pe", "expert_v"):
        sh[k] = f(inp[k])[:L]
    caw = f(inp["conv_a_w"])[:L]
    sh["conv_a_w"] = np.ascontiguousarray(caw.reshape(L, 31, 3, 128).transpose(0, 3, 2, 1))
    for k in ("conv_a_b", "ln_a_g", "ln_a_b", "ln_c_g", "ln_c_b"):
        sh[k] = _cols(f(inp[k])[:L], 3)
    cbw = f(inp["conv_b_w"])[:L]
    sh["conv_b_w"] = np.ascontiguousarray(cbw.reshape(L, 3, 2, 128).transpose(0, 3, 2, 1))
    ws = f(inp["w_s"])[:L]
    sh["w_sT"] = np.ascontiguousarray(ws.reshape(L, 3, 2, 128, 128).transpose(0, 4, 1, 2, 3).reshape(L, 128, 3, 256))
    bs = f(inp["b_s"])[:L]
    bsr = bs.reshape(L, 3, 2, 128)
    sh["b_sb"] = np.ascontiguousarray(np.repeat(bsr.transpose(0, 2, 1, 3), 64, axis=1))
    sk = f(inp["sub_keys"])[:L]
    sh["keysT"] = np.ascontiguousarray(sk.reshape(L, 16, 128, 128).transpose(0, 3, 1, 2))
    sh["expert_uT"] = np.ascontiguousarray(f(inp["expert_u"])[:L].transpose(0, 2, 1))
    return sh


_CACHE = {}


def kernel(**inputs):
    S, L = SEQ, DEPTH
    key = (S, L)
    if key not in _CACHE:
        _CACHE[key] = build(S, L)[0]
    nc = _CACHE[key]
    sh = prep_shared(inputs, L)
    x = np.asarray(inputs["x"], dtype=np.float32)
    p = np.asarray(inputs["p"], dtype=np.float32)
    in_maps = []
    for c in range(NCORES):
        m = dict(sh)
        m["x"] = np.ascontiguousarray(x[c])
        m["p"] = np.ascontiguousarray(p[:, c])
        in_maps.append(m)
    res = run_bass_kernel_spmd(nc, in_maps, core_ids=list(range(NCORES)))
    return np.stack([np.asarray(r["out"], dtype=np.float32) for r in res.results], axis=0)
```
